# Optimizing a Trainium2 kernel written in Bass

```python
import jax, jax.numpy as jnp
from jax import lax
import numpy as np

D_MODEL = 4096
BATCH = 2
SEQ = 8192
DEPTH = 1

N_META = 16
BLOCK = 128
PAD = BLOCK - N_META
EPS = 1e-6
NEG = -1e30
FOX_HEADS = 16
FOX_HD = 128
FOX_W = FOX_HEADS * FOX_HD
RET_HEADS = 16
RET_DK = 128
RET_DV = 256
RET_QK_W = RET_HEADS * RET_DK
RET_V_W = RET_HEADS * RET_DV
ROPE_BASE = 10000.0
PEER_HEADS = 8
PEER_NKEYS = 128
PEER_N = PEER_NKEYS * PEER_NKEYS
PEER_QDIM = 256
PEER_HALF = PEER_QDIM // 2
PEER_TOPK = 16
PEER_CHUNK = 128
W_IN_SPLITS = (FOX_W, FOX_W, FOX_W, FOX_HEADS,
               RET_QK_W, RET_QK_W, RET_V_W, RET_V_W,
               D_MODEL, D_MODEL)
W_IN_COLS = 3 * FOX_W + FOX_HEADS + 2 * RET_QK_W + 2 * RET_V_W + 2 * D_MODEL

kernel_name = "hybrid_fox_retnet_peer_block"


def _rmsnorm(x, g):
    xf = x.astype(jnp.float32)
    xf = xf * lax.rsqrt(jnp.mean(xf * xf, axis=-1, keepdims=True) + EPS)
    return xf.astype(x.dtype) * g


def _split_points():
    pts, acc = [], 0
    for w in W_IN_SPLITS[:-1]:
        acc += w
        pts.append(acc)
    return pts


def _heads(t, n_heads):
    b, l, _ = t.shape
    return t.reshape(b, l, n_heads, -1).transpose(0, 2, 1, 3)


def _rotary(x, pos):
    half = x.shape[-1] // 2
    inv = ROPE_BASE ** (-jnp.arange(half, dtype=jnp.float32) / half)
    ang = pos[:, None] * inv[None, :]
    cos, sin = jnp.cos(ang), jnp.sin(ang)
    xf = x.astype(jnp.float32)
    x1, x2 = xf[..., :half], xf[..., half:]
    return jnp.concatenate([x1 * cos - x2 * sin, x1 * sin + x2 * cos], axis=-1).astype(x.dtype)


def _fox_attention(q, k, v, logf):
    b, h, l, hd = q.shape
    nb = l // BLOCK
    c = jnp.cumsum(logf, axis=-1)
    kpos = jnp.arange(l)
    scale = hd ** -0.5

    def one_block(i):
        start = i * BLOCK
        qb = lax.dynamic_slice_in_dim(q, start, BLOCK, axis=2)
        cb = lax.dynamic_slice_in_dim(c, start, BLOCK, axis=2)
        qpos = start + jnp.arange(BLOCK)
        s = jnp.einsum('bhqd,bhkd->bhqk', qb, k).astype(jnp.float32) * scale
        s = s + (cb[..., :, None] - c[..., None, :])
        mask = (kpos[None, :] <= qpos[:, None]) & (kpos[None, :] >= PAD)
        s = jnp.where(mask, s, NEG)
        p = jax.nn.softmax(s, axis=-1)
        return jnp.einsum('bhqk,bhkd->bhqd', p.astype(v.dtype), v)

    out = lax.map(one_block, jnp.arange(nb))
    return out.transpose(1, 0, 3, 2, 4).reshape(b, l, h * hd)


def _retention(q, k, v):
    b, h, l, dk = q.shape
    dv = v.shape[-1]
    nc = l // BLOCK
    log_gamma = jnp.log1p(-(2.0 ** (-5.0 - jnp.arange(h, dtype=jnp.float32))))
    idx = jnp.arange(BLOCK, dtype=jnp.float32)
    diff = idx[:, None] - idx[None, :]
    decay = jnp.where(diff >= 0, jnp.exp(log_gamma[:, None, None] * jnp.maximum(diff, 0.0)), 0.0)
    xi = jnp.exp(log_gamma[:, None] * (idx + 1.0))
    zeta = jnp.exp(log_gamma[:, None] * (BLOCK - 1.0 - idx))
    chunk_decay = jnp.exp(log_gamma * BLOCK)

    qc = q.reshape(b, h, nc, BLOCK, dk).astype(jnp.float32)
    kc = k.reshape(b, h, nc, BLOCK, dk).astype(jnp.float32)
    vc = v.reshape(b, h, nc, BLOCK, dv).astype(jnp.float32)
    s = jnp.einsum('bhnid,bhnjd->bhnij', qc, kc) * decay[:, None]
    intra = jnp.einsum('bhnij,bhnje->bhnie', s, vc)

    def step(state, inp):
        qn, kn, vn = inp
        cross = jnp.einsum('bhid,bhde->bhie', qn, state) * xi[..., None]
        state = state * chunk_decay[:, None, None] + jnp.einsum('bhjd,bhje->bhde', kn * zeta[..., None], vn)
        return state, cross

    state0 = jnp.zeros((b, h, dk, dv), jnp.float32)
    _, cross = lax.scan(step, state0, (qc.transpose(2, 0, 1, 3, 4),
                                       kc.transpose(2, 0, 1, 3, 4),
                                       vc.transpose(2, 0, 1, 3, 4)))
    o = intra + cross.transpose(1, 2, 0, 3, 4)
    return o.reshape(b, h, l, dv)


def _head_groupnorm(o, g):
    mu = jnp.mean(o, axis=-1, keepdims=True)
    var = jnp.mean(jnp.square(o - mu), axis=-1, keepdims=True)
    o = (o - mu) * lax.rsqrt(var + EPS)
    b, h, l, dv = o.shape
    return o.transpose(0, 2, 1, 3).reshape(b, l, h * dv) * g.astype(jnp.float32)


def _mixer(h, pos, valid, norm_g, w_in, b_forget, q_norm_g, k_norm_g, ret_norm_g, w_proj_fox, w_proj_ret, w_out):
    u = _rmsnorm(h, norm_g)
    z = u @ w_in
    qa, ka, va, fa, qr, kr, vr, gr, gate_a, gate_r = jnp.split(z, _split_points(), axis=-1)
    qa = _rmsnorm(_heads(qa, FOX_HEADS), q_norm_g)
    ka = _rmsnorm(_heads(ka, FOX_HEADS), k_norm_g)
    logf = jax.nn.log_sigmoid((fa + b_forget).astype(jnp.float32)).transpose(0, 2, 1)
    ya = _fox_attention(qa, ka, _heads(va, FOX_HEADS), logf)
    vmask = valid.astype(h.dtype)[:, None]
    qr = _rotary(_heads(qr, RET_HEADS), pos)
    kr = _rotary(_heads(kr, RET_HEADS), pos) * (RET_DK ** -0.5) * vmask
    vr = _heads(vr, RET_HEADS) * vmask
    o = _head_groupnorm(_retention(qr, kr, vr), ret_norm_g)
    yr = jax.nn.silu(gr) * o.astype(h.dtype)
    merged = jax.nn.sigmoid(gate_a) * (ya @ w_proj_fox) + jax.nn.sigmoid(gate_r) * (yr @ w_proj_ret)
    return merged @ w_out


def _peer(h, norm_g, w_q, keys_1, keys_2, u_tab, v_tab):
    b, l, d = h.shape
    x = _rmsnorm(h, norm_g)
    tokens = x.reshape(-1, PEER_CHUNK, d)

    def chunk(t):
        q = (t @ w_q).reshape(PEER_CHUNK, PEER_HEADS, PEER_QDIM)
        q1, q2 = q[..., :PEER_HALF], q[..., PEER_HALF:]
        s1 = jnp.einsum('thd,hnd->thn', q1, keys_1)
        s2 = jnp.einsum('thd,hnd->thn', q2, keys_2)
        v1, i1 = lax.top_k(s1, PEER_TOPK)
        v2, i2 = lax.top_k(s2, PEER_TOPK)
        cand = (v1[..., :, None] + v2[..., None, :]).reshape(PEER_CHUNK, PEER_HEADS, PEER_TOPK * PEER_TOPK)
        cid = (i1[..., :, None] * PEER_NKEYS + i2[..., None, :]).reshape(PEER_CHUNK, PEER_HEADS, PEER_TOPK * PEER_TOPK)
        sv, sp = lax.top_k(cand, PEER_TOPK)
        eid = jnp.take_along_axis(cid, sp, axis=-1)
        g = jax.nn.softmax(sv.astype(jnp.float32), axis=-1)
        act = jax.nn.gelu(jnp.einsum('td,thkd->thk', t, u_tab[eid]).astype(jnp.float32), approximate=False)
        return jnp.einsum('thk,thkd->td', (g * act).astype(t.dtype), v_tab[eid])

    y = lax.map(chunk, tokens)
    return y.reshape(b, l, d)


def setup_inputs(seed: int = 0) -> dict:
    key = jax.random.key(seed)
    ks = jax.random.split(key, 20)
    f32 = jnp.float32
    nrm = lambda k, shape, s: jax.random.normal(k, shape, f32) * s
    return {
        "x": nrm(ks[0], (BATCH, SEQ, D_MODEL), 1.0),
        "meta_tokens": nrm(ks[1], (N_META, D_MODEL), 1.0),
        "norm_mix_g": 1.0 + nrm(ks[2], (DEPTH, D_MODEL), 0.02),
        "w_in": nrm(ks[3], (DEPTH, D_MODEL, W_IN_COLS), D_MODEL ** -0.5),
        "b_forget": jnp.linspace(1.0, 5.0, FOX_HEADS, dtype=f32)[None, :] + nrm(ks[4], (DEPTH, FOX_HEADS), 0.1),
        "q_norm_g": 1.0 + nrm(ks[5], (DEPTH, FOX_HD), 0.02),
        "k_norm_g": 1.0 + nrm(ks[6], (DEPTH, FOX_HD), 0.02),
        "ret_norm_g": 1.0 + nrm(ks[7], (DEPTH, RET_V_W), 0.02),
        "w_proj_fox": nrm(ks[8], (DEPTH, FOX_W, D_MODEL), FOX_W ** -0.5),
        "w_proj_ret": nrm(ks[9], (DEPTH, RET_V_W, D_MODEL), RET_V_W ** -0.5),
        "w_out": nrm(ks[10], (DEPTH, D_MODEL, D_MODEL), D_MODEL ** -0.5),
        "norm_ffn_g": 1.0 + nrm(ks[11], (DEPTH, D_MODEL), 0.02),
        "peer_w_q": nrm(ks[12], (DEPTH, D_MODEL, PEER_HEADS * PEER_QDIM), D_MODEL ** -0.5),
        "peer_keys_1": nrm(ks[13], (DEPTH, PEER_HEADS, PEER_NKEYS, PEER_HALF), PEER_HALF ** -0.5),
        "peer_keys_2": nrm(ks[14], (DEPTH, PEER_HEADS, PEER_NKEYS, PEER_HALF), PEER_HALF ** -0.5),
        "peer_u": nrm(ks[15], (DEPTH, PEER_N, D_MODEL), D_MODEL ** -0.5),
        "peer_v": nrm(ks[16], (DEPTH, PEER_N, D_MODEL), (PEER_HEADS * PEER_TOPK) ** -0.5),
    }


def reference(x, meta_tokens, norm_mix_g, w_in, b_forget, q_norm_g, k_norm_g, ret_norm_g,
              w_proj_fox, w_proj_ret, w_out, norm_ffn_g, peer_w_q, peer_keys_1, peer_keys_2,
              peer_u, peer_v):
    b = x.shape[0]
    pad = jnp.zeros((b, PAD, D_MODEL), x.dtype)
    meta = jnp.broadcast_to(meta_tokens.astype(x.dtype)[None], (b, N_META, D_MODEL))
    h = jnp.concatenate([pad, meta, x], axis=1)
    lp = h.shape[1]
    pos = jnp.arange(lp, dtype=jnp.float32) - PAD
    valid = jnp.arange(lp) >= PAD
    for i in range(DEPTH):
        h = h + _mixer(h, pos, valid, norm_mix_g[i], w_in[i], b_forget[i], q_norm_g[i], k_norm_g[i],
                       ret_norm_g[i], w_proj_fox[i], w_proj_ret[i], w_out[i])
        h = h + _peer(h, norm_ffn_g[i], peer_w_q[i], peer_keys_1[i], peer_keys_2[i], peer_u[i], peer_v[i])
    return h[:, PAD + N_META:]
```

```python
import numpy as np
from contextlib import ExitStack
import concourse.bass as bass
import concourse.mybir as mybir
from concourse.bass_utils import run_bass_kernel_spmd

F32 = mybir.dt.float32
BF16 = mybir.dt.bfloat16
U32 = mybir.dt.uint32
AF = mybir.ActivationFunctionType
ALU = mybir.AluOpType
AX = mybir.AxisListType

N_META = 16
PAD = 112
EPS = 1e-6
H = 16
HD = 128
DV = 256
PH = 8
NK = 128
TOPK = 16
ROPE_BASE = 10000.0


class Res:
    __slots__ = ("w", "r", "name")

    def __init__(self, name=""):
        self.w = None
        self.r = {}
        self.name = name


class DSem:
    __slots__ = ("sem", "tot")


class Ins:
    __slots__ = ("eng", "fn", "waits", "sig", "sigval", "dsem", "dval", "idx", "ph")


class Tile:
    __slots__ = ("t", "res", "ds")

    def __init__(self, t, name):
        self.t = t
        self.res = Res(name)
        self.ds = None


class Pool:
    def __init__(self, tiles):
        self.tiles = tiles
        self.i = 0

    def next(self):
        t = self.tiles[self.i % len(self.tiles)]
        self.i += 1
        return t


class _Rec:
    def __getattr__(self, name):
        def f(*a, **k):
            self.call = (name, a, k)
            return self
        return f


class Sched:
    ENGS = ("pe", "act", "dve", "pool", "sp")

    def __init__(self, nc, stack):
        self.nc = nc
        self.stack = stack
        self.lists = {e: [] for e in self.ENGS}
        self.esem = {e: stack.enter_context(nc.semaphore("es_" + e)) for e in self.ENGS}
        self.dsems = []
        self.free_ds = []
        self.lastc = {e: None for e in self.ENGS}
        self.cur_phase = "p0"
        self.scopes = False

    def get_ds(self):
        if self.free_ds:
            return self.free_ds.pop()
        d = DSem()
        d.sem = self.stack.enter_context(self.nc.semaphore("ds%d" % len(self.dsems)))
        d.tot = 0
        self.dsems.append(d)
        return d

    def op(self, eng, fn, reads=(), writes=(), dsem=None):
        rec = _Rec()
        fn(rec)
        name, a, k = rec.call
        ins = Ins()
        ins.eng = eng
        ins.fn = lambda e: getattr(e, name)(*a, **k)
        ins.sig = False
        ins.sigval = 0
        ins.dsem = dsem
        ins.idx = len(self.lists[eng])
        ins.ph = self.cur_phase
        deps = {}

        def add(d):
            if d is None:
                return
            if d.dsem is not None:
                deps[("d", id(d.dsem))] = d
            else:
                if d.eng == eng and eng == "pe" and dsem is None:
                    return
                k = ("c", d.eng)
                if k not in deps or deps[k].idx < d.idx:
                    deps[k] = d
        for t in reads:
            add(t.res.w)
        for t in writes:
            add(t.res.w)
            for d in t.res.r.values():
                add(d)
        waits = []
        for d in deps.values():
            if d.dsem is not None:
                waits.append((d.dsem, d.dsem.tot))
            else:
                d.sig = True
                waits.append(d)
        ins.waits = waits
        if dsem is not None:
            dsem.tot += 16
            ins.dval = dsem.tot
        else:
            self.lastc[eng] = ins
        key = ("d", id(dsem)) if dsem is not None else ("c", eng)
        for t in reads:
            t.res.r[key] = ins
        for t in writes:
            t.res.w = ins
            t.res.r = {}
        self.lists[eng].append(ins)
        return ins

    def dma(self, q, out, in_, tile, load):
        if tile.ds is None:
            tile.ds = self.get_ds()
        fn = lambda e: e.dma_start(out=out, in_=in_)
        if load:
            return self.op(q, fn, writes=[tile], dsem=tile.ds)
        return self.op(q, fn, reads=[tile], dsem=tile.ds)

    def barrier(self, tiles=()):
        last = [self.lastc[e] for e in self.ENGS if self.lastc[e] is not None]
        dtot = [(d, d.tot) for d in self.dsems if d.tot > 0]
        for d in last:
            d.sig = True
        for e in self.ENGS:
            ins = self.op(e, lambda eng: eng.nop())
            ins.waits = list(last) + list(dtot)
        for t in tiles:
            if t.ds is not None:
                self.free_ds.append(t.ds)
                t.ds = None

    def emit(self):
        nc = self.nc
        for e in self.ENGS:
            c = 0
            for ins in self.lists[e]:
                if ins.dsem is None and ins.sig:
                    c += 1
                    ins.sigval = c
        with nc.Block() as block:
            def run(e):
                def body(eng):
                    waited = {}
                    cur = None
                    for ins in self.lists[e]:
                        if self.scopes and ins.ph != cur:
                            if cur is not None:
                                scope.__exit__(None, None, None)
                            scope = nc.named_scope(ins.ph)
                            scope.__enter__()
                            cur = ins.ph
                        for w in ins.waits:
                            if isinstance(w, tuple):
                                sem, val = w[0].sem, w[1]
                            else:
                                sem, val = self.esem[w.eng], w.sigval
                            k = id(sem)
                            if waited.get(k, 0) >= val:
                                continue
                            waited[k] = val
                            eng.wait_ge(sem, val)
                        bi = ins.fn(eng)
                        if ins.dsem is not None:
                            bi.then_inc(ins.dsem.sem, 16)
                        elif ins.sig:
                            bi.then_inc(self.esem[e], 1)
                    if e == "sp":
                        for d in self.dsems:
                            if d.tot > 0:
                                eng.wait_ge(d.sem, d.tot)
                    if cur is not None:
                        scope.__exit__(None, None, None)
                return body
            block.tensor(run("pe"))
            block.scalar(run("act"))
            block.vector(run("dve"))
            block.gpsimd(run("pool"))
            block.sync(run("sp"))


def build(D, NCTX, NOWN, dbg=False, scopes=False):
    KC = D // 128
    T = NCTX * 128
    TO = NOWN * 128
    OWN0 = NCTX - NOWN
    NE = NK * NK
    nc = bass.Bass("TRN2", target_bir_lowering=False)

    def din(name, shape, dt=F32):
        return nc.dram_tensor(name, shape, dt, kind="ExternalInput").ap()

    def dscr(name, shape, dt=BF16):
        return nc.dram_tensor(name, shape, dt, kind="ExternalOutput" if dbg else "Internal").ap()

    ctx_x = din("ctx_x", [T, D])
    valid_d = din("valid", [128, NCTX])
    cs_d = din("cossin", [T, 128])
    ktab_d = din("ktab", [T, H])
    qtab_d = din("qtab", [TO, H])
    gmix_d = din("norm_mix_g", [D])
    w_in = din("w_in", [D, 26640 - 8192 + 2 * D])
    bf_d = din("b_forget", [H])
    qg_d = din("q_norm_g", [HD])
    kg_d = din("k_norm_g", [HD])
    rg_d = din("ret_norm_g", [H * DV])
    wpf = din("w_proj_fox", [H * HD, D])
    wpr = din("w_proj_ret", [H * DV, D])
    wout = din("w_out", [D, D])
    gffn_d = din("norm_ffn_g", [D])
    wq = din("peer_w_q", [D, PH * 256])
    k1t = din("k1t", [PH, 128, NK])
    k2t = din("k2t", [PH, 128, NK])
    ut_d = din("ut", [NE // 256, 128, KC, 256])
    pv_d = din("peer_v", [NE, D])
    out_d = nc.dram_tensor("out", [TO, D], F32, kind="ExternalOutput").ap()

    uT_d = dscr("uT", [128, KC, T])
    kT_d = dscr("kT", [H, 128, T])
    vf_d = dscr("vf", [T, H, 129])
    krw_d = dscr("krw", [T, H, 128])
    krT_d = dscr("krT", [H, 128, TO])
    vr_d = dscr("vr", [T, H * DV])
    qT_d = dscr("qT", [H, 128, TO])
    qrT_d = dscr("qrT", [H, 128, TO])
    sg_d = dscr("sg", [TO, H * DV])
    ga_d = dscr("ga", [TO, D])
    gr_d = dscr("gr", [TO, D])
    ya_d = dscr("ya", [TO, H * HD])
    yr_d = dscr("yr", [TO, H * DV])
    h1_d = dscr("h1", [TO, D], F32)
    xnT_d = dscr("xnT", [128, KC, TO])
    qpT_d = dscr("qpT", [2 * PH, 128, TO])
    G_d = dscr("G", [NK, 128, TO])
    oraw_d = dscr("oraw", [H, TO, DV], F32) if dbg else None
    stt_d = dscr("sttd", [H, 128, DV], BF16) if dbg else None

    c_qa, c_ka, c_va, c_fa = 0, 2048, 4096, 6144
    c_qr = 6160
    c_kr = c_qr + 2048
    c_vr = c_kr + 2048
    c_gr = c_vr + 4096
    c_ga = c_gr + 4096
    c_gtr = c_ga + D

    with ExitStack() as st:
        S = Sched(nc, st)
        S.scopes = scopes
        psf = [Tile(st.enter_context(nc.psum_tensor("psf%d" % i, [128, 512], F32)), "psf%d" % i) for i in range(6)]
        psb = [Tile(st.enter_context(nc.psum_tensor("psb%d" % i, [128, 1024], BF16)), "psb%d" % i) for i in range(2)]
        PSA = Pool(psf[0:4])
        PSACC = Pool(psf[4:6])
        PSB = Pool(psb)

        cnt = [0]

        class Phase:
            def __init__(self):
                self.st = ExitStack()
                self.tiles = []

            def sb(self, name, shape, dt):
                cnt[0] += 1
                name = "s%d_%s" % (cnt[0], name)
                t = Tile(self.st.enter_context(nc.sbuf_tensor(name, shape, dt)), name)
                self.tiles.append(t)
                return t

            def pool(self, name, n, shape, dt):
                return Pool([self.sb("%s%d" % (name, i), shape, dt) for i in range(n)])

            def close(self):
                S.barrier(self.tiles + psf + psb)
                self.st.close()

        P0 = Phase()
        ident = P0.sb("ident", [128, 128], BF16)
        identf = P0.sb("identf", [128, 128], F32)
        tri = P0.sb("tri", [128, 128], BF16)
        trif = P0.sb("trif", [128, 128], F32)
        onesf = P0.sb("onesf", [128, 128], F32)
        iotar = P0.sb("iotar", [128, 128], F32)
        valid = P0.sb("valid", [128, NCTX], F32)
        ktab = P0.sb("ktab", [128, NCTX, H], F32)
        qtab = P0.sb("qtab", [128, NOWN, H], F32)
        Lfull = P0.sb("Lfull", [128, NCTX, H], F32)
        Lb = P0.sb("Lb", [128, NCTX + 1, H], F32)
        ctmp = P0.sb("ctmp", [128, 128], F32)
        mhalf = P0.sb("mhalf", [128, 8], F32)
        S.op("pool", lambda e: e.memset(mhalf.t[:], -0.5), writes=[mhalf])
        S.op("pool", lambda e: e.iota(ctmp.t[:], pattern=[[1, 128]], base=0, channel_multiplier=-1,
                                      allow_small_or_imprecise_dtypes=True), writes=[ctmp])
        S.op("dve", lambda e: e.tensor_single_scalar(out=identf.t[:], in_=ctmp.t[:], scalar=0.0, op=ALU.is_equal),
             reads=[ctmp], writes=[identf])
        S.op("dve", lambda e: e.tensor_copy(out=ident.t[:], in_=identf.t[:]), reads=[identf], writes=[ident])
        S.op("dve", lambda e: e.tensor_single_scalar(out=trif.t[:], in_=ctmp.t[:], scalar=0.0, op=ALU.is_ge),
             reads=[ctmp], writes=[trif])
        S.op("dve", lambda e: e.tensor_copy(out=tri.t[:], in_=trif.t[:]), reads=[trif], writes=[tri])
        S.op("pool", lambda e: e.memset(onesf.t[:], 1.0), writes=[onesf])
        S.op("pool", lambda e: e.iota(iotar.t[:], pattern=[[1, 128]], base=0, channel_multiplier=0,
                                      allow_small_or_imprecise_dtypes=True), writes=[iotar])
        S.op("pool", lambda e: e.memset(Lb.t[:, 0, :], 0.0), writes=[Lb])
        S.dma("sp", valid.t[:], valid_d, valid, True)
        S.dma("sp", ktab.t[:], ktab_d.rearrange("(b p) h -> p b h", p=128), ktab, True)
        S.dma("sp", qtab.t[:], qtab_d.rearrange("(b p) h -> p b h", p=128), qtab, True)

        def bcast_load(ph, name, src, n, scale=None):
            t = ph.sb(name, [128, n], F32)
            S.dma("sp", t.t[:], src.partition_broadcast(128), t, True)
            if scale is not None:
                S.op("dve", lambda e: e.tensor_scalar(out=t.t[:], in0=t.t[:], scalar1=float(scale), scalar2=None,
                                                      op0=ALU.mult), reads=[t], writes=[t])
            return t

        def rmsnorm_T(ph, nblk, src_rows, g_bc, dstT, xpool, jpool, npool, opool):
            ss = ph.pool("ss", 2, [128, 2], F32)
            xq = []

            def xload(b):
                xt_ = xpool.next()
                S.dma("sp", xt_.t[:], src_rows(b), xt_, True)
                xq.append(xt_)
            for b in range(min(2, nblk)):
                xload(b)
            for blk in range(nblk):
                xt = xq[blk]
                if blk + 2 < nblk:
                    xload(blk + 2)
                jk = jpool.next()
                s1 = ss.next()
                S.op("pool", lambda e, s1=s1: e.memset(s1.t[:], 0.0), writes=[s1])
                S.op("act", lambda e, xt=xt, jk=jk, s1=s1: e.activation(out=jk.t[:], in_=xt.t[:], func=AF.Square,
                                                                         accum_out=s1.t[:, 0:1]),
                     reads=[xt], writes=[jk, s1])
                S.op("dve", lambda e, s1=s1: e.tensor_scalar(out=s1.t[:, 1:2], in0=s1.t[:, 0:1], scalar1=1.0 / D,
                                                              scalar2=EPS, op0=ALU.mult, op1=ALU.add),
                     reads=[s1], writes=[s1])
                S.op("pool", lambda e, s1=s1: e.tensor_tensor(out=s1.t[:, 1:2], in0=s1.t[:, 1:2], in1=mhalf.t[:, 0:1], op=ALU.pow),
                     reads=[s1, mhalf], writes=[s1])
                xn = npool.next()
                S.op("dve", lambda e, xt=xt, xn=xn, s1=s1: e.scalar_tensor_tensor(
                    out=xn.t[:], in0=xt.t[:], scalar=s1.t[:, 1:2], in1=g_bc.t[:], op0=ALU.mult, op1=ALU.mult),
                    reads=[xt, s1, g_bc], writes=[xn])
                ot = opool.next()
                for k8 in range(0, KC, 8):
                    pb = PSB.next()
                    n8 = min(8, KC - k8)
                    for j in range(n8):
                        S.op("pe", lambda e, pb=pb, xn=xn, j=j, k8=k8: e.transpose(
                            pb.t[:, j * 128:(j + 1) * 128], xn.t[:, (k8 + j) * 128:(k8 + j + 1) * 128], ident.t[:]),
                            reads=[xn, ident], writes=[pb])
                    eng = "act" if (k8 // 8) % 2 == 0 else "dve"
                    if eng == "act":
                        S.op("act", lambda e, pb=pb, ot=ot, k8=k8, n8=n8: e.copy(
                            out=ot.t[:, k8:k8 + n8, :], in_=pb.t[:, 0:n8 * 128].rearrange("p (k t) -> p k t", t=128)),
                            reads=[pb], writes=[ot])
                    else:
                        S.op("dve", lambda e, pb=pb, ot=ot, k8=k8, n8=n8: e.tensor_copy(
                            out=ot.t[:, k8:k8 + n8, :], in_=pb.t[:, 0:n8 * 128].rearrange("p (k t) -> p k t", t=128)),
                            reads=[pb], writes=[ot])
                S.dma("sp", dstT[:, :, blk * 128:(blk + 1) * 128], ot.t[:], ot, False)

        S.cur_phase = "p1"
        P1 = Phase()
        g_bc = bcast_load(P1, "gmix", gmix_d, D)
        rmsnorm_T(P1, NCTX, lambda blk: ctx_x[blk * 128:(blk + 1) * 128, :], g_bc, uT_d,
                  P1.pool("x", 3, [128, D], F32), P1.pool("jk", 1, [128, D], BF16),
                  P1.pool("xn", 2, [128, D], BF16), P1.pool("uo", 2, [128, KC, 128], BF16))
        P1.close()

        def gemm_multi(xT, nblk, kcs, wpool, jobs):
            nkc = len(kcs)
            tiles = []
            for (W, N, evac) in jobs:
                for nt in range((N + 511) // 512):
                    tiles.append((W, nt, min(512, N - nt * 512), evac))

            def load(ti):
                W, nt, nw, _ = tiles[ti]
                wt = wpool.next()
                n0 = nt * 512
                for k4 in range(0, nkc, 8):
                    k5 = min(nkc, k4 + 8)
                    S.dma("pool", wt.t[:, k4:k5, 0:nw],
                          W[k4 * 128:k5 * 128, n0:n0 + nw].rearrange("(kc p) n -> p kc n", p=128), wt, True)
                return wt
            nxt = load(0)
            for ti, (W, nt, nw, evac) in enumerate(tiles):
                wt = nxt
                if ti + 1 < len(tiles):
                    nxt = load(ti + 1)
                for tb in range(nblk):
                    ps = PSA.next()
                    for i, kc in enumerate(kcs):
                        S.op("pe", lambda e, ps=ps, wt=wt, i=i, kc=kc, tb=tb, nw=nw: e.matmul(
                            ps.t[:, 0:nw], lhsT=xT.t[:, kc, tb * 128:(tb + 1) * 128], rhs=wt.t[:, i, 0:nw],
                            start=(i == 0), stop=(i == nkc - 1)), reads=[xT, wt], writes=[ps])
                    evac(tb, nt, ps, nw)

        def gemm(xT, nblk, kcs, W, N, wpool, evac, pre=None):
            gemm_multi(xT, nblk, kcs, wpool, [(W, N, evac)])

        def load_xT(xT, srcT, b0, nb):
            for k4 in range(0, KC, 8):
                k5 = min(KC, k4 + 8)
                S.dma("sp", xT.t[:, k4:k5, 0:nb * 128], srcT[:, k4:k5, b0 * 128:(b0 + nb) * 128], xT, True)

        def qknorm_T(ph, ps, g_t, dst, h0, col0, ssp, knp, ktp):
            s4 = ssp.next()
            kn = knp.next()
            S.op("pool", lambda e: e.memset(s4.t[:], 0.0), writes=[s4])
            for h in range(4):
                S.op("act", lambda e, h=h: e.activation(out=kn.t[:, h * 128:(h + 1) * 128], in_=ps.t[:, h * 128:(h + 1) * 128],
                                                         func=AF.Square, accum_out=s4.t[:, h:h + 1]),
                     reads=[ps], writes=[kn, s4])
            S.op("dve", lambda e: e.tensor_scalar(out=s4.t[:, 4:8], in0=s4.t[:, 0:4], scalar1=1.0 / HD, scalar2=EPS,
                                                  op0=ALU.mult, op1=ALU.add), reads=[s4], writes=[s4])
            S.op("pool", lambda e: e.tensor_tensor(out=s4.t[:, 4:8], in0=s4.t[:, 4:8], in1=mhalf.t[:, 0:4], op=ALU.pow),
                 reads=[s4, mhalf], writes=[s4])
            for h in range(4):
                S.op("dve", lambda e, h=h: e.scalar_tensor_tensor(
                    out=kn.t[:, h * 128:(h + 1) * 128], in0=ps.t[:, h * 128:(h + 1) * 128], scalar=s4.t[:, 4 + h:5 + h],
                    in1=g_t.t[:], op0=ALU.mult, op1=ALU.mult), reads=[ps, s4, g_t], writes=[kn])
            transpose4(kn, dst, h0, col0, ktp)

        def transpose4(kn, dst, h0, col0, ktp):
            pb = PSB.next()
            for h in range(4):
                S.op("pe", lambda e, h=h: e.transpose(pb.t[:, h * 128:(h + 1) * 128], kn.t[:, h * 128:(h + 1) * 128],
                                                      ident.t[:]), reads=[kn, ident], writes=[pb])
            kt = ktp.next()
            S.op("act", lambda e: e.copy(out=kt.t[:], in_=pb.t[:, 0:512]), reads=[pb], writes=[kt])
            S.dma("sp", dst[h0:h0 + 4, :, col0:col0 + 128].rearrange("h p t -> p h t"),
                  kt.t[:].rearrange("p (h t) -> p h t", t=128), kt, False)

        def rotary(ph, ps, cs, tabcol, ro_p, kn_p):
            ro = ro_p.next()
            pv = ps.t[:, 0:512].rearrange("p (h two f) -> p h two f", h=4, two=2)
            rv = ro.t[:].rearrange("p (a h f) -> p a h f", a=4, h=4)
            cosb = cs.t[:, 0:64].unsqueeze(1).to_broadcast([128, 4, 64])
            sinb = cs.t[:, 64:128].unsqueeze(1).to_broadcast([128, 4, 64])
            S.op("dve", lambda e: e.tensor_tensor(out=rv[:, 0], in0=pv[:, :, 0, :], in1=cosb, op=ALU.mult), reads=[ps, cs], writes=[ro])
            S.op("dve", lambda e: e.tensor_tensor(out=rv[:, 1], in0=pv[:, :, 1, :], in1=sinb, op=ALU.mult), reads=[ps, cs], writes=[ro])
            S.op("dve", lambda e: e.tensor_tensor(out=rv[:, 2], in0=pv[:, :, 0, :], in1=sinb, op=ALU.mult), reads=[ps, cs], writes=[ro])
            S.op("dve", lambda e: e.tensor_tensor(out=rv[:, 3], in0=pv[:, :, 1, :], in1=cosb, op=ALU.mult), reads=[ps, cs], writes=[ro])
            S.op("pool", lambda e: e.tensor_tensor(out=rv[:, 0], in0=rv[:, 0], in1=rv[:, 1], op=ALU.subtract), reads=[ro], writes=[ro])
            S.op("pool", lambda e: e.tensor_tensor(out=rv[:, 2], in0=rv[:, 2], in1=rv[:, 3], op=ALU.add), reads=[ro], writes=[ro])
            kn = kn_p.next()
            kv = kn.t[:].rearrange("p (h two f) -> p h two f", h=4, two=2)
            tb_ = tabcol.unsqueeze(2).to_broadcast([128, 4, 64])
            S.op("dve", lambda e: e.tensor_tensor(out=kv[:, :, 0, :], in0=rv[:, 0], in1=tb_, op=ALU.mult), reads=[ro, ktab, qtab], writes=[kn])
            S.op("dve", lambda e: e.tensor_tensor(out=kv[:, :, 1, :], in0=rv[:, 2], in1=tb_, op=ALU.mult), reads=[ro, ktab, qtab], writes=[kn])
            return kn

        TT2 = min(NCTX, 11)
        S.cur_phase = "p2"
        P2 = Phase()
        xT2 = P2.sb("xT2", [128, KC, TT2 * 128], BF16)
        wp2 = P2.pool("w", 2, [128, KC, 512], BF16)
        kg_bc = bcast_load(P2, "kg", kg_d, HD)
        bf_bc = bcast_load(P2, "bfb", bf_d, H)
        ss4 = P2.pool("s4", 3, [128, 8], F32)
        knp = P2.pool("kn", 3, [128, 512], BF16)
        ktp = P2.pool("kt", 3, [128, 512], BF16)
        vsp = P2.pool("vs", 3, [128, 4, 129], BF16)
        rop = P2.pool("ro", 2, [128, 1024], F32)
        csp = P2.pool("cs", 3, [128, 128], F32)
        fz = P2.pool("fz", 2, [128, 48], F32)
        for b0 in range(0, NCTX, TT2):
            nb = min(TT2, NCTX - b0)
            load_xT(xT2, uT_d, b0, nb)

            def ev_ka(tb, nt, ps, nw, b0=b0):
                qknorm_T(P2, ps, kg_bc, kT_d, nt * 4, (b0 + tb) * 128, ss4, knp, ktp)

            def ev_va(tb, nt, ps, nw, b0=b0):
                blk = b0 + tb
                vs = vsp.next()
                S.op("dve", lambda e: e.tensor_scalar(out=vs.t[:, :, 0:128], in0=ps.t[:, 0:512].rearrange("p (h f) -> p h f", h=4),
                                                      scalar1=valid.t[:, blk:blk + 1], scalar2=None, op0=ALU.mult),
                     reads=[ps, valid], writes=[vs])
                S.op("pool", lambda e: e.tensor_copy(out=vs.t[:, :, 128:129],
                                                     in_=valid.t[:, blk:blk + 1].unsqueeze(1).to_broadcast([128, 4, 1])),
                     reads=[valid], writes=[vs])
                S.dma("sp", vf_d[blk * 128:(blk + 1) * 128, nt * 4:nt * 4 + 4, :], vs.t[:], vs, False)

            def ev_fa(tb, nt, ps, nw, b0=b0):
                blk = b0 + tb
                z = fz.next()
                S.op("dve", lambda e: e.tensor_tensor(out=z.t[:, 0:16], in0=ps.t[:, 0:16], in1=bf_bc.t[:], op=ALU.add),
                     reads=[ps, bf_bc], writes=[z])
                S.op("act", lambda e: e.activation(out=z.t[:, 16:32], in_=z.t[:, 0:16], func=AF.Exp, scale=-1.0), reads=[z], writes=[z])
                S.op("act", lambda e: e.activation(out=z.t[:, 32:48], in_=z.t[:, 16:32], func=AF.Ln, bias=1.0), reads=[z], writes=[z])
                p2 = PSA.next()
                S.op("pe", lambda e: e.matmul(p2.t[:, 0:16], lhsT=trif.t[:], rhs=z.t[:, 32:48], start=True, stop=True),
                     reads=[trif, z], writes=[p2])
                S.op("pe", lambda e: e.matmul(p2.t[:, 16:32], lhsT=onesf.t[:], rhs=z.t[:, 32:48], start=True, stop=True),
                     reads=[onesf, z], writes=[p2])
                S.op("dve", lambda e: e.tensor_tensor(out=Lfull.t[:, blk, :], in0=p2.t[:, 0:16], in1=Lb.t[:, blk, :], op=ALU.add),
                     reads=[p2, Lb], writes=[Lfull])
                S.op("dve", lambda e: e.tensor_tensor(out=Lb.t[:, blk + 1, :], in0=p2.t[:, 16:32], in1=Lb.t[:, blk, :], op=ALU.add),
                     reads=[p2, Lb], writes=[Lb])

            def ev_kr(tb, nt, ps, nw, b0=b0):
                blk = b0 + tb
                cs = csp.next()
                S.dma("sp", cs.t[:], cs_d[blk * 128:(blk + 1) * 128, :], cs, True)
                kn = rotary(P2, ps, cs, ktab.t[:, blk, nt * 4:nt * 4 + 4], rop, knp)
                if blk < OWN0:
                    S.dma("sp", krw_d[blk * 128:(blk + 1) * 128, nt * 4:nt * 4 + 4, :],
                          kn.t[:].rearrange("p (h f) -> p h f", h=4), kn, False)
                else:
                    transpose4(kn, krT_d, nt * 4, (blk - OWN0) * 128, ktp)

            def ev_vr(tb, nt, ps, nw, b0=b0):
                blk = b0 + tb
                vs = knp.next()
                S.op("act", lambda e: e.activation(out=vs.t[:], in_=ps.t[:, 0:512], func=AF.Copy,
                                                   scale=valid.t[:, blk:blk + 1]), reads=[ps, valid], writes=[vs])
                S.dma("sp", vr_d[blk * 128:(blk + 1) * 128, nt * 512:(nt + 1) * 512], vs.t[:], vs, False)

            kcs = list(range(KC))
            gemm_multi(xT2, nb, kcs, wp2, [
                (w_in[:, c_fa:c_fa + 16], 16, ev_fa),
                (w_in[:, c_ka:c_ka + 2048], 2048, ev_ka),
                (w_in[:, c_va:c_va + 2048], 2048, ev_va),
                (w_in[:, c_kr:c_kr + 2048], 2048, ev_kr),
                (w_in[:, c_vr:c_vr + 4096], 4096, ev_vr)])
        P2.close()

        TT3 = min(NOWN, 8)
        S.cur_phase = "p3"
        P3 = Phase()
        xT3 = P3.sb("xT3", [128, KC, TT3 * 128], BF16)
        wp3 = P3.pool("w", 2, [128, KC, 512], BF16)
        qg_bc = bcast_load(P3, "qg", qg_d, HD, scale=HD ** -0.5)
        ss4 = P3.pool("s4", 3, [128, 8], F32)
        knp = P3.pool("kn", 3, [128, 512], BF16)
        ktp = P3.pool("kt", 3, [128, 512], BF16)
        rop = P3.pool("ro", 2, [128, 1024], F32)
        csp = P3.pool("cs", 3, [128, 128], F32)
        for b0 in range(0, NOWN, TT3):
            nb = min(TT3, NOWN - b0)
            load_xT(xT3, uT_d, OWN0 + b0, nb)

            def ev_qa(tb, nt, ps, nw, b0=b0):
                qknorm_T(P3, ps, qg_bc, qT_d, nt * 4, (b0 + tb) * 128, ss4, knp, ktp)

            def ev_qr(tb, nt, ps, nw, b0=b0):
                blk = b0 + tb
                cs = csp.next()
                S.dma("sp", cs.t[:], cs_d[(OWN0 + blk) * 128:(OWN0 + blk + 1) * 128, :], cs, True)
                kn = rotary(P3, ps, cs, qtab.t[:, blk, nt * 4:nt * 4 + 4], rop, knp)
                transpose4(kn, qrT_d, nt * 4, blk * 128, ktp)

            def ev_act(dst, func):
                def ev(tb, nt, ps, nw, b0=b0):
                    blk = b0 + tb
                    vs = knp.next()
                    S.op("act", lambda e: e.activation(out=vs.t[:], in_=ps.t[:, 0:512], func=func), reads=[ps], writes=[vs])
                    S.dma("sp", dst[blk * 128:(blk + 1) * 128, nt * 512:(nt + 1) * 512], vs.t[:], vs, False)
                return ev
            kcs = list(range(KC))
            gemm_multi(xT3, nb, kcs, wp3, [
                (w_in[:, c_qa:c_qa + 2048], 2048, ev_qa),
                (w_in[:, c_qr:c_qr + 2048], 2048, ev_qr),
                (w_in[:, c_gr:c_gr + 4096], 4096, ev_act(sg_d, AF.Silu)),
                (w_in[:, c_ga:c_ga + D], D, ev_act(ga_d, AF.Sigmoid)),
                (w_in[:, c_gtr:c_gtr + D], D, ev_act(gr_d, AF.Sigmoid))])
        P3.close()

        S.cur_phase = "p4"
        P4 = Phase()
        qTp = P4.pool("qT", 2, [128, TO], BF16)
        kTp = P4.pool("kT", 2, [128, T], BF16)
        vp = P4.pool("v", 2, [128, NCTX, 129], BF16)
        bip = P4.pool("bias", 2, [128, NOWN, NCTX], F32)
        ptp = P4.pool("pt", 6, [128, 128], BF16)
        yap = P4.pool("yas", 2, [128, NOWN, 128], BF16)
        rcp = P4.pool("rc", 2, [128, 1], F32)
        def p4load(h):
            qT = qTp.next(); kT = kTp.next(); vv = vp.next()
            S.dma("sp", qT.t[:], qT_d[h], qT, True)
            S.dma("sp", kT.t[:], kT_d[h], kT, True)
            S.dma("sp", vv.t[:], vf_d[:, h, :].rearrange("(b p) f -> p b f", p=128), vv, True)
            return qT, kT, vv
        nx4 = p4load(0)
        for h in range(H):
            qT, kT, vv = nx4
            if h + 1 < H:
                nx4 = p4load(h + 1)
            yas = yap.next()
            biasT = bip.next()
            for i in range(NOWN):
                gi = OWN0 + i
                S.op("dve", lambda e, i=i, gi=gi: e.tensor_scalar(
                    out=biasT.t[:, i, 0:gi + 1], in0=Lfull.t[:, 0:gi + 1, h], scalar1=Lb.t[:, gi, h:h + 1], scalar2=None,
                    op0=ALU.subtract), reads=[Lfull, Lb], writes=[biasT])
            pairs = [(i, j) for i in range(NOWN) for j in range(OWN0 + i + 1)]
            pss = {}

            def qk(p):
                i, j = pairs[p]
                ps = PSA.next()
                pss[p] = ps
                S.op("pe", lambda e: e.matmul(ps.t[:, 0:128], lhsT=kT.t[:, j * 128:(j + 1) * 128], rhs=qT.t[:, i * 128:(i + 1) * 128],
                                              start=True, stop=True), reads=[kT, qT], writes=[ps])
            LA = 3
            for p in range(min(LA, len(pairs))):
                qk(p)
            acc = None
            for p, (i, j) in enumerate(pairs):
                if p + LA < len(pairs):
                    qk(p + LA)
                gi = OWN0 + i
                if j == 0:
                    acc = PSACC.next()
                ps = pss.pop(p)
                pt = ptp.next()
                S.op("act", lambda e: e.activation(out=pt.t[:], in_=ps.t[:, 0:128], func=AF.Exp, bias=biasT.t[:, i, j:j + 1], scale=1.0),
                     reads=[ps, biasT], writes=[pt])
                if j == gi:
                    S.op("pool", lambda e: e.tensor_tensor(out=pt.t[:], in0=pt.t[:], in1=tri.t[:], op=ALU.mult),
                         reads=[pt, tri], writes=[pt])
                S.op("pe", lambda e: e.matmul(acc.t[:, 0:129], lhsT=pt.t[:], rhs=vv.t[:, j, :], start=(j == 0), stop=(j == gi)),
                     reads=[pt, vv], writes=[acc])
                if j == gi:
                    rc = rcp.next()
                    S.op("dve", lambda e: e.reciprocal(out=rc.t[:], in_=acc.t[:, 128:129]), reads=[acc], writes=[rc])
                    S.op("dve", lambda e: e.tensor_scalar(out=yas.t[:, i, :], in0=acc.t[:, 0:128], scalar1=rc.t[:, 0:1], scalar2=None,
                                                          op0=ALU.mult), reads=[acc, rc], writes=[yas])
            S.dma("sp", ya_d[:, h * 128:(h + 1) * 128].rearrange("(b p) f -> p b f", p=128), yas.t[:], yas, False)
        P4.close()

        S.cur_phase = "p5"
        P5 = Phase()
        rg_bc = bcast_load(P5, "rg", rg_d, H * DV)
        NPV = max(OWN0, 1)
        krwp = P5.pool("krw", 2, [128, NPV, 128], BF16)
        vrp = P5.pool("vr", 2, [128, NCTX, DV], BF16)
        qrp = P5.pool("qrT", 2, [128, TO], BF16)
        krp = P5.pool("krT", 2, [128, TO], BF16)
        sgp = P5.pool("sg", 2, [128, NOWN, DV], BF16)
        stp = P5.pool("st", 2, [128, DV], BF16)
        ptp = P5.pool("pt", 6, [128, 128], BF16)
        yrp = P5.pool("yrs", 1, [128, NOWN, DV], BF16)
        smp = P5.pool("sm", 3, [128, 8], F32)
        jkp = P5.pool("jk", 2, [128, DV], F32)
        onp = P5.pool("on", 2, [128, DV], F32)
        def p5load(h):
            krw = krwp.next(); vr = vrp.next(); qr = qrp.next(); kr = krp.next(); sg = sgp.next()
            S.dma("sp", krw.t[:, 0:OWN0, :], krw_d[0:OWN0 * 128, h, :].rearrange("(b p) f -> p b f", p=128), krw, True)
            S.dma("sp", vr.t[:], vr_d[:, h * DV:(h + 1) * DV].rearrange("(b p) f -> p b f", p=128), vr, True)
            S.dma("sp", qr.t[:], qrT_d[h], qr, True)
            S.dma("sp", kr.t[:], krT_d[h], kr, True)
            S.dma("sp", sg.t[:], sg_d[:, h * DV:(h + 1) * DV].rearrange("(b p) f -> p b f", p=128), sg, True)
            return krw, vr, qr, kr, sg
        nx5 = p5load(0)
        for h in range(H):
            gam = 1.0 - 2.0 ** (-5.0 - h)
            krw, vr, qr, kr, sg = nx5
            if h + 1 < H:
                nx5 = p5load(h + 1)
            yrs = yrp.next()
            sp_ = PSA.next()
            for j in range(OWN0):
                S.op("pe", lambda e, j=j: e.matmul(sp_.t[:, 0:DV], lhsT=krw.t[:, j, :], rhs=vr.t[:, j, :],
                                                   start=(j == 0), stop=(j == OWN0 - 1)), reads=[krw, vr], writes=[sp_])
            stt = stp.next()
            S.op("act", lambda e, stt=stt, sp_=sp_, gam=gam: e.activation(out=stt.t[:], in_=sp_.t[:, 0:DV], func=AF.Copy, scale=float(gam)),
                 reads=[sp_], writes=[stt])
            if dbg:
                S.dma("sp", stt_d[h], stt.t[:], stt, False)
            pairs = [(i, j) for i in range(NOWN) for j in range(i + 1)]
            pss = {}

            def sk(p):
                i, j = pairs[p]
                ps = PSA.next()
                pss[p] = ps
                S.op("pe", lambda e: e.matmul(ps.t[:, 0:128], lhsT=kr.t[:, j * 128:(j + 1) * 128], rhs=qr.t[:, i * 128:(i + 1) * 128],
                                              start=True, stop=True), reads=[kr, qr], writes=[ps])
            LA = 3
            for p in range(min(LA, len(pairs))):
                sk(p)
            acc = None
            for p, (i, j) in enumerate(pairs):
                if p + LA < len(pairs):
                    sk(p + LA)
                if j == 0:
                    acc = PSACC.next()
                    S.op("pe", lambda e: e.matmul(acc.t[:, 0:DV], lhsT=qr.t[:, i * 128:(i + 1) * 128], rhs=stt.t[:],
                                                  start=True, stop=False), reads=[qr, stt], writes=[acc])
                ps = pss.pop(p)
                pt = ptp.next()
                if j == i:
                    S.op("dve", lambda e: e.tensor_tensor(out=pt.t[:], in0=ps.t[:, 0:128], in1=tri.t[:], op=ALU.mult),
                         reads=[ps, tri], writes=[pt])
                else:
                    S.op("act", lambda e: e.copy(out=pt.t[:], in_=ps.t[:, 0:128]), reads=[ps], writes=[pt])
                S.op("pe", lambda e: e.matmul(acc.t[:, 0:DV], lhsT=pt.t[:], rhs=vr.t[:, OWN0 + j, :], start=False, stop=(j == i)),
                     reads=[pt, vr], writes=[acc])
                if j != i:
                    continue
                sm = smp.next(); jk = jkp.next(); on = onp.next()
                S.op("pool", lambda e, sm=sm: e.memset(sm.t[:], 0.0), writes=[sm])
                S.op("act", lambda e, acc=acc, jk=jk, sm=sm: e.activation(out=jk.t[:], in_=acc.t[:, 0:DV], func=AF.Copy,
                                                                         accum_out=sm.t[:, 0:1]), reads=[acc], writes=[jk, sm])
                if dbg:
                    S.dma("sp", oraw_d[h, i * 128:(i + 1) * 128, :], jk.t[:], jk, False)
                S.op("act", lambda e, acc=acc, jk=jk, sm=sm: e.activation(out=jk.t[:], in_=acc.t[:, 0:DV], func=AF.Square,
                                                                         accum_out=sm.t[:, 1:2]), reads=[acc], writes=[jk, sm])
                S.op("dve", lambda e, sm=sm: e.tensor_scalar(out=sm.t[:, 2:4], in0=sm.t[:, 0:2], scalar1=1.0 / DV, scalar2=None, op0=ALU.mult),
                     reads=[sm], writes=[sm])
                S.op("dve", lambda e, sm=sm: e.tensor_tensor(out=sm.t[:, 4:5], in0=sm.t[:, 2:3], in1=sm.t[:, 2:3], op=ALU.mult),
                     reads=[sm], writes=[sm])
                S.op("dve", lambda e, sm=sm: e.tensor_tensor(out=sm.t[:, 5:6], in0=sm.t[:, 3:4], in1=sm.t[:, 4:5], op=ALU.subtract),
                     reads=[sm], writes=[sm])
                S.op("dve", lambda e, sm=sm: e.tensor_scalar(out=sm.t[:, 6:7], in0=sm.t[:, 5:6], scalar1=EPS, scalar2=None,
                                                              op0=ALU.add), reads=[sm], writes=[sm])
                S.op("pool", lambda e, sm=sm: e.tensor_tensor(out=sm.t[:, 6:7], in0=sm.t[:, 6:7], in1=mhalf.t[:, 0:1], op=ALU.pow),
                     reads=[sm, mhalf], writes=[sm])
                S.op("dve", lambda e, sm=sm, acc=acc, on=on: e.tensor_scalar(
                    out=on.t[:], in0=acc.t[:, 0:DV], scalar1=sm.t[:, 2:3], scalar2=sm.t[:, 6:7], op0=ALU.subtract, op1=ALU.mult),
                    reads=[acc, sm], writes=[on])
                S.op("pool", lambda e, on=on, h=h: e.tensor_tensor(out=on.t[:], in0=on.t[:], in1=rg_bc.t[:, h * DV:(h + 1) * DV], op=ALU.mult),
                     reads=[on, rg_bc], writes=[on])
                S.op("pool", lambda e, on=on, sg=sg, yrs=yrs, i=i: e.tensor_tensor(out=yrs.t[:, i, :], in0=on.t[:], in1=sg.t[:, i, :], op=ALU.mult),
                     reads=[on, sg], writes=[yrs])
            S.dma("sp", yr_d[:, h * DV:(h + 1) * DV].rearrange("(b p) f -> p b f", p=128), yrs.t[:], yrs, False)
        P5.close()

        TT6 = min(NOWN, 2)
        KA = H * HD // 128
        KR = H * DV // 128
        S.cur_phase = "p6"
        P6 = Phase()
        yT = P6.sb("yT", [128, KA + KR, TT6 * 128], BF16)
        mbf = P6.sb("mbf", [128, TT6, D], BF16)
        mT = P6.sb("mT", [128, KC, TT6 * 128], BF16)
        wp6 = P6.pool("w", 2, [128, 32, 512], BF16)
        wap6 = P6.pool("wa", 2, [128, KA, 512], BF16)
        ysp = P6.pool("ys", 1, [128, H * HD + H * DV], BF16)
        gtp = P6.pool("gt", 4, [128, 512], BF16)
        xp6 = P6.pool("x6", 3, [128, 512], F32)
        tp6 = P6.pool("t6", 4, [128, 512], F32)
        for b0 in range(0, NOWN, TT6):
            nb = min(TT6, NOWN - b0)
            for tb in range(nb):
                blk = b0 + tb
                ys = ysp.next()
                S.dma("sp", ys.t[:, 0:H * HD], ya_d[blk * 128:(blk + 1) * 128, :], ys, True)
                S.dma("sp", ys.t[:, H * HD:], yr_d[blk * 128:(blk + 1) * 128, :], ys, True)
                for k8 in range(0, KA + KR, 8):
                    pb = PSB.next()
                    for j in range(8):
                        S.op("pe", lambda e, pb=pb, ys=ys, j=j, k8=k8: e.transpose(
                            pb.t[:, j * 128:(j + 1) * 128], ys.t[:, (k8 + j) * 128:(k8 + j + 1) * 128], ident.t[:]),
                            reads=[ys, ident], writes=[pb])
                    S.op("act" if (k8 // 8) % 2 == 0 else "dve",
                         (lambda e, pb=pb, k8=k8, tb=tb: e.copy(out=yT.t[:, k8:k8 + 8, tb * 128:(tb + 1) * 128],
                                                               in_=pb.t[:].rearrange("p (k t) -> p k t", t=128)))
                         if (k8 // 8) % 2 == 0 else
                         (lambda e, pb=pb, k8=k8, tb=tb: e.tensor_copy(out=yT.t[:, k8:k8 + 8, tb * 128:(tb + 1) * 128],
                                                                      in_=pb.t[:].rearrange("p (k t) -> p k t", t=128))),
                         reads=[pb], writes=[yT])
            def p6load(nt):
                wa = wap6.next(); wr = wp6.next()
                for k4 in range(0, KA, 8):
                    S.dma("pool", wa.t[:, k4:k4 + 8, :], wpf[k4 * 128:(k4 + 8) * 128, nt * 512:(nt + 1) * 512].rearrange("(kc p) n -> p kc n", p=128), wa, True)
                for k4 in range(0, KR, 8):
                    S.dma("pool", wr.t[:, k4:k4 + 8, :], wpr[k4 * 128:(k4 + 8) * 128, nt * 512:(nt + 1) * 512].rearrange("(kc p) n -> p kc n", p=128), wr, True)
                return wa, wr
            nx6 = p6load(0)
            for nt in range(D // 512):
                wa, wr = nx6
                if nt + 1 < D // 512:
                    nx6 = p6load(nt + 1)
                for tb in range(nb):
                    blk = b0 + tb
                    pa = PSA.next(); pr = PSA.next()
                    for kc in range(KA):
                        S.op("pe", lambda e, pa=pa, wa=wa, kc=kc, tb=tb: e.matmul(pa.t[:], lhsT=yT.t[:, kc, tb * 128:(tb + 1) * 128],
                                                                                 rhs=wa.t[:, kc, :], start=(kc == 0), stop=(kc == KA - 1)),
                             reads=[yT, wa], writes=[pa])
                    for kc in range(KR):
                        S.op("pe", lambda e, pr=pr, wr=wr, kc=kc, tb=tb: e.matmul(pr.t[:], lhsT=yT.t[:, KA + kc, tb * 128:(tb + 1) * 128],
                                                                                 rhs=wr.t[:, kc, :], start=(kc == 0), stop=(kc == KR - 1)),
                             reads=[yT, wr], writes=[pr])
                    g1 = gtp.next(); g2 = gtp.next(); t1 = tp6.next(); t2 = tp6.next()
                    S.dma("sp", g1.t[:], ga_d[blk * 128:(blk + 1) * 128, nt * 512:(nt + 1) * 512], g1, True)
                    S.dma("sp", g2.t[:], gr_d[blk * 128:(blk + 1) * 128, nt * 512:(nt + 1) * 512], g2, True)
                    S.op("dve", lambda e, t1=t1, pa=pa, g1=g1: e.tensor_tensor(out=t1.t[:], in0=pa.t[:], in1=g1.t[:], op=ALU.mult), reads=[pa, g1], writes=[t1])
                    S.op("dve", lambda e, t2=t2, pr=pr, g2=g2: e.tensor_tensor(out=t2.t[:], in0=pr.t[:], in1=g2.t[:], op=ALU.mult), reads=[pr, g2], writes=[t2])
                    S.op("dve", lambda e, t1=t1, t2=t2, tb=tb, nt=nt: e.tensor_tensor(out=mbf.t[:, tb, nt * 512:(nt + 1) * 512], in0=t1.t[:], in1=t2.t[:], op=ALU.add),
                         reads=[t1, t2], writes=[mbf])
            for tb in range(nb):
                for k8 in range(0, KC, 8):
                    pb = PSB.next()
                    n8 = min(8, KC - k8)
                    for j in range(n8):
                        S.op("pe", lambda e, pb=pb, j=j, k8=k8, tb=tb: e.transpose(
                            pb.t[:, j * 128:(j + 1) * 128], mbf.t[:, tb, (k8 + j) * 128:(k8 + j + 1) * 128], ident.t[:]),
                            reads=[mbf, ident], writes=[pb])
                    S.op("act", lambda e, pb=pb, k8=k8, n8=n8, tb=tb: e.copy(out=mT.t[:, k8:k8 + n8, tb * 128:(tb + 1) * 128],
                                                                            in_=pb.t[:, 0:n8 * 128].rearrange("p (k t) -> p k t", t=128)),
                         reads=[pb], writes=[mT])

            def ev_out(tb, nt, ps, nw, b0=b0):
                blk = b0 + tb
                xt = xp6.next()
                S.dma("sp", xt.t[:], ctx_x[(OWN0 + blk) * 128:(OWN0 + blk + 1) * 128, nt * 512:(nt + 1) * 512], xt, True)
                S.op("dve", lambda e: e.tensor_tensor(out=xt.t[:], in0=ps.t[:], in1=xt.t[:], op=ALU.add), reads=[ps, xt], writes=[xt])
                S.dma("sp", h1_d[blk * 128:(blk + 1) * 128, nt * 512:(nt + 1) * 512], xt.t[:], xt, False)
            gemm(mT, nb, list(range(KC)), wout, D, wp6, ev_out)
        P6.close()

        S.cur_phase = "p7"
        P7 = Phase()
        gf_bc = bcast_load(P7, "gffn", gffn_d, D)
        rmsnorm_T(P7, NOWN, lambda blk: h1_d[blk * 128:(blk + 1) * 128, :], gf_bc, xnT_d,
                  P7.pool("x", 3, [128, D], F32), P7.pool("jk", 1, [128, D], BF16),
                  P7.pool("xn", 2, [128, D], BF16), P7.pool("uo", 2, [128, KC, 128], BF16))
        P7.close()

        TT7 = min(NOWN, 8)
        S.cur_phase = "p7b"
        P7b = Phase()
        xT7 = P7b.sb("xT7", [128, KC, TT7 * 128], BF16)
        wp7 = P7b.pool("w", 2, [128, KC, 512], BF16)
        knp = P7b.pool("kn", 3, [128, 512], BF16)
        ktp = P7b.pool("kt", 3, [128, 512], BF16)
        for b0 in range(0, NOWN, TT7):
            nb = min(TT7, NOWN - b0)
            load_xT(xT7, xnT_d, b0, nb)

            def ev_q(tb, nt, ps, nw, b0=b0):
                kn = knp.next()
                S.op("act", lambda e: e.copy(out=kn.t[:], in_=ps.t[:, 0:512]), reads=[ps], writes=[kn])
                transpose4(kn, qpT_d, nt * 4, (b0 + tb) * 128, ktp)
            gemm(xT7, nb, list(range(KC)), wq, PH * 256, wp7, ev_q)
        P7b.close()

        S.cur_phase = "p7c"
        P7c = Phase()
        kk = P7c.sb("kk", [128, 2 * PH, NK], BF16)
        for hh in range(PH):
            S.dma("pool", kk.t[:, 2 * hh, :], k1t[hh], kk, True)
            S.dma("pool", kk.t[:, 2 * hh + 1, :], k2t[hh], kk, True)
        qpp = P7c.pool("qp", 2, [128, 2 * PH, 128], BF16)
        scp = P7c.pool("sc", 2, [128, 2 * PH, NK], F32)
        tmpp = P7c.pool("tmp", 2, [128, 256], F32)
        v12p = P7c.pool("v12", 2, [128, 2 * PH, 16], F32)
        idxp = P7c.pool("idx", 2, [128, PH, 16], U32)
        idfp = P7c.pool("idf", 2, [128, PH * 16], F32)
        candp = P7c.pool("cand", 1, [128, PH, 256], F32)
        tvp = P7c.pool("tv", 2, [128, PH, 16], F32)
        smp = P7c.pool("sm", 2, [128, 4, PH], F32)
        pp_ = P7c.pool("pp", 1, [128, 16, NK], F32)
        ep_ = P7c.pool("ep", 1, [128, 16, NK], F32)
        Rp = P7c.pool("R", 1, [128, 128, NK], BF16)
        Rtp = P7c.pool("Rt", 1, [128, 128, NK], BF16)
        OHp = P7c.pool("OH", 1, [128, 128, NK], BF16)
        itp = P7c.pool("it", 2, [128, 128], F32)
        for blk in range(NOWN):
            qp = qpp.next()
            S.dma("sp", qp.t[:], qpT_d[:, :, blk * 128:(blk + 1) * 128].rearrange("c p t -> p c t"), qp, True)
            sc = scp.next()
            for c4 in range(0, 2 * PH, 4):
                ps = PSA.next()
                for j in range(4):
                    S.op("pe", lambda e, ps=ps, qp=qp, c4=c4, j=j: e.matmul(
                        ps.t[:, j * 128:(j + 1) * 128], lhsT=qp.t[:, c4 + j, :], rhs=kk.t[:, c4 + j, :], start=True, stop=True),
                        reads=[qp, kk], writes=[ps])
                S.op("act", lambda e, ps=ps, sc=sc, c4=c4: e.copy(out=sc.t[:, c4:c4 + 4, :], in_=ps.t[:].rearrange("p (c n) -> p c n", n=NK)),
                     reads=[ps], writes=[sc])
            v12 = v12p.next(); idx = idxp.next()
            for c in range(2 * PH):
                tmp = tmpp.next()
                S.op("dve", lambda e, c=c: e.max(out=v12.t[:, c, 0:8], in_=sc.t[:, c, :]), reads=[sc], writes=[v12])
                S.op("dve", lambda e, c=c, tmp=tmp: e.match_replace(out=tmp.t[:, 0:NK], in_to_replace=v12.t[:, c, 0:8], in_values=sc.t[:, c, :],
                                                                    imm_value=-1e30), reads=[sc, v12], writes=[tmp])
                S.op("dve", lambda e, c=c, tmp=tmp: e.max(out=v12.t[:, c, 8:16], in_=tmp.t[:, 0:NK]), reads=[tmp], writes=[v12])
                if c % 2 == 0:
                    S.op("dve", lambda e, c=c: e.max_index(out=idx.t[:, c // 2, 0:8], in_max=v12.t[:, c, 0:8], in_values=sc.t[:, c, :]),
                         reads=[sc, v12], writes=[idx])
                    S.op("dve", lambda e, c=c, tmp=tmp: e.max_index(out=idx.t[:, c // 2, 8:16], in_max=v12.t[:, c, 8:16], in_values=tmp.t[:, 0:NK]),
                         reads=[tmp, v12], writes=[idx])
            cand = candp.next(); tv = tvp.next(); sm = smp.next()
            vv = v12.t[:].rearrange("p (h two) k -> p h two k", two=2)
            S.op("pool", lambda e, cand=cand, vv=vv: e.tensor_tensor(
                out=cand.t[:].rearrange("p h (a b) -> p h a b", a=16),
                in0=vv[:, :, 0, :].unsqueeze(3).to_broadcast([128, PH, 16, 16]),
                in1=vv[:, :, 1, :].unsqueeze(2).to_broadcast([128, PH, 16, 16]), op=ALU.add), reads=[v12], writes=[cand])
            for hh in range(PH):
                tmp = tmpp.next()
                S.op("dve", lambda e, hh=hh: e.max(out=tv.t[:, hh, 0:8], in_=cand.t[:, hh, :]), reads=[cand], writes=[tv])
                S.op("dve", lambda e, hh=hh, tmp=tmp: e.match_replace(out=tmp.t[:], in_to_replace=tv.t[:, hh, 0:8], in_values=cand.t[:, hh, :],
                                                                      imm_value=-1e30), reads=[cand, tv], writes=[tmp])
                S.op("dve", lambda e, hh=hh, tmp=tmp: e.max(out=tv.t[:, hh, 8:16], in_=tmp.t[:]), reads=[tmp], writes=[tv])
            S.op("dve", lambda e, sm=sm, tv=tv: e.tensor_scalar(out=sm.t[:, 0, :], in0=tv.t[:, :, 0], scalar1=-1.0, scalar2=None, op0=ALU.mult),
                 reads=[tv], writes=[sm])
            S.op("dve", lambda e, sm=sm, tv=tv: e.tensor_copy(out=sm.t[:, 3, :], in_=tv.t[:, :, 15]), reads=[tv], writes=[sm])
            ex = tmpp.next()
            S.op("dve", lambda e, sm=sm, tv=tv, ex=ex: e.tensor_tensor(
                out=ex.t[:, 0:PH * 16].rearrange("p (h k) -> p h k", k=16), in0=tv.t[:],
                in1=sm.t[:, 0, :].unsqueeze(2).to_broadcast([128, PH, 16]), op=ALU.add), reads=[tv, sm], writes=[ex])
            S.op("act", lambda e, ex=ex: e.activation(out=ex.t[:, 0:PH * 16], in_=ex.t[:, 0:PH * 16], func=AF.Exp), reads=[ex], writes=[ex])
            S.op("dve", lambda e, sm=sm, ex=ex: e.tensor_reduce(out=sm.t[:, 1, :], in_=ex.t[:, 0:PH * 16].rearrange("p (h k) -> p h k", k=16),
                                                               axis=AX.X, op=ALU.add), reads=[ex], writes=[sm])
            S.op("act", lambda e, sm=sm: e.activation(out=sm.t[:, 2, :], in_=sm.t[:, 1, :], func=AF.Ln), reads=[sm], writes=[sm])
            S.op("dve", lambda e, sm=sm: e.tensor_tensor(out=sm.t[:, 2, :], in0=sm.t[:, 0, :], in1=sm.t[:, 2, :], op=ALU.subtract),
                 reads=[sm], writes=[sm])
            R = Rp.next()
            for hh in range(PH):
                pp = pp_.next(); ep = ep_.next()
                S.op("pool", lambda e, pp=pp, hh=hh: e.tensor_tensor(
                    out=pp.t[:], in0=sc.t[:, 2 * hh + 1, :].unsqueeze(1).to_broadcast([128, 16, NK]),
                    in1=v12.t[:, 2 * hh, :].unsqueeze(2).to_broadcast([128, 16, NK]), op=ALU.add), reads=[sc, v12], writes=[pp])
                S.op("act", lambda e, pp=pp, ep=ep, hh=hh, sm=sm: e.activation(out=ep.t[:], in_=pp.t[:], func=AF.Exp,
                                                                              bias=sm.t[:, 2, hh:hh + 1], scale=1.0),
                     reads=[pp, sm], writes=[ep])
                S.op("dve", lambda e, pp=pp, ep=ep, hh=hh, sm=sm, R=R: e.scalar_tensor_tensor(
                    out=R.t[:, hh * 16:(hh + 1) * 16, :], in0=pp.t[:], scalar=sm.t[:, 3, hh:hh + 1], in1=ep.t[:],
                    op0=ALU.is_ge, op1=ALU.mult), reads=[pp, ep, sm], writes=[R])
            Rt = Rtp.next()
            for i8 in range(0, NK, 8):
                pb = PSB.next()
                for j in range(8):
                    S.op("pe", lambda e, pb=pb, R=R, i8=i8, j=j: e.transpose(pb.t[:, j * 128:(j + 1) * 128], R.t[:, :, i8 + j], ident.t[:]),
                         reads=[R, ident], writes=[pb])
                eng = "act" if (i8 // 8) % 2 == 0 else "dve"
                outv = Rt.t[:, :, i8:i8 + 8].rearrange("p t i -> p i t")
                if eng == "act":
                    S.op("act", lambda e, pb=pb, outv=outv: e.copy(out=outv, in_=pb.t[:].rearrange("p (i t) -> p i t", t=128)), reads=[pb], writes=[Rt])
                else:
                    S.op("dve", lambda e, pb=pb, outv=outv: e.tensor_copy(out=outv, in_=pb.t[:].rearrange("p (i t) -> p i t", t=128)), reads=[pb], writes=[Rt])
            idf = idfp.next()
            S.op("dve", lambda e, idf=idf, idx=idx: e.tensor_copy(out=idf.t[:], in_=idx.t[:].rearrange("p h k -> p (h k)")), reads=[idx], writes=[idf])
            pt_ = PSA.next()
            S.op("pe", lambda e, pt_=pt_, idf=idf: e.transpose(pt_.t[:, 0:128], idf.t[:], identf.t[:]), reads=[idf, identf], writes=[pt_])
            it = itp.next()
            S.op("act", lambda e, it=it, pt_=pt_: e.copy(out=it.t[:], in_=pt_.t[:, 0:128]), reads=[pt_], writes=[it])
            OH = OHp.next()
            S.op("dve", lambda e, OH=OH, it=it: e.tensor_tensor(
                out=OH.t[:], in0=iotar.t[:].unsqueeze(1).to_broadcast([128, 128, NK]),
                in1=it.t[:].unsqueeze(2).to_broadcast([128, 128, NK]), op=ALU.is_equal), reads=[iotar, it], writes=[OH])
            G = R
            for t4 in range(0, 128, 4):
                ps = PSA.next()
                for j in range(4):
                    S.op("pe", lambda e, ps=ps, t4=t4, j=j: e.matmul(ps.t[:, j * 128:(j + 1) * 128], lhsT=Rt.t[:, t4 + j, :], rhs=OH.t[:, t4 + j, :],
                                                                      start=True, stop=True), reads=[Rt, OH], writes=[ps])
                outv = G.t[:, :, t4:t4 + 4].rearrange("p i t -> p t i")
                if (t4 // 4) % 2 == 0:
                    S.op("act", lambda e, ps=ps, outv=outv: e.copy(out=outv, in_=ps.t[:].rearrange("p (t i) -> p t i", i=NK)), reads=[ps], writes=[G])
                else:
                    S.op("dve", lambda e, ps=ps, outv=outv: e.tensor_copy(out=outv, in_=ps.t[:].rearrange("p (t i) -> p t i", i=NK)), reads=[ps], writes=[G])
            for i16 in range(0, NK, 16):
                S.dma("sp", G_d[i16:i16 + 16, :, blk * 128:(blk + 1) * 128].rearrange("i p t -> p i t"), G.t[:, i16:i16 + 16, :], G, False)
        P7c.close()

        TT8 = min(NOWN, 4)
        S.cur_phase = "p8"
        P8 = Phase()
        xT8 = P8.sb("xT8", [128, KC, TT8 * 128], BF16)
        yacc = P8.sb("yacc", [128, TT8, D], F32)
        utp = P8.pool("ut", 2, [128, KC, 256], BF16)
        vtp = P8.pool("vt", 4, [128, D], BF16)
        gp8 = P8.pool("g8", 4, [128, TT8 * 128], BF16)
        gep = P8.pool("ge", 3, [128, TT8 * 128], F32)
        atp = P8.pool("at", 4, [128, TT8 * 128], BF16)
        for b0 in range(0, NOWN, TT8):
            nb = min(TT8, NOWN - b0)
            ntok = nb * 128
            load_xT(xT8, xnT_d, b0, nb)
            for tb in range(nb):
                S.dma("sp", yacc.t[:, tb, :], h1_d[(b0 + tb) * 128:(b0 + tb + 1) * 128, :], yacc, True)
            def p8load(grp):
                ut = utp.next()
                for k4 in range(0, KC, 8):
                    k5 = min(KC, k4 + 8)
                    S.dma("pool", ut.t[:, k4:k5, :], ut_d[grp, :, k4:k5, :], ut, True)
                vts_ = []
                g8s_ = []
                for cl in range(2):
                    c = grp * 2 + cl
                    vt = vtp.next()
                    for d4 in range(0, D, 2048):
                        d5 = min(D, d4 + 2048)
                        S.dma("pool", vt.t[:, d4:d5], pv_d[c * 128:(c + 1) * 128, d4:d5], vt, True)
                    g8 = gp8.next()
                    S.dma("sp", g8.t[:, 0:ntok], G_d[c, :, b0 * 128:b0 * 128 + ntok], g8, True)
                    vts_.append(vt); g8s_.append(g8)
                return ut, vts_, g8s_
            nx8 = p8load(0)
            for grp in range(NE // 256):
                ut, vts, g8s = nx8
                if grp + 1 < NE // 256:
                    nx8 = p8load(grp + 1)
                ats = []
                for cl in range(2):
                    g8 = g8s[cl]
                    ps = PSA.next()
                    for kc in range(KC):
                        S.op("pe", lambda e, ps=ps, ut=ut, kc=kc, cl=cl, ntok=ntok: e.matmul(
                            ps.t[:, 0:ntok], lhsT=ut.t[:, kc, cl * 128:(cl + 1) * 128], rhs=xT8.t[:, kc, 0:ntok],
                            start=(kc == 0), stop=(kc == KC - 1)), reads=[ut, xT8], writes=[ps])
                    ge = gep.next(); at = atp.next()
                    S.op("act", lambda e, ps=ps, ge=ge, ntok=ntok: e.activation(out=ge.t[:, 0:ntok], in_=ps.t[:, 0:ntok], func=AF.Gelu),
                         reads=[ps], writes=[ge])
                    S.op("dve", lambda e, ge=ge, g8=g8, at=at, ntok=ntok: e.tensor_tensor(out=at.t[:, 0:ntok], in0=ge.t[:, 0:ntok], in1=g8.t[:, 0:ntok], op=ALU.mult),
                         reads=[ge, g8], writes=[at])
                    ats.append(at)
                for tb in range(nb):
                    for dt in range(D // 512):
                        ps = PSA.next()
                        for cl in range(2):
                            S.op("pe", lambda e, ps=ps, cl=cl, tb=tb, dt=dt, at=ats[cl], vt=vts[cl]: e.matmul(
                                ps.t[:], lhsT=at.t[:, tb * 128:(tb + 1) * 128], rhs=vt.t[:, dt * 512:(dt + 1) * 512],
                                start=(cl == 0), stop=(cl == 1)), reads=[ats[cl], vts[cl]], writes=[ps])
                        S.op("dve", lambda e, ps=ps, tb=tb, dt=dt: e.tensor_tensor(
                            out=yacc.t[:, tb, dt * 512:(dt + 1) * 512], in0=ps.t[:], in1=yacc.t[:, tb, dt * 512:(dt + 1) * 512], op=ALU.add),
                            reads=[ps, yacc], writes=[yacc])
            for tb in range(nb):
                S.dma("sp", out_d[(b0 + tb) * 128:(b0 + tb + 1) * 128, :], yacc.t[:, tb, :], yacc, False)
        P8.close()
        P0.close()
        S.emit()
    return nc


_CACHE = {}


def kernel(x, meta_tokens, norm_mix_g, w_in, b_forget, q_norm_g, k_norm_g, ret_norm_g, w_proj_fox, w_proj_ret,
           w_out, norm_ffn_g, peer_w_q, peer_keys_1, peer_keys_2, peer_u, peer_v, _dbg=False):
    f = lambda a: np.ascontiguousarray(np.asarray(a, dtype=np.float32))
    x = f(x)
    B, SEQ, D = x.shape
    NB = SEQ // 128
    NOWN = NB // 4
    NCTX = 1 + NB
    T = NCTX * 128
    KC = D // 128
    key = (D, NCTX, NOWN)
    key = (D, NCTX, NOWN, _dbg)
    if key not in _CACHE:
        _CACHE[key] = build(D, NCTX, NOWN, _dbg)
    nc = _CACHE[key]
    meta = f(meta_tokens)
    pu = f(peer_u)[0]
    NE = pu.shape[0]
    ut = np.ascontiguousarray(pu.reshape(NE // 256, 256, KC, 128).transpose(0, 3, 2, 1))
    shared = {
        "norm_mix_g": f(norm_mix_g)[0], "w_in": f(w_in)[0], "b_forget": f(b_forget)[0], "q_norm_g": f(q_norm_g)[0],
        "k_norm_g": f(k_norm_g)[0], "ret_norm_g": f(ret_norm_g)[0], "w_proj_fox": f(w_proj_fox)[0],
        "w_proj_ret": f(w_proj_ret)[0], "w_out": f(w_out)[0], "norm_ffn_g": f(norm_ffn_g)[0],
        "peer_w_q": f(peer_w_q)[0],
        "k1t": np.ascontiguousarray(f(peer_keys_1)[0].transpose(0, 2, 1)),
        "k2t": np.ascontiguousarray(f(peer_keys_2)[0].transpose(0, 2, 1)),
        "ut": ut, "peer_v": f(peer_v)[0],
    }
    gam = 1.0 - 2.0 ** (-5.0 - np.arange(H, dtype=np.float64))
    inv = ROPE_BASE ** (-np.arange(64, dtype=np.float64) / 64)
    in_maps = []
    for c in range(8):
        b, g = c // 4, c % 4
        ndum = (3 - g) * NOWN * 128
        nprev = g * NOWN * 128
        ctx = np.zeros((T, D), np.float32)
        ctx[ndum + PAD:ndum + 128] = meta
        ctx[ndum + 128:ndum + 128 + nprev] = x[b, :nprev]
        ctx[T - NOWN * 128:] = x[b, nprev:nprev + NOWN * 128]
        n = np.arange(T) - ndum
        valid = (n >= PAD).astype(np.float32)
        pos = (n - PAD).astype(np.float64)
        ang = pos[:, None] * inv[None, :]
        cossin = np.concatenate([np.cos(ang), np.sin(ang)], axis=1).astype(np.float32)
        start = 128 + nprev
        lpos = (n - start).astype(np.float64)
        own = n >= start
        ktab = np.zeros((T, H), np.float64)
        with np.errstate(over="ignore", under="ignore"):
            ktab[~own] = np.exp(np.log(gam)[None, :] * (start - 1 - n[~own])[:, None])
            ktab[own] = np.exp(-np.log(gam)[None, :] * lpos[own][:, None])
            qtab = np.exp(np.log(gam)[None, :] * lpos[own][:, None])
        ktab = ktab * (128.0 ** -0.5) * valid[:, None]
        m = dict(shared)
        m.update({"ctx_x": ctx, "valid": np.ascontiguousarray(valid.reshape(NCTX, 128).T), "cossin": cossin, "ktab": ktab.astype(np.float32),
                  "qtab": qtab.astype(np.float32)})
        in_maps.append(m)
    res = run_bass_kernel_spmd(nc, in_maps, core_ids=list(range(8)))
    if _dbg:
        return res.results, in_maps
    out = np.zeros((B, SEQ, D), np.float32)
    for c in range(8):
        b, g = c // 4, c % 4
        out[b, g * NOWN * 128:(g + 1) * NOWN * 128] = res.results[c]["out"]
    return out
```

```python
import numpy as np
from contextlib import ExitStack
import concourse.bass as bass
import concourse.mybir as mybir
from concourse.bass_utils import run_bass_kernel_spmd

F32 = mybir.dt.float32
BF16 = mybir.dt.bfloat16
U32 = mybir.dt.uint32
AF = mybir.ActivationFunctionType
ALU = mybir.AluOpType
AX = mybir.AxisListType

N_META = 16
PAD = 112
EPS = 1e-6
H = 16
HD = 128
DV = 256
PH = 8
NK = 128
TOPK = 16
ROPE_BASE = 10000.0


class Res:
    __slots__ = ("w", "r", "name")

    def __init__(self, name=""):
        self.w = None
        self.r = {}
        self.name = name


class DSem:
    __slots__ = ("sem", "tot")


class Ins:
    __slots__ = ("eng", "fn", "waits", "sig", "sigval", "dsem", "dval", "idx", "ph")


class Tile:
    __slots__ = ("t", "res", "ds", "_subs")

    def __init__(self, t, name):
        self.t = t
        self.res = Res(name)
        self.ds = None
        self._subs = None

    def subs(self, n):
        if self._subs is None:
            self._subs = [Tile(self.t, "%s.%d" % (self.res.name, i)) for i in range(n)]
        return self._subs


class Pool:
    def __init__(self, tiles):
        self.tiles = tiles
        self.i = 0

    def next(self):
        t = self.tiles[self.i % len(self.tiles)]
        self.i += 1
        return t


class _Rec:
    def __getattr__(self, name):
        def f(*a, **k):
            self.call = (name, a, k)
            return self
        return f


class Sched:
    ENGS = ("pe", "act", "dve", "pool", "sp")

    def __init__(self, nc, stack):
        self.nc = nc
        self.stack = stack
        self.lists = {e: [] for e in self.ENGS}
        self.esem = {e: stack.enter_context(nc.semaphore("es_" + e)) for e in self.ENGS}
        self.dsems = []
        self.free_ds = []
        self.lastc = {e: None for e in self.ENGS}
        self.cur_phase = "p0"
        self.scopes = False

    def get_ds(self):
        if self.free_ds:
            return self.free_ds.pop()
        d = DSem()
        d.sem = self.stack.enter_context(self.nc.semaphore("ds%d" % len(self.dsems)))
        d.tot = 0
        self.dsems.append(d)
        return d

    def op(self, eng, fn, reads=(), writes=(), dsem=None):
        rec = _Rec()
        fn(rec)
        name, a, k = rec.call
        ins = Ins()
        ins.eng = eng
        ins.fn = lambda e: getattr(e, name)(*a, **k)
        ins.sig = False
        ins.sigval = 0
        ins.dsem = dsem
        ins.idx = len(self.lists[eng])
        ins.ph = self.cur_phase
        deps = {}

        def add(d):
            if d is None:
                return
            if d.dsem is not None:
                deps[("d", id(d.dsem))] = d
            else:
                if d.eng == eng and eng == "pe" and dsem is None:
                    return
                k = ("c", d.eng)
                if k not in deps or deps[k].idx < d.idx:
                    deps[k] = d
        for t in reads:
            add(t.res.w)
        for t in writes:
            add(t.res.w)
            for d in t.res.r.values():
                add(d)
        waits = []
        for d in deps.values():
            if d.dsem is not None:
                waits.append((d.dsem, d.dsem.tot))
            else:
                d.sig = True
                waits.append(d)
        ins.waits = waits
        if dsem is not None:
            dsem.tot += 16
            ins.dval = dsem.tot
        else:
            self.lastc[eng] = ins
        key = ("d", id(dsem)) if dsem is not None else ("c", eng)
        for t in reads:
            t.res.r[key] = ins
        for t in writes:
            t.res.w = ins
            t.res.r = {}
        self.lists[eng].append(ins)
        return ins

    def dma(self, q, out, in_, tile, load):
        if tile.ds is None:
            tile.ds = self.get_ds()
        fn = lambda e: e.dma_start(out=out, in_=in_)
        if load:
            return self.op(q, fn, writes=[tile], dsem=tile.ds)
        return self.op(q, fn, reads=[tile], dsem=tile.ds)

    def barrier(self, tiles=()):
        last = [self.lastc[e] for e in self.ENGS if self.lastc[e] is not None]
        dtot = [(d, d.tot) for d in self.dsems if d.tot > 0]
        for d in last:
            d.sig = True
        for e in self.ENGS:
            ins = self.op(e, lambda eng: eng.nop())
            ins.waits = list(last) + list(dtot)
        for t in tiles:
            if t.ds is not None:
                self.free_ds.append(t.ds)
                t.ds = None

    def emit(self):
        nc = self.nc
        for e in self.ENGS:
            c = 0
            for ins in self.lists[e]:
                if ins.dsem is None and ins.sig:
                    c += 1
                    ins.sigval = c
        with nc.Block() as block:
            def run(e):
                def body(eng):
                    waited = {}
                    cur = None
                    for ins in self.lists[e]:
                        if self.scopes and ins.ph != cur:
                            if cur is not None:
                                scope.__exit__(None, None, None)
                            scope = nc.named_scope(ins.ph)
                            scope.__enter__()
                            cur = ins.ph
                        for w in ins.waits:
                            if isinstance(w, tuple):
                                sem, val = w[0].sem, w[1]
                            else:
                                sem, val = self.esem[w.eng], w.sigval
                            k = id(sem)
                            if waited.get(k, 0) >= val:
                                continue
                            waited[k] = val
                            eng.wait_ge(sem, val)
                        bi = ins.fn(eng)
                        if ins.dsem is not None:
                            bi.then_inc(ins.dsem.sem, 16)
                        elif ins.sig:
                            bi.then_inc(self.esem[e], 1)
                    if e == "sp":
                        for d in self.dsems:
                            if d.tot > 0:
                                eng.wait_ge(d.sem, d.tot)
                    if cur is not None:
                        scope.__exit__(None, None, None)
                return body
            block.tensor(run("pe"))
            block.scalar(run("act"))
            block.vector(run("dve"))
            block.gpsimd(run("pool"))
            block.sync(run("sp"))


def build(D, NCTX, NOWN, dbg=False, scopes=False):
    KC = D // 128
    T = NCTX * 128
    TO = NOWN * 128
    OWN0 = NCTX - NOWN
    NE = NK * NK
    nc = bass.Bass("TRN2", target_bir_lowering=False)

    def din(name, shape, dt=F32):
        return nc.dram_tensor(name, shape, dt, kind="ExternalInput").ap()

    def dscr(name, shape, dt=BF16):
        return nc.dram_tensor(name, shape, dt, kind="ExternalOutput" if dbg else "Internal").ap()

    ctx_x = din("ctx_x", [T, D])
    valid_d = din("valid", [128, NCTX])
    cs_d = din("cossin", [T, 128])
    ktab_d = din("ktab", [T, H])
    qtab_d = din("qtab", [TO, H])
    gmix_d = din("norm_mix_g", [D])
    w_in = din("w_in", [D, 26640 - 8192 + 2 * D])
    bf_d = din("b_forget", [H])
    qg_d = din("q_norm_g", [HD])
    kg_d = din("k_norm_g", [HD])
    rg_d = din("ret_norm_g", [H * DV])
    wpf = din("w_proj_fox", [H * HD, D])
    wpr = din("w_proj_ret", [H * DV, D])
    wout = din("w_out", [D, D])
    gffn_d = din("norm_ffn_g", [D])
    wq = din("peer_w_q", [D, PH * 256])
    k1t = din("k1t", [PH, 128, NK])
    k2t = din("k2t", [PH, 128, NK])
    ut_d = din("ut", [NE // 256, 128, KC, 256])
    pv_d = din("peer_v", [NE, D])
    out_d = nc.dram_tensor("out", [TO, D], F32, kind="ExternalOutput").ap()

    uT_d = dscr("uT", [128, KC, T])
    kT_d = dscr("kT", [H, 128, T])
    vf_d = dscr("vf", [T, H, 129])
    krw_d = dscr("krw", [T, H, 128])
    krT_d = dscr("krT", [H, 128, TO])
    vr_d = dscr("vr", [T, H * DV])
    qT_d = dscr("qT", [H, 128, TO])
    qrT_d = dscr("qrT", [H, 128, TO])
    sg_d = dscr("sg", [TO, H * DV])
    ga_d = dscr("ga", [TO, D])
    gr_d = dscr("gr", [TO, D])
    ya_d = dscr("ya", [TO, H * HD])
    yr_d = dscr("yr", [TO, H * DV])
    h1_d = dscr("h1", [TO, D], F32)
    xnT_d = dscr("xnT", [128, KC, TO])
    qpT_d = dscr("qpT", [2 * PH, 128, TO])
    G_d = dscr("G", [NK, 128, TO])
    oraw_d = dscr("oraw", [H, TO, DV], F32) if dbg else None
    stt_d = dscr("sttd", [H, 128, DV], BF16) if dbg else None

    c_qa, c_ka, c_va, c_fa = 0, 2048, 4096, 6144
    c_qr = 6160
    c_kr = c_qr + 2048
    c_vr = c_kr + 2048
    c_gr = c_vr + 4096
    c_ga = c_gr + 4096
    c_gtr = c_ga + D

    with ExitStack() as st:
        S = Sched(nc, st)
        S.scopes = scopes
        psf = [Tile(st.enter_context(nc.psum_tensor("psf%d" % i, [128, 512], F32)), "psf%d" % i) for i in range(6)]
        psb = [Tile(st.enter_context(nc.psum_tensor("psb%d" % i, [128, 1024], BF16)), "psb%d" % i) for i in range(2)]
        PSA = Pool(psf[0:4])
        PSACC = Pool(psf[4:6])
        PSB = Pool(psb)

        cnt = [0]

        class Phase:
            def __init__(self):
                self.st = ExitStack()
                self.tiles = []

            def sb(self, name, shape, dt):
                cnt[0] += 1
                name = "s%d_%s" % (cnt[0], name)
                t = Tile(self.st.enter_context(nc.sbuf_tensor(name, shape, dt)), name)
                self.tiles.append(t)
                return t

            def pool(self, name, n, shape, dt):
                return Pool([self.sb("%s%d" % (name, i), shape, dt) for i in range(n)])

            def close(self):
                S.barrier(self.tiles + psf + psb)
                self.st.close()

        P0 = Phase()
        ident = P0.sb("ident", [128, 128], BF16)
        identf = P0.sb("identf", [128, 128], F32)
        tri = P0.sb("tri", [128, 128], BF16)
        trif = P0.sb("trif", [128, 128], F32)
        onesf = P0.sb("onesf", [128, 128], F32)
        iotar = P0.sb("iotar", [128, 128], F32)
        ctmp = P0.sb("ctmp", [128, 128], F32)
        mhalf = P0.sb("mhalf", [128, 8], F32)
        S.op("pool", lambda e: e.memset(mhalf.t[:], -0.5), writes=[mhalf])
        PA = Phase()
        valid = PA.sb("valid", [128, NCTX], F32)
        ktab = PA.sb("ktab", [128, NCTX, H], F32)
        qtab = PA.sb("qtab", [128, NOWN, H], F32)
        Lfull = PA.sb("Lfull", [128, NCTX, H], F32)
        Lb = PA.sb("Lb", [128, NCTX + 1, H], F32)
        S.op("pool", lambda e: e.iota(ctmp.t[:], pattern=[[1, 128]], base=0, channel_multiplier=-1,
                                      allow_small_or_imprecise_dtypes=True), writes=[ctmp])
        S.op("dve", lambda e: e.tensor_single_scalar(out=identf.t[:], in_=ctmp.t[:], scalar=0.0, op=ALU.is_equal),
             reads=[ctmp], writes=[identf])
        S.op("dve", lambda e: e.tensor_copy(out=ident.t[:], in_=identf.t[:]), reads=[identf], writes=[ident])
        S.op("dve", lambda e: e.tensor_single_scalar(out=trif.t[:], in_=ctmp.t[:], scalar=0.0, op=ALU.is_ge),
             reads=[ctmp], writes=[trif])
        S.op("dve", lambda e: e.tensor_copy(out=tri.t[:], in_=trif.t[:]), reads=[trif], writes=[tri])
        S.op("pool", lambda e: e.memset(onesf.t[:], 1.0), writes=[onesf])
        S.op("pool", lambda e: e.iota(iotar.t[:], pattern=[[1, 128]], base=0, channel_multiplier=0,
                                      allow_small_or_imprecise_dtypes=True), writes=[iotar])
        S.op("pool", lambda e: e.memset(Lb.t[:, 0, :], 0.0), writes=[Lb])
        S.dma("sp", valid.t[:], valid_d, valid, True)
        S.dma("sp", ktab.t[:], ktab_d.rearrange("(b p) h -> p b h", p=128), ktab, True)
        S.dma("sp", qtab.t[:], qtab_d.rearrange("(b p) h -> p b h", p=128), qtab, True)

        def bcast_load(ph, name, src, n, scale=None):
            t = ph.sb(name, [128, n], F32)
            S.dma("sp", t.t[:], src.partition_broadcast(128), t, True)
            if scale is not None:
                S.op("dve", lambda e: e.tensor_scalar(out=t.t[:], in0=t.t[:], scalar1=float(scale), scalar2=None,
                                                      op0=ALU.mult), reads=[t], writes=[t])
            return t

        def rmsnorm_T(ph, nblk, src_rows, g_bc, dstT, xpool, jpool, npool, opool):
            ss = ph.pool("ss", 2, [128, 2], F32)
            xq = []

            def xload(b):
                xt_ = xpool.next()
                S.dma("sp", xt_.t[:], src_rows(b), xt_, True)
                xq.append(xt_)
            for b in range(min(2, nblk)):
                xload(b)
            for blk in range(nblk):
                xt = xq[blk]
                if blk + 2 < nblk:
                    xload(blk + 2)
                jk = jpool.next()
                s1 = ss.next()
                S.op("pool", lambda e, s1=s1: e.memset(s1.t[:], 0.0), writes=[s1])
                S.op("act", lambda e, xt=xt, jk=jk, s1=s1: e.activation(out=jk.t[:], in_=xt.t[:], func=AF.Square,
                                                                         accum_out=s1.t[:, 0:1]),
                     reads=[xt], writes=[jk, s1])
                S.op("dve", lambda e, s1=s1: e.tensor_scalar(out=s1.t[:, 1:2], in0=s1.t[:, 0:1], scalar1=1.0 / D,
                                                              scalar2=EPS, op0=ALU.mult, op1=ALU.add),
                     reads=[s1], writes=[s1])
                S.op("pool", lambda e, s1=s1: e.tensor_tensor(out=s1.t[:, 1:2], in0=s1.t[:, 1:2], in1=mhalf.t[:, 0:1], op=ALU.pow),
                     reads=[s1, mhalf], writes=[s1])
                xn = npool.next()
                S.op("dve", lambda e, xt=xt, xn=xn, s1=s1: e.scalar_tensor_tensor(
                    out=xn.t[:], in0=xt.t[:], scalar=s1.t[:, 1:2], in1=g_bc.t[:], op0=ALU.mult, op1=ALU.mult),
                    reads=[xt, s1, g_bc], writes=[xn])
                ot = opool.next()
                for k8 in range(0, KC, 8):
                    pb = PSB.next()
                    n8 = min(8, KC - k8)
                    for j in range(n8):
                        S.op("pe", lambda e, pb=pb, xn=xn, j=j, k8=k8: e.transpose(
                            pb.t[:, j * 128:(j + 1) * 128], xn.t[:, (k8 + j) * 128:(k8 + j + 1) * 128], ident.t[:]),
                            reads=[xn, ident], writes=[pb])
                    eng = "act" if (k8 // 8) % 2 == 0 else "dve"
                    if eng == "act":
                        S.op("act", lambda e, pb=pb, ot=ot, k8=k8, n8=n8: e.copy(
                            out=ot.t[:, k8:k8 + n8, :], in_=pb.t[:, 0:n8 * 128].rearrange("p (k t) -> p k t", t=128)),
                            reads=[pb], writes=[ot])
                    else:
                        S.op("dve", lambda e, pb=pb, ot=ot, k8=k8, n8=n8: e.tensor_copy(
                            out=ot.t[:, k8:k8 + n8, :], in_=pb.t[:, 0:n8 * 128].rearrange("p (k t) -> p k t", t=128)),
                            reads=[pb], writes=[ot])
                S.dma("sp", dstT[:, :, blk * 128:(blk + 1) * 128], ot.t[:], ot, False)

        S.cur_phase = "p1"
        P1 = Phase()
        g_bc = bcast_load(P1, "gmix", gmix_d, D)
        rmsnorm_T(P1, NCTX, lambda blk: ctx_x[blk * 128:(blk + 1) * 128, :], g_bc, uT_d,
                  P1.pool("x", 3, [128, D], F32), P1.pool("jk", 1, [128, D], BF16),
                  P1.pool("xn", 2, [128, D], BF16), P1.pool("uo", 2, [128, KC, 128], BF16))
        P1.close()

        def gemm_multi(xT, nblk, kcs, wpool, jobs):
            nkc = len(kcs)
            tiles = []
            for (W, N, evac) in jobs:
                for nt in range((N + 511) // 512):
                    tiles.append((W, nt, min(512, N - nt * 512), evac))

            def load(ti):
                W, nt, nw, _ = tiles[ti]
                wt = wpool.next()
                n0 = nt * 512
                for k4 in range(0, nkc, 8):
                    k5 = min(nkc, k4 + 8)
                    S.dma("pool", wt.t[:, k4:k5, 0:nw],
                          W[k4 * 128:k5 * 128, n0:n0 + nw].rearrange("(kc p) n -> p kc n", p=128), wt, True)
                return wt
            nxt = load(0)
            for ti, (W, nt, nw, evac) in enumerate(tiles):
                wt = nxt
                if ti + 1 < len(tiles):
                    nxt = load(ti + 1)
                for tb in range(nblk):
                    ps = PSA.next()
                    for i, kc in enumerate(kcs):
                        S.op("pe", lambda e, ps=ps, wt=wt, i=i, kc=kc, tb=tb, nw=nw: e.matmul(
                            ps.t[:, 0:nw], lhsT=xT.t[:, kc, tb * 128:(tb + 1) * 128], rhs=wt.t[:, i, 0:nw],
                            start=(i == 0), stop=(i == nkc - 1)), reads=[xT, wt], writes=[ps])
                    evac(tb, nt, ps, nw)

        def gemm(xT, nblk, kcs, W, N, wpool, evac, pre=None):
            gemm_multi(xT, nblk, kcs, wpool, [(W, N, evac)])

        def load_xT(xT, srcT, b0, nb):
            for k4 in range(0, KC, 8):
                k5 = min(KC, k4 + 8)
                S.dma("sp", xT.t[:, k4:k5, 0:nb * 128], srcT[:, k4:k5, b0 * 128:(b0 + nb) * 128], xT, True)

        def qknorm_T(ph, ps, g_t, dst, h0, col0, ssp, knp, ktp):
            s4 = ssp.next()
            kn = knp.next()
            S.op("pool", lambda e: e.memset(s4.t[:], 0.0), writes=[s4])
            for h in range(4):
                S.op("act", lambda e, h=h: e.activation(out=kn.t[:, h * 128:(h + 1) * 128], in_=ps.t[:, h * 128:(h + 1) * 128],
                                                         func=AF.Square, accum_out=s4.t[:, h:h + 1]),
                     reads=[ps], writes=[kn, s4])
            S.op("dve", lambda e: e.tensor_scalar(out=s4.t[:, 4:8], in0=s4.t[:, 0:4], scalar1=1.0 / HD, scalar2=EPS,
                                                  op0=ALU.mult, op1=ALU.add), reads=[s4], writes=[s4])
            S.op("pool", lambda e: e.tensor_tensor(out=s4.t[:, 4:8], in0=s4.t[:, 4:8], in1=mhalf.t[:, 0:4], op=ALU.pow),
                 reads=[s4, mhalf], writes=[s4])
            for h in range(4):
                S.op("dve", lambda e, h=h: e.scalar_tensor_tensor(
                    out=kn.t[:, h * 128:(h + 1) * 128], in0=ps.t[:, h * 128:(h + 1) * 128], scalar=s4.t[:, 4 + h:5 + h],
                    in1=g_t.t[:], op0=ALU.mult, op1=ALU.mult), reads=[ps, s4, g_t], writes=[kn])
            transpose4(kn, dst, h0, col0, ktp)

        def transpose4(kn, dst, h0, col0, ktp):
            pb = PSB.next()
            for h in range(4):
                S.op("pe", lambda e, h=h: e.transpose(pb.t[:, h * 128:(h + 1) * 128], kn.t[:, h * 128:(h + 1) * 128],
                                                      ident.t[:]), reads=[kn, ident], writes=[pb])
            kt = ktp.next()
            S.op("act", lambda e: e.copy(out=kt.t[:], in_=pb.t[:, 0:512]), reads=[pb], writes=[kt])
            S.dma("sp", dst[h0:h0 + 4, :, col0:col0 + 128].rearrange("h p t -> p h t"),
                  kt.t[:].rearrange("p (h t) -> p h t", t=128), kt, False)

        def rotary(ph, ps, cs, tabcol, ro_p, kn_p):
            ro = ro_p.next()
            pv = ps.t[:, 0:512].rearrange("p (h two f) -> p h two f", h=4, two=2)
            rv = ro.t[:].rearrange("p (a h f) -> p a h f", a=4, h=4)
            cosb = cs.t[:, 0:64].unsqueeze(1).to_broadcast([128, 4, 64])
            sinb = cs.t[:, 64:128].unsqueeze(1).to_broadcast([128, 4, 64])
            S.op("dve", lambda e: e.tensor_tensor(out=rv[:, 0], in0=pv[:, :, 0, :], in1=cosb, op=ALU.mult), reads=[ps, cs], writes=[ro])
            S.op("dve", lambda e: e.tensor_tensor(out=rv[:, 1], in0=pv[:, :, 1, :], in1=sinb, op=ALU.mult), reads=[ps, cs], writes=[ro])
            S.op("dve", lambda e: e.tensor_tensor(out=rv[:, 2], in0=pv[:, :, 0, :], in1=sinb, op=ALU.mult), reads=[ps, cs], writes=[ro])
            S.op("dve", lambda e: e.tensor_tensor(out=rv[:, 3], in0=pv[:, :, 1, :], in1=cosb, op=ALU.mult), reads=[ps, cs], writes=[ro])
            S.op("pool", lambda e: e.tensor_tensor(out=rv[:, 0], in0=rv[:, 0], in1=rv[:, 1], op=ALU.subtract), reads=[ro], writes=[ro])
            S.op("pool", lambda e: e.tensor_tensor(out=rv[:, 2], in0=rv[:, 2], in1=rv[:, 3], op=ALU.add), reads=[ro], writes=[ro])
            kn = kn_p.next()
            kv = kn.t[:].rearrange("p (h two f) -> p h two f", h=4, two=2)
            tb_ = tabcol.unsqueeze(2).to_broadcast([128, 4, 64])
            S.op("dve", lambda e: e.tensor_tensor(out=kv[:, :, 0, :], in0=rv[:, 0], in1=tb_, op=ALU.mult), reads=[ro, ktab, qtab], writes=[kn])
            S.op("dve", lambda e: e.tensor_tensor(out=kv[:, :, 1, :], in0=rv[:, 2], in1=tb_, op=ALU.mult), reads=[ro, ktab, qtab], writes=[kn])
            return kn

        TT2 = min(NCTX, 11)
        S.cur_phase = "p2"
        P2 = Phase()
        xT2 = P2.sb("xT2", [128, KC, TT2 * 128], BF16)
        wp2 = P2.pool("w", 2, [128, KC, 512], BF16)
        kg_bc = bcast_load(P2, "kg", kg_d, HD)
        bf_bc = bcast_load(P2, "bfb", bf_d, H)
        ss4 = P2.pool("s4", 3, [128, 8], F32)
        knp = P2.pool("kn", 3, [128, 512], BF16)
        ktp = P2.pool("kt", 3, [128, 512], BF16)
        vsp = P2.pool("vs", 3, [128, 4, 129], BF16)
        rop = P2.pool("ro", 2, [128, 1024], F32)
        csp = P2.pool("cs", 3, [128, 128], F32)
        fz = P2.pool("fz", 2, [128, 48], F32)
        for b0 in range(0, NCTX, TT2):
            nb = min(TT2, NCTX - b0)
            load_xT(xT2, uT_d, b0, nb)

            def ev_ka(tb, nt, ps, nw, b0=b0):
                qknorm_T(P2, ps, kg_bc, kT_d, nt * 4, (b0 + tb) * 128, ss4, knp, ktp)

            def ev_va(tb, nt, ps, nw, b0=b0):
                blk = b0 + tb
                vs = vsp.next()
                S.op("dve", lambda e: e.tensor_scalar(out=vs.t[:, :, 0:128], in0=ps.t[:, 0:512].rearrange("p (h f) -> p h f", h=4),
                                                      scalar1=valid.t[:, blk:blk + 1], scalar2=None, op0=ALU.mult),
                     reads=[ps, valid], writes=[vs])
                S.op("pool", lambda e: e.tensor_copy(out=vs.t[:, :, 128:129],
                                                     in_=valid.t[:, blk:blk + 1].unsqueeze(1).to_broadcast([128, 4, 1])),
                     reads=[valid], writes=[vs])
                S.dma("sp", vf_d[blk * 128:(blk + 1) * 128, nt * 4:nt * 4 + 4, :], vs.t[:], vs, False)

            def ev_fa(tb, nt, ps, nw, b0=b0):
                blk = b0 + tb
                z = fz.next()
                S.op("dve", lambda e: e.tensor_tensor(out=z.t[:, 0:16], in0=ps.t[:, 0:16], in1=bf_bc.t[:], op=ALU.add),
                     reads=[ps, bf_bc], writes=[z])
                S.op("act", lambda e: e.activation(out=z.t[:, 16:32], in_=z.t[:, 0:16], func=AF.Exp, scale=-1.0), reads=[z], writes=[z])
                S.op("act", lambda e: e.activation(out=z.t[:, 32:48], in_=z.t[:, 16:32], func=AF.Ln, bias=1.0), reads=[z], writes=[z])
                p2 = PSA.next()
                S.op("pe", lambda e: e.matmul(p2.t[:, 0:16], lhsT=trif.t[:], rhs=z.t[:, 32:48], start=True, stop=True),
                     reads=[trif, z], writes=[p2])
                S.op("pe", lambda e: e.matmul(p2.t[:, 16:32], lhsT=onesf.t[:], rhs=z.t[:, 32:48], start=True, stop=True),
                     reads=[onesf, z], writes=[p2])
                S.op("dve", lambda e: e.tensor_tensor(out=Lfull.t[:, blk, :], in0=p2.t[:, 0:16], in1=Lb.t[:, blk, :], op=ALU.add),
                     reads=[p2, Lb], writes=[Lfull])
                S.op("dve", lambda e: e.tensor_tensor(out=Lb.t[:, blk + 1, :], in0=p2.t[:, 16:32], in1=Lb.t[:, blk, :], op=ALU.add),
                     reads=[p2, Lb], writes=[Lb])

            def ev_kr(tb, nt, ps, nw, b0=b0):
                blk = b0 + tb
                cs = csp.next()
                S.dma("sp", cs.t[:], cs_d[blk * 128:(blk + 1) * 128, :], cs, True)
                kn = rotary(P2, ps, cs, ktab.t[:, blk, nt * 4:nt * 4 + 4], rop, knp)
                if blk < OWN0:
                    S.dma("sp", krw_d[blk * 128:(blk + 1) * 128, nt * 4:nt * 4 + 4, :],
                          kn.t[:].rearrange("p (h f) -> p h f", h=4), kn, False)
                else:
                    transpose4(kn, krT_d, nt * 4, (blk - OWN0) * 128, ktp)

            def ev_vr(tb, nt, ps, nw, b0=b0):
                blk = b0 + tb
                vs = knp.next()
                S.op("act", lambda e: e.activation(out=vs.t[:], in_=ps.t[:, 0:512], func=AF.Copy,
                                                   scale=valid.t[:, blk:blk + 1]), reads=[ps, valid], writes=[vs])
                S.dma("sp", vr_d[blk * 128:(blk + 1) * 128, nt * 512:(nt + 1) * 512], vs.t[:], vs, False)

            kcs = list(range(KC))
            gemm_multi(xT2, nb, kcs, wp2, [
                (w_in[:, c_fa:c_fa + 16], 16, ev_fa),
                (w_in[:, c_ka:c_ka + 2048], 2048, ev_ka),
                (w_in[:, c_va:c_va + 2048], 2048, ev_va),
                (w_in[:, c_kr:c_kr + 2048], 2048, ev_kr),
                (w_in[:, c_vr:c_vr + 4096], 4096, ev_vr)])
        P2.close()

        TT3 = min(NOWN, 8)
        S.cur_phase = "p3"
        P3 = Phase()
        xT3 = P3.sb("xT3", [128, KC, TT3 * 128], BF16)
        wp3 = P3.pool("w", 2, [128, KC, 512], BF16)
        qg_bc = bcast_load(P3, "qg", qg_d, HD, scale=HD ** -0.5)
        ss4 = P3.pool("s4", 3, [128, 8], F32)
        knp = P3.pool("kn", 3, [128, 512], BF16)
        ktp = P3.pool("kt", 3, [128, 512], BF16)
        rop = P3.pool("ro", 2, [128, 1024], F32)
        csp = P3.pool("cs", 3, [128, 128], F32)
        for b0 in range(0, NOWN, TT3):
            nb = min(TT3, NOWN - b0)
            load_xT(xT3, uT_d, OWN0 + b0, nb)

            def ev_qa(tb, nt, ps, nw, b0=b0):
                qknorm_T(P3, ps, qg_bc, qT_d, nt * 4, (b0 + tb) * 128, ss4, knp, ktp)

            def ev_qr(tb, nt, ps, nw, b0=b0):
                blk = b0 + tb
                cs = csp.next()
                S.dma("sp", cs.t[:], cs_d[(OWN0 + blk) * 128:(OWN0 + blk + 1) * 128, :], cs, True)
                kn = rotary(P3, ps, cs, qtab.t[:, blk, nt * 4:nt * 4 + 4], rop, knp)
                transpose4(kn, qrT_d, nt * 4, blk * 128, ktp)

            def ev_act(dst, func):
                def ev(tb, nt, ps, nw, b0=b0):
                    blk = b0 + tb
                    vs = knp.next()
                    S.op("act", lambda e: e.activation(out=vs.t[:], in_=ps.t[:, 0:512], func=func), reads=[ps], writes=[vs])
                    S.dma("sp", dst[blk * 128:(blk + 1) * 128, nt * 512:(nt + 1) * 512], vs.t[:], vs, False)
                return ev
            kcs = list(range(KC))
            gemm_multi(xT3, nb, kcs, wp3, [
                (w_in[:, c_qa:c_qa + 2048], 2048, ev_qa),
                (w_in[:, c_qr:c_qr + 2048], 2048, ev_qr),
                (w_in[:, c_gr:c_gr + 4096], 4096, ev_act(sg_d, AF.Silu)),
                (w_in[:, c_ga:c_ga + D], D, ev_act(ga_d, AF.Sigmoid)),
                (w_in[:, c_gtr:c_gtr + D], D, ev_act(gr_d, AF.Sigmoid))])
        P3.close()

        S.cur_phase = "p4"
        P4 = Phase()
        qTp = P4.pool("qT", 2, [128, TO], BF16)
        kTp = P4.pool("kT", 2, [128, T], BF16)
        vp = P4.pool("v", 2, [128, NCTX, 129], BF16)
        bip = P4.pool("bias", 2, [128, NOWN, NCTX], F32)
        ptp = P4.pool("pt", 6, [128, 128], BF16)
        yap = P4.pool("yas", 2, [128, NOWN, 128], BF16)
        rcp = P4.pool("rc", 2, [128, 1], F32)
        def p4load(h):
            qT = qTp.next(); kT = kTp.next(); vv = vp.next()
            S.dma("sp", qT.t[:], qT_d[h], qT, True)
            S.dma("sp", kT.t[:], kT_d[h], kT, True)
            S.dma("sp", vv.t[:], vf_d[:, h, :].rearrange("(b p) f -> p b f", p=128), vv, True)
            return qT, kT, vv
        nx4 = p4load(0)
        for h in range(H):
            qT, kT, vv = nx4
            if h + 1 < H:
                nx4 = p4load(h + 1)
            yas = yap.next()
            biasT = bip.next()
            for i in range(NOWN):
                gi = OWN0 + i
                S.op("dve", lambda e, i=i, gi=gi: e.tensor_scalar(
                    out=biasT.t[:, i, 0:gi + 1], in0=Lfull.t[:, 0:gi + 1, h], scalar1=Lb.t[:, gi, h:h + 1], scalar2=None,
                    op0=ALU.subtract), reads=[Lfull, Lb], writes=[biasT])
            pairs = [(i, j) for i in range(NOWN) for j in range(OWN0 + i + 1)]
            pss = {}

            def qk(p):
                i, j = pairs[p]
                ps = PSA.next()
                pss[p] = ps
                S.op("pe", lambda e: e.matmul(ps.t[:, 0:128], lhsT=kT.t[:, j * 128:(j + 1) * 128], rhs=qT.t[:, i * 128:(i + 1) * 128],
                                              start=True, stop=True), reads=[kT, qT], writes=[ps])
            LA = 3
            for p in range(min(LA, len(pairs))):
                qk(p)
            acc = None
            for p, (i, j) in enumerate(pairs):
                if p + LA < len(pairs):
                    qk(p + LA)
                gi = OWN0 + i
                if j == 0:
                    acc = PSACC.next()
                ps = pss.pop(p)
                pt = ptp.next()
                S.op("act", lambda e: e.activation(out=pt.t[:], in_=ps.t[:, 0:128], func=AF.Exp, bias=biasT.t[:, i, j:j + 1], scale=1.0),
                     reads=[ps, biasT], writes=[pt])
                if j == gi:
                    S.op("pool", lambda e: e.tensor_tensor(out=pt.t[:], in0=pt.t[:], in1=tri.t[:], op=ALU.mult),
                         reads=[pt, tri], writes=[pt])
                S.op("pe", lambda e: e.matmul(acc.t[:, 0:129], lhsT=pt.t[:], rhs=vv.t[:, j, :], start=(j == 0), stop=(j == gi)),
                     reads=[pt, vv], writes=[acc])
                if j == gi:
                    rc = rcp.next()
                    S.op("dve", lambda e: e.reciprocal(out=rc.t[:], in_=acc.t[:, 128:129]), reads=[acc], writes=[rc])
                    S.op("dve", lambda e: e.tensor_scalar(out=yas.t[:, i, :], in0=acc.t[:, 0:128], scalar1=rc.t[:, 0:1], scalar2=None,
                                                          op0=ALU.mult), reads=[acc, rc], writes=[yas])
            S.dma("sp", ya_d[:, h * 128:(h + 1) * 128].rearrange("(b p) f -> p b f", p=128), yas.t[:], yas, False)
        P4.close()

        S.cur_phase = "p5"
        P5 = Phase()
        rg_bc = bcast_load(P5, "rg", rg_d, H * DV)
        NPV = max(OWN0, 1)
        krwp = P5.pool("krw", 2, [128, NPV, 128], BF16)
        vrp = P5.pool("vr", 2, [128, NCTX, DV], BF16)
        qrp = P5.pool("qrT", 2, [128, TO], BF16)
        krp = P5.pool("krT", 2, [128, TO], BF16)
        sgp = P5.pool("sg", 2, [128, NOWN, DV], BF16)
        stp = P5.pool("st", 2, [128, DV], BF16)
        ptp = P5.pool("pt", 6, [128, 128], BF16)
        yrp = P5.pool("yrs", 1, [128, NOWN, DV], BF16)
        smp = P5.pool("sm", 3, [128, 8], F32)
        jkp = P5.pool("jk", 2, [128, DV], F32)
        onp = P5.pool("on", 2, [128, DV], F32)
        def p5load(h):
            krw = krwp.next(); vr = vrp.next(); qr = qrp.next(); kr = krp.next(); sg = sgp.next()
            S.dma("sp", krw.t[:, 0:OWN0, :], krw_d[0:OWN0 * 128, h, :].rearrange("(b p) f -> p b f", p=128), krw, True)
            S.dma("sp", vr.t[:], vr_d[:, h * DV:(h + 1) * DV].rearrange("(b p) f -> p b f", p=128), vr, True)
            S.dma("sp", qr.t[:], qrT_d[h], qr, True)
            S.dma("sp", kr.t[:], krT_d[h], kr, True)
            S.dma("sp", sg.t[:], sg_d[:, h * DV:(h + 1) * DV].rearrange("(b p) f -> p b f", p=128), sg, True)
            return krw, vr, qr, kr, sg
        nx5 = p5load(0)
        for h in range(H):
            gam = 1.0 - 2.0 ** (-5.0 - h)
            krw, vr, qr, kr, sg = nx5
            if h + 1 < H:
                nx5 = p5load(h + 1)
            yrs = yrp.next()
            sp_ = PSA.next()
            for j in range(OWN0):
                S.op("pe", lambda e, j=j: e.matmul(sp_.t[:, 0:DV], lhsT=krw.t[:, j, :], rhs=vr.t[:, j, :],
                                                   start=(j == 0), stop=(j == OWN0 - 1)), reads=[krw, vr], writes=[sp_])
            stt = stp.next()
            S.op("act", lambda e, stt=stt, sp_=sp_, gam=gam: e.activation(out=stt.t[:], in_=sp_.t[:, 0:DV], func=AF.Copy, scale=float(gam)),
                 reads=[sp_], writes=[stt])
            if dbg:
                S.dma("sp", stt_d[h], stt.t[:], stt, False)
            pairs = [(i, j) for i in range(NOWN) for j in range(i + 1)]
            pss = {}

            def sk(p):
                i, j = pairs[p]
                ps = PSA.next()
                pss[p] = ps
                S.op("pe", lambda e: e.matmul(ps.t[:, 0:128], lhsT=kr.t[:, j * 128:(j + 1) * 128], rhs=qr.t[:, i * 128:(i + 1) * 128],
                                              start=True, stop=True), reads=[kr, qr], writes=[ps])
            LA = 3
            for p in range(min(LA, len(pairs))):
                sk(p)
            acc = None
            for p, (i, j) in enumerate(pairs):
                if p + LA < len(pairs):
                    sk(p + LA)
                if j == 0:
                    acc = PSACC.next()
                    S.op("pe", lambda e: e.matmul(acc.t[:, 0:DV], lhsT=qr.t[:, i * 128:(i + 1) * 128], rhs=stt.t[:],
                                                  start=True, stop=False), reads=[qr, stt], writes=[acc])
                ps = pss.pop(p)
                pt = ptp.next()
                if j == i:
                    S.op("dve", lambda e: e.tensor_tensor(out=pt.t[:], in0=ps.t[:, 0:128], in1=tri.t[:], op=ALU.mult),
                         reads=[ps, tri], writes=[pt])
                else:
                    S.op("act", lambda e: e.copy(out=pt.t[:], in_=ps.t[:, 0:128]), reads=[ps], writes=[pt])
                S.op("pe", lambda e: e.matmul(acc.t[:, 0:DV], lhsT=pt.t[:], rhs=vr.t[:, OWN0 + j, :], start=False, stop=(j == i)),
                     reads=[pt, vr], writes=[acc])
                if j != i:
                    continue
                sm = smp.next(); jk = jkp.next(); on = onp.next()
                S.op("pool", lambda e, sm=sm: e.memset(sm.t[:], 0.0), writes=[sm])
                S.op("act", lambda e, acc=acc, jk=jk, sm=sm: e.activation(out=jk.t[:], in_=acc.t[:, 0:DV], func=AF.Copy,
                                                                         accum_out=sm.t[:, 0:1]), reads=[acc], writes=[jk, sm])
                if dbg:
                    S.dma("sp", oraw_d[h, i * 128:(i + 1) * 128, :], jk.t[:], jk, False)
                S.op("act", lambda e, acc=acc, jk=jk, sm=sm: e.activation(out=jk.t[:], in_=acc.t[:, 0:DV], func=AF.Square,
                                                                         accum_out=sm.t[:, 1:2]), reads=[acc], writes=[jk, sm])
                S.op("dve", lambda e, sm=sm: e.tensor_scalar(out=sm.t[:, 2:4], in0=sm.t[:, 0:2], scalar1=1.0 / DV, scalar2=None, op0=ALU.mult),
                     reads=[sm], writes=[sm])
                S.op("dve", lambda e, sm=sm: e.tensor_tensor(out=sm.t[:, 4:5], in0=sm.t[:, 2:3], in1=sm.t[:, 2:3], op=ALU.mult),
                     reads=[sm], writes=[sm])
                S.op("dve", lambda e, sm=sm: e.tensor_tensor(out=sm.t[:, 5:6], in0=sm.t[:, 3:4], in1=sm.t[:, 4:5], op=ALU.subtract),
                     reads=[sm], writes=[sm])
                S.op("dve", lambda e, sm=sm: e.tensor_scalar(out=sm.t[:, 6:7], in0=sm.t[:, 5:6], scalar1=EPS, scalar2=None,
                                                              op0=ALU.add), reads=[sm], writes=[sm])
                S.op("pool", lambda e, sm=sm: e.tensor_tensor(out=sm.t[:, 6:7], in0=sm.t[:, 6:7], in1=mhalf.t[:, 0:1], op=ALU.pow),
                     reads=[sm, mhalf], writes=[sm])
                S.op("dve", lambda e, sm=sm, acc=acc, on=on: e.tensor_scalar(
                    out=on.t[:], in0=acc.t[:, 0:DV], scalar1=sm.t[:, 2:3], scalar2=sm.t[:, 6:7], op0=ALU.subtract, op1=ALU.mult),
                    reads=[acc, sm], writes=[on])
                S.op("pool", lambda e, on=on, h=h: e.tensor_tensor(out=on.t[:], in0=on.t[:], in1=rg_bc.t[:, h * DV:(h + 1) * DV], op=ALU.mult),
                     reads=[on, rg_bc], writes=[on])
                S.op("pool", lambda e, on=on, sg=sg, yrs=yrs, i=i: e.tensor_tensor(out=yrs.t[:, i, :], in0=on.t[:], in1=sg.t[:, i, :], op=ALU.mult),
                     reads=[on, sg], writes=[yrs])
            S.dma("sp", yr_d[:, h * DV:(h + 1) * DV].rearrange("(b p) f -> p b f", p=128), yrs.t[:], yrs, False)
        P5.close()
        PA.close()

        TT6 = min(NOWN, 4)
        KA = H * HD // 128
        KR = H * DV // 128
        S.cur_phase = "p6"
        P6 = Phase()
        yT = P6.sb("yT", [128, KA + KR, TT6 * 128], BF16)
        mbf = P6.sb("mbf", [128, TT6, D], BF16)
        mT = yT
        wp6 = P6.pool("w", 2, [128, 32, 512], BF16)
        wap6 = P6.pool("wa", 2, [128, KA, 512], BF16)
        ysp = P6.pool("ys", 2, [128, 2048], BF16)
        gtp = P6.pool("gt", 4, [128, 512], BF16)
        xp6 = P6.pool("x6", 2, [128, 512], F32)
        tp6 = P6.pool("t6", 3, [128, 512], F32)
        for b0 in range(0, NOWN, TT6):
            nb = min(TT6, NOWN - b0)
            for tb in range(nb):
                blk = b0 + tb
                for part in range((KA + KR) // 16):
                    ys = ysp.next()
                    if part == 0:
                        S.dma("sp", ys.t[:], ya_d[blk * 128:(blk + 1) * 128, :], ys, True)
                    else:
                        S.dma("sp", ys.t[:], yr_d[blk * 128:(blk + 1) * 128, (part - 1) * 2048:part * 2048], ys, True)
                    for k8 in range(0, 16, 8):
                        pb = PSB.next()
                        for j in range(8):
                            S.op("pe", lambda e: e.transpose(pb.t[:, j * 128:(j + 1) * 128], ys.t[:, (k8 + j) * 128:(k8 + j + 1) * 128], ident.t[:]),
                                 reads=[ys, ident], writes=[pb])
                        kk0 = part * 16 + k8
                        if (k8 // 8) % 2 == 0:
                            S.op("act", lambda e: e.copy(out=yT.t[:, kk0:kk0 + 8, tb * 128:(tb + 1) * 128],
                                                         in_=pb.t[:].rearrange("p (k t) -> p k t", t=128)), reads=[pb], writes=[yT])
                        else:
                            S.op("dve", lambda e: e.tensor_copy(out=yT.t[:, kk0:kk0 + 8, tb * 128:(tb + 1) * 128],
                                                                in_=pb.t[:].rearrange("p (k t) -> p k t", t=128)), reads=[pb], writes=[yT])
            def p6load(nt):
                wa = wap6.next(); wr = wp6.next()
                for k4 in range(0, KA, 8):
                    S.dma("pool", wa.t[:, k4:k4 + 8, :], wpf[k4 * 128:(k4 + 8) * 128, nt * 512:(nt + 1) * 512].rearrange("(kc p) n -> p kc n", p=128), wa, True)
                for k4 in range(0, KR, 8):
                    S.dma("pool", wr.t[:, k4:k4 + 8, :], wpr[k4 * 128:(k4 + 8) * 128, nt * 512:(nt + 1) * 512].rearrange("(kc p) n -> p kc n", p=128), wr, True)
                return wa, wr
            nx6 = p6load(0)
            for nt in range(D // 512):
                wa, wr = nx6
                if nt + 1 < D // 512:
                    nx6 = p6load(nt + 1)
                for tb in range(nb):
                    blk = b0 + tb
                    pa = PSA.next(); pr = PSA.next()
                    for kc in range(KA):
                        S.op("pe", lambda e, pa=pa, wa=wa, kc=kc, tb=tb: e.matmul(pa.t[:], lhsT=yT.t[:, kc, tb * 128:(tb + 1) * 128],
                                                                                 rhs=wa.t[:, kc, :], start=(kc == 0), stop=(kc == KA - 1)),
                             reads=[yT, wa], writes=[pa])
                    for kc in range(KR):
                        S.op("pe", lambda e, pr=pr, wr=wr, kc=kc, tb=tb: e.matmul(pr.t[:], lhsT=yT.t[:, KA + kc, tb * 128:(tb + 1) * 128],
                                                                                 rhs=wr.t[:, kc, :], start=(kc == 0), stop=(kc == KR - 1)),
                             reads=[yT, wr], writes=[pr])
                    g1 = gtp.next(); g2 = gtp.next(); t1 = tp6.next(); t2 = tp6.next()
                    S.dma("sp", g1.t[:], ga_d[blk * 128:(blk + 1) * 128, nt * 512:(nt + 1) * 512], g1, True)
                    S.dma("sp", g2.t[:], gr_d[blk * 128:(blk + 1) * 128, nt * 512:(nt + 1) * 512], g2, True)
                    S.op("dve", lambda e, t1=t1, pa=pa, g1=g1: e.tensor_tensor(out=t1.t[:], in0=pa.t[:], in1=g1.t[:], op=ALU.mult), reads=[pa, g1], writes=[t1])
                    S.op("dve", lambda e, t2=t2, pr=pr, g2=g2: e.tensor_tensor(out=t2.t[:], in0=pr.t[:], in1=g2.t[:], op=ALU.mult), reads=[pr, g2], writes=[t2])
                    S.op("dve", lambda e, t1=t1, t2=t2, tb=tb, nt=nt: e.tensor_tensor(out=mbf.t[:, tb, nt * 512:(nt + 1) * 512], in0=t1.t[:], in1=t2.t[:], op=ALU.add),
                         reads=[t1, t2], writes=[mbf])
            for tb in range(nb):
                for k8 in range(0, KC, 8):
                    pb = PSB.next()
                    n8 = min(8, KC - k8)
                    for j in range(n8):
                        S.op("pe", lambda e, pb=pb, j=j, k8=k8, tb=tb: e.transpose(
                            pb.t[:, j * 128:(j + 1) * 128], mbf.t[:, tb, (k8 + j) * 128:(k8 + j + 1) * 128], ident.t[:]),
                            reads=[mbf, ident], writes=[pb])
                    S.op("act", lambda e, pb=pb, k8=k8, n8=n8, tb=tb: e.copy(out=mT.t[:, k8:k8 + n8, tb * 128:(tb + 1) * 128],
                                                                            in_=pb.t[:, 0:n8 * 128].rearrange("p (k t) -> p k t", t=128)),
                         reads=[pb], writes=[mT])

            def ev_out(tb, nt, ps, nw, b0=b0):
                blk = b0 + tb
                xt = xp6.next()
                S.dma("sp", xt.t[:], ctx_x[(OWN0 + blk) * 128:(OWN0 + blk + 1) * 128, nt * 512:(nt + 1) * 512], xt, True)
                S.op("dve", lambda e: e.tensor_tensor(out=xt.t[:], in0=ps.t[:], in1=xt.t[:], op=ALU.add), reads=[ps, xt], writes=[xt])
                S.dma("sp", h1_d[blk * 128:(blk + 1) * 128, nt * 512:(nt + 1) * 512], xt.t[:], xt, False)
            gemm(mT, nb, list(range(KC)), wout, D, wp6, ev_out)
        P6.close()

        S.cur_phase = "p7"
        P7 = Phase()
        gf_bc = bcast_load(P7, "gffn", gffn_d, D)
        rmsnorm_T(P7, NOWN, lambda blk: h1_d[blk * 128:(blk + 1) * 128, :], gf_bc, xnT_d,
                  P7.pool("x", 3, [128, D], F32), P7.pool("jk", 1, [128, D], BF16),
                  P7.pool("xn", 2, [128, D], BF16), P7.pool("uo", 2, [128, KC, 128], BF16))
        P7.close()

        TT7 = min(NOWN, 8)
        S.cur_phase = "p7b"
        P7b = Phase()
        xT7 = P7b.sb("xT7", [128, KC, TT7 * 128], BF16)
        wp7 = P7b.pool("w", 2, [128, KC, 512], BF16)
        knp = P7b.pool("kn", 3, [128, 512], BF16)
        ktp = P7b.pool("kt", 3, [128, 512], BF16)
        for b0 in range(0, NOWN, TT7):
            nb = min(TT7, NOWN - b0)
            load_xT(xT7, xnT_d, b0, nb)

            def ev_q(tb, nt, ps, nw, b0=b0):
                kn = knp.next()
                S.op("act", lambda e: e.copy(out=kn.t[:], in_=ps.t[:, 0:512]), reads=[ps], writes=[kn])
                transpose4(kn, qpT_d, nt * 4, (b0 + tb) * 128, ktp)
            gemm(xT7, nb, list(range(KC)), wq, PH * 256, wp7, ev_q)
        P7b.close()

        S.cur_phase = "p7c"
        P7c = Phase()
        kk = P7c.sb("kk", [128, 2 * PH, NK], BF16)
        for hh in range(PH):
            S.dma("pool", kk.t[:, 2 * hh, :], k1t[hh], kk, True)
            S.dma("pool", kk.t[:, 2 * hh + 1, :], k2t[hh], kk, True)
        qpp = P7c.pool("qp", 2, [128, 2 * PH, 128], BF16)
        scp = P7c.pool("sc", 2, [128, 2 * PH, NK], F32)
        tmpp = P7c.pool("tmp", 2, [128, 256], F32)
        t16p = P7c.pool("t16", 1, [128, 2 * PH, NK], F32)
        t2p = P7c.pool("t2", 1, [128, PH, 256], F32)
        v12p = P7c.pool("v12", 2, [128, 2 * PH, 16], F32)
        idxp = P7c.pool("idx", 2, [128, PH, 16], U32)
        idfp = P7c.pool("idf", 2, [128, PH * 16], F32)
        candp = P7c.pool("cand", 1, [128, PH, 256], F32)
        tvp = P7c.pool("tv", 2, [128, PH, 16], F32)
        smp = P7c.pool("sm", 2, [128, 4, PH], F32)
        pp_ = P7c.pool("pp", 2, [128, 16, NK], F32)
        ep_ = P7c.pool("ep", 2, [128, 16, NK], F32)
        Rp = P7c.pool("R", 1, [128, 128, NK], BF16)
        Rtp = P7c.pool("Rt", 1, [128, 128, NK], BF16)
        OHp = P7c.pool("OH", 1, [128, 128, NK], BF16)
        itp = P7c.pool("it", 2, [128, 128], F32)
        for blk in range(NOWN):
            qp = qpp.next()
            S.dma("sp", qp.t[:], qpT_d[:, :, blk * 128:(blk + 1) * 128].rearrange("c p t -> p c t"), qp, True)
            sc = scp.next()
            for c4 in range(0, 2 * PH, 4):
                ps = PSA.next()
                for j in range(4):
                    S.op("pe", lambda e, ps=ps, qp=qp, c4=c4, j=j: e.matmul(
                        ps.t[:, j * 128:(j + 1) * 128], lhsT=qp.t[:, c4 + j, :], rhs=kk.t[:, c4 + j, :], start=True, stop=True),
                        reads=[qp, kk], writes=[ps])
                S.op("act", lambda e, ps=ps, sc=sc, c4=c4: e.copy(out=sc.t[:, c4:c4 + 4, :], in_=ps.t[:].rearrange("p (c n) -> p c n", n=NK)),
                     reads=[ps], writes=[sc])
            v12 = v12p.next(); idx = idxp.next(); t16 = t16p.next(); t2 = t2p.next()
            v12s = v12.subs(2 * PH); idxs = idx.subs(PH); t16s = t16.subs(2 * PH); t2s = t2.subs(PH)
            for c in range(2 * PH):
                S.op("dve", lambda e: e.max(out=v12.t[:, c, 0:8], in_=sc.t[:, c, :]), reads=[sc], writes=[v12s[c]])
            for c in range(2 * PH):
                S.op("dve", lambda e: e.match_replace(out=t16.t[:, c, :], in_to_replace=v12.t[:, c, 0:8], in_values=sc.t[:, c, :],
                                                      imm_value=-1e30), reads=[sc, v12s[c]], writes=[t16s[c]])
            for c in range(2 * PH):
                S.op("dve", lambda e: e.max(out=v12.t[:, c, 8:16], in_=t16.t[:, c, :]), reads=[t16s[c]], writes=[v12s[c]])
            for hh in range(PH):
                c = 2 * hh
                S.op("dve", lambda e: e.max_index(out=idx.t[:, hh, 0:8], in_max=v12.t[:, c, 0:8], in_values=sc.t[:, c, :]),
                     reads=[sc, v12s[c]], writes=[idxs[hh]])
            for hh in range(PH):
                c = 2 * hh
                S.op("dve", lambda e: e.max_index(out=idx.t[:, hh, 8:16], in_max=v12.t[:, c, 8:16], in_values=t16.t[:, c, :]),
                     reads=[t16s[c], v12s[c]], writes=[idxs[hh]])
            cand = candp.next(); tv = tvp.next(); sm = smp.next()
            tvs = tv.subs(PH)
            vv = v12.t[:].rearrange("p (h two) k -> p h two k", two=2)
            S.op("pool", lambda e: e.tensor_tensor(
                out=cand.t[:].rearrange("p h (a b) -> p h a b", a=16),
                in0=vv[:, :, 0, :].unsqueeze(3).to_broadcast([128, PH, 16, 16]),
                in1=vv[:, :, 1, :].unsqueeze(2).to_broadcast([128, PH, 16, 16]), op=ALU.add), reads=v12s, writes=[cand])
            for hh in range(PH):
                S.op("dve", lambda e: e.max(out=tv.t[:, hh, 0:8], in_=cand.t[:, hh, :]), reads=[cand], writes=[tvs[hh]])
            for hh in range(PH):
                S.op("dve", lambda e: e.match_replace(out=t2.t[:, hh, :], in_to_replace=tv.t[:, hh, 0:8], in_values=cand.t[:, hh, :],
                                                      imm_value=-1e30), reads=[cand, tvs[hh]], writes=[t2s[hh]])
            for hh in range(PH):
                S.op("dve", lambda e: e.max(out=tv.t[:, hh, 8:16], in_=t2.t[:, hh, :]), reads=[t2s[hh]], writes=[tvs[hh]])
            S.op("dve", lambda e, sm=sm, tv=tv: e.tensor_scalar(out=sm.t[:, 0, :], in0=tv.t[:, :, 0], scalar1=-1.0, scalar2=None, op0=ALU.mult),
                 reads=tvs, writes=[sm])
            S.op("dve", lambda e, sm=sm, tv=tv: e.tensor_copy(out=sm.t[:, 3, :], in_=tv.t[:, :, 15]), reads=tvs, writes=[sm])
            ex = tmpp.next()
            S.op("dve", lambda e, sm=sm, tv=tv, ex=ex: e.tensor_tensor(
                out=ex.t[:, 0:PH * 16].rearrange("p (h k) -> p h k", k=16), in0=tv.t[:],
                in1=sm.t[:, 0, :].unsqueeze(2).to_broadcast([128, PH, 16]), op=ALU.add), reads=tvs + [sm], writes=[ex])
            S.op("act", lambda e, ex=ex: e.activation(out=ex.t[:, 0:PH * 16], in_=ex.t[:, 0:PH * 16], func=AF.Exp), reads=[ex], writes=[ex])
            S.op("dve", lambda e, sm=sm, ex=ex: e.tensor_reduce(out=sm.t[:, 1, :], in_=ex.t[:, 0:PH * 16].rearrange("p (h k) -> p h k", k=16),
                                                               axis=AX.X, op=ALU.add), reads=[ex], writes=[sm])
            S.op("act", lambda e, sm=sm: e.activation(out=sm.t[:, 2, :], in_=sm.t[:, 1, :], func=AF.Ln), reads=[sm], writes=[sm])
            S.op("dve", lambda e, sm=sm: e.tensor_tensor(out=sm.t[:, 2, :], in0=sm.t[:, 0, :], in1=sm.t[:, 2, :], op=ALU.subtract),
                 reads=[sm], writes=[sm])
            R = Rp.next()
            for hh in range(PH):
                pp = pp_.next(); ep = ep_.next()
                S.op("pool", lambda e, pp=pp, hh=hh: e.tensor_tensor(
                    out=pp.t[:], in0=sc.t[:, 2 * hh + 1, :].unsqueeze(1).to_broadcast([128, 16, NK]),
                    in1=v12.t[:, 2 * hh, :].unsqueeze(2).to_broadcast([128, 16, NK]), op=ALU.add), reads=[sc, v12s[2 * hh]], writes=[pp])
                S.op("act", lambda e, pp=pp, ep=ep, hh=hh, sm=sm: e.activation(out=ep.t[:], in_=pp.t[:], func=AF.Exp,
                                                                              bias=sm.t[:, 2, hh:hh + 1], scale=1.0),
                     reads=[pp, sm], writes=[ep])
                S.op("dve", lambda e, pp=pp, ep=ep, hh=hh, sm=sm, R=R: e.scalar_tensor_tensor(
                    out=R.t[:, hh * 16:(hh + 1) * 16, :], in0=pp.t[:], scalar=sm.t[:, 3, hh:hh + 1], in1=ep.t[:],
                    op0=ALU.is_ge, op1=ALU.mult), reads=[pp, ep, sm], writes=[R])
            Rt = Rtp.next()
            for i8 in range(0, NK, 8):
                pb = PSB.next()
                for j in range(8):
                    S.op("pe", lambda e, pb=pb, R=R, i8=i8, j=j: e.transpose(pb.t[:, j * 128:(j + 1) * 128], R.t[:, :, i8 + j], ident.t[:]),
                         reads=[R, ident], writes=[pb])
                eng = "act" if (i8 // 8) % 2 == 0 else "dve"
                outv = Rt.t[:, :, i8:i8 + 8].rearrange("p t i -> p i t")
                if eng == "act":
                    S.op("act", lambda e, pb=pb, outv=outv: e.copy(out=outv, in_=pb.t[:].rearrange("p (i t) -> p i t", t=128)), reads=[pb], writes=[Rt])
                else:
                    S.op("dve", lambda e, pb=pb, outv=outv: e.tensor_copy(out=outv, in_=pb.t[:].rearrange("p (i t) -> p i t", t=128)), reads=[pb], writes=[Rt])
            idf = idfp.next()
            S.op("dve", lambda e, idf=idf, idx=idx: e.tensor_copy(out=idf.t[:], in_=idx.t[:].rearrange("p h k -> p (h k)")), reads=idxs, writes=[idf])
            pt_ = PSA.next()
            S.op("pe", lambda e, pt_=pt_, idf=idf: e.transpose(pt_.t[:, 0:128], idf.t[:], identf.t[:]), reads=[idf, identf], writes=[pt_])
            it = itp.next()
            S.op("act", lambda e, it=it, pt_=pt_: e.copy(out=it.t[:], in_=pt_.t[:, 0:128]), reads=[pt_], writes=[it])
            OH = OHp.next()
            S.op("dve", lambda e, OH=OH, it=it: e.tensor_tensor(
                out=OH.t[:], in0=iotar.t[:].unsqueeze(1).to_broadcast([128, 128, NK]),
                in1=it.t[:].unsqueeze(2).to_broadcast([128, 128, NK]), op=ALU.is_equal), reads=[iotar, it], writes=[OH])
            G = R
            for t4 in range(0, 128, 4):
                ps = PSA.next()
                for j in range(4):
                    S.op("pe", lambda e, ps=ps, t4=t4, j=j: e.matmul(ps.t[:, j * 128:(j + 1) * 128], lhsT=Rt.t[:, t4 + j, :], rhs=OH.t[:, t4 + j, :],
                                                                      start=True, stop=True), reads=[Rt, OH], writes=[ps])
                outv = G.t[:, :, t4:t4 + 4].rearrange("p i t -> p t i")
                if (t4 // 4) % 2 == 0:
                    S.op("act", lambda e, ps=ps, outv=outv: e.copy(out=outv, in_=ps.t[:].rearrange("p (t i) -> p t i", i=NK)), reads=[ps], writes=[G])
                else:
                    S.op("dve", lambda e, ps=ps, outv=outv: e.tensor_copy(out=outv, in_=ps.t[:].rearrange("p (t i) -> p t i", i=NK)), reads=[ps], writes=[G])
            for i16 in range(0, NK, 16):
                S.dma("sp", G_d[i16:i16 + 16, :, blk * 128:(blk + 1) * 128].rearrange("i p t -> p i t"), G.t[:, i16:i16 + 16, :], G, False)
        P7c.close()

        TT8 = min(NOWN, 4)
        S.cur_phase = "p8"
        P8 = Phase()
        xT8 = P8.sb("xT8", [128, KC, TT8 * 128], BF16)
        yacc = P8.sb("yacc", [128, TT8, D], F32)
        utp = P8.pool("ut", 2, [128, KC, 256], BF16)
        vtp = P8.pool("vt", 4, [128, D], BF16)
        gp8 = P8.pool("g8", 4, [128, TT8 * 128], BF16)
        gep = P8.pool("ge", 3, [128, TT8 * 128], F32)
        atp = P8.pool("at", 4, [128, TT8 * 128], BF16)
        for b0 in range(0, NOWN, TT8):
            nb = min(TT8, NOWN - b0)
            ntok = nb * 128
            load_xT(xT8, xnT_d, b0, nb)
            for tb in range(nb):
                S.dma("sp", yacc.t[:, tb, :], h1_d[(b0 + tb) * 128:(b0 + tb + 1) * 128, :], yacc, True)
            def p8load(grp):
                ut = utp.next()
                for k4 in range(0, KC, 8):
                    k5 = min(KC, k4 + 8)
                    S.dma("pool", ut.t[:, k4:k5, :], ut_d[grp, :, k4:k5, :], ut, True)
                vts_ = []
                g8s_ = []
                for cl in range(2):
                    c = grp * 2 + cl
                    vt = vtp.next()
                    for d4 in range(0, D, 2048):
                        d5 = min(D, d4 + 2048)
                        S.dma("pool", vt.t[:, d4:d5], pv_d[c * 128:(c + 1) * 128, d4:d5], vt, True)
                    g8 = gp8.next()
                    S.dma("sp", g8.t[:, 0:ntok], G_d[c, :, b0 * 128:b0 * 128 + ntok], g8, True)
                    vts_.append(vt); g8s_.append(g8)
                return ut, vts_, g8s_
            nx8 = p8load(0)
            for grp in range(NE // 256):
                ut, vts, g8s = nx8
                if grp + 1 < NE // 256:
                    nx8 = p8load(grp + 1)
                ats = []
                for cl in range(2):
                    g8 = g8s[cl]
                    ps = PSA.next()
                    for kc in range(KC):
                        S.op("pe", lambda e, ps=ps, ut=ut, kc=kc, cl=cl, ntok=ntok: e.matmul(
                            ps.t[:, 0:ntok], lhsT=ut.t[:, kc, cl * 128:(cl + 1) * 128], rhs=xT8.t[:, kc, 0:ntok],
                            start=(kc == 0), stop=(kc == KC - 1)), reads=[ut, xT8], writes=[ps])
                    ge = gep.next(); at = atp.next()
                    S.op("act", lambda e, ps=ps, ge=ge, ntok=ntok: e.activation(out=ge.t[:, 0:ntok], in_=ps.t[:, 0:ntok], func=AF.Gelu),
                         reads=[ps], writes=[ge])
                    S.op("dve", lambda e, ge=ge, g8=g8, at=at, ntok=ntok: e.tensor_tensor(out=at.t[:, 0:ntok], in0=ge.t[:, 0:ntok], in1=g8.t[:, 0:ntok], op=ALU.mult),
                         reads=[ge, g8], writes=[at])
                    ats.append(at)
                for tb in range(nb):
                    for dt in range(D // 512):
                        ps = PSA.next()
                        for cl in range(2):
                            S.op("pe", lambda e, ps=ps, cl=cl, tb=tb, dt=dt, at=ats[cl], vt=vts[cl]: e.matmul(
                                ps.t[:], lhsT=at.t[:, tb * 128:(tb + 1) * 128], rhs=vt.t[:, dt * 512:(dt + 1) * 512],
                                start=(cl == 0), stop=(cl == 1)), reads=[ats[cl], vts[cl]], writes=[ps])
                        S.op("dve", lambda e, ps=ps, tb=tb, dt=dt: e.tensor_tensor(
                            out=yacc.t[:, tb, dt * 512:(dt + 1) * 512], in0=ps.t[:], in1=yacc.t[:, tb, dt * 512:(dt + 1) * 512], op=ALU.add),
                            reads=[ps, yacc], writes=[yacc])
            for tb in range(nb):
                S.dma("sp", out_d[(b0 + tb) * 128:(b0 + tb + 1) * 128, :], yacc.t[:, tb, :], yacc, False)
        P8.close()
        P0.close()
        S.emit()
    return nc


_CACHE = {}


def kernel(x, meta_tokens, norm_mix_g, w_in, b_forget, q_norm_g, k_norm_g, ret_norm_g, w_proj_fox, w_proj_ret,
           w_out, norm_ffn_g, peer_w_q, peer_keys_1, peer_keys_2, peer_u, peer_v, _dbg=False):
    f = lambda a: np.ascontiguousarray(np.asarray(a, dtype=np.float32))
    x = f(x)
    B, SEQ, D = x.shape
    NB = SEQ // 128
    NOWN = NB // 4
    NCTX = 1 + NB
    T = NCTX * 128
    KC = D // 128
    key = (D, NCTX, NOWN)
    key = (D, NCTX, NOWN, _dbg)
    if key not in _CACHE:
        _CACHE[key] = build(D, NCTX, NOWN, _dbg)
    nc = _CACHE[key]
    meta = f(meta_tokens)
    pu = f(peer_u)[0]
    NE = pu.shape[0]
    ut = np.ascontiguousarray(pu.reshape(NE // 256, 256, KC, 128).transpose(0, 3, 2, 1))
    shared = {
        "norm_mix_g": f(norm_mix_g)[0], "w_in": f(w_in)[0], "b_forget": f(b_forget)[0], "q_norm_g": f(q_norm_g)[0],
        "k_norm_g": f(k_norm_g)[0], "ret_norm_g": f(ret_norm_g)[0], "w_proj_fox": f(w_proj_fox)[0],
        "w_proj_ret": f(w_proj_ret)[0], "w_out": f(w_out)[0], "norm_ffn_g": f(norm_ffn_g)[0],
        "peer_w_q": f(peer_w_q)[0],
        "k1t": np.ascontiguousarray(f(peer_keys_1)[0].transpose(0, 2, 1)),
        "k2t": np.ascontiguousarray(f(peer_keys_2)[0].transpose(0, 2, 1)),
        "ut": ut, "peer_v": f(peer_v)[0],
    }
    gam = 1.0 - 2.0 ** (-5.0 - np.arange(H, dtype=np.float64))
    inv = ROPE_BASE ** (-np.arange(64, dtype=np.float64) / 64)
    in_maps = []
    for c in range(8):
        b, g = c // 4, c % 4
        ndum = (3 - g) * NOWN * 128
        nprev = g * NOWN * 128
        ctx = np.zeros((T, D), np.float32)
        ctx[ndum + PAD:ndum + 128] = meta
        ctx[ndum + 128:ndum + 128 + nprev] = x[b, :nprev]
        ctx[T - NOWN * 128:] = x[b, nprev:nprev + NOWN * 128]
        n = np.arange(T) - ndum
        valid = (n >= PAD).astype(np.float32)
        pos = (n - PAD).astype(np.float64)
        ang = pos[:, None] * inv[None, :]
        cossin = np.concatenate([np.cos(ang), np.sin(ang)], axis=1).astype(np.float32)
        start = 128 + nprev
        lpos = (n - start).astype(np.float64)
        own = n >= start
        ktab = np.zeros((T, H), np.float64)
        with np.errstate(over="ignore", under="ignore"):
            ktab[~own] = np.exp(np.log(gam)[None, :] * (start - 1 - n[~own])[:, None])
            ktab[own] = np.exp(-np.log(gam)[None, :] * lpos[own][:, None])
            qtab = np.exp(np.log(gam)[None, :] * lpos[own][:, None])
        ktab = ktab * (128.0 ** -0.5) * valid[:, None]
        m = dict(shared)
        m.update({"ctx_x": ctx, "valid": np.ascontiguousarray(valid.reshape(NCTX, 128).T), "cossin": cossin, "ktab": ktab.astype(np.float32),
                  "qtab": qtab.astype(np.float32)})
        in_maps.append(m)
    res = run_bass_kernel_spmd(nc, in_maps, core_ids=list(range(8)))
    if _dbg:
        return res.results, in_maps
    out = np.zeros((B, SEQ, D), np.float32)
    for c in range(8):
        b, g = c // 4, c % 4
        out[b, g * NOWN * 128:(g + 1) * NOWN * 128] = res.results[c]["out"]
    return out
```

```python
import numpy as np
from contextlib import ExitStack
import concourse.bass as bass
import concourse.mybir as mybir
from concourse.bass_utils import run_bass_kernel_spmd

F32 = mybir.dt.float32
BF16 = mybir.dt.bfloat16
U32 = mybir.dt.uint32
AF = mybir.ActivationFunctionType
ALU = mybir.AluOpType
AX = mybir.AxisListType

N_META = 16
PAD = 112
EPS = 1e-6
H = 16
HD = 128
DV = 256
PH = 8
NK = 128
TOPK = 16
ROPE_BASE = 10000.0


class Res:
    __slots__ = ("w", "r", "name")

    def __init__(self, name=""):
        self.w = None
        self.r = {}
        self.name = name


class DSem:
    __slots__ = ("sem", "tot")


class Ins:
    __slots__ = ("eng", "fn", "waits", "sig", "sigval", "dsem", "dval", "idx", "ph")


class Tile:
    __slots__ = ("t", "res", "ds", "_subs")

    def __init__(self, t, name):
        self.t = t
        self.res = Res(name)
        self.ds = None
        self._subs = None

    def subs(self, n):
        if self._subs is None:
            self._subs = [Tile(self.t, "%s.%d" % (self.res.name, i)) for i in range(n)]
        return self._subs


class Pool:
    def __init__(self, tiles):
        self.tiles = tiles
        self.i = 0

    def next(self):
        t = self.tiles[self.i % len(self.tiles)]
        self.i += 1
        return t


class _Rec:
    def __getattr__(self, name):
        def f(*a, **k):
            self.call = (name, a, k)
            return self
        return f


class Sched:
    ENGS = ("pe", "act", "dve", "pool", "sp")

    def __init__(self, nc, stack):
        self.nc = nc
        self.stack = stack
        self.lists = {e: [] for e in self.ENGS}
        self.esem = {e: stack.enter_context(nc.semaphore("es_" + e)) for e in self.ENGS}
        self.dsems = []
        self.free_ds = []
        self.lastc = {e: None for e in self.ENGS}
        self.cur_phase = "p0"
        self.scopes = False

    def get_ds(self):
        if self.free_ds:
            return self.free_ds.pop()
        d = DSem()
        d.sem = self.stack.enter_context(self.nc.semaphore("ds%d" % len(self.dsems)))
        d.tot = 0
        self.dsems.append(d)
        return d

    def op(self, eng, fn, reads=(), writes=(), dsem=None):
        rec = _Rec()
        fn(rec)
        name, a, k = rec.call
        ins = Ins()
        ins.eng = eng
        ins.fn = lambda e: getattr(e, name)(*a, **k)
        ins.sig = False
        ins.sigval = 0
        ins.dsem = dsem
        ins.idx = len(self.lists[eng])
        ins.ph = self.cur_phase
        deps = {}

        def add(d):
            if d is None:
                return
            if d.dsem is not None:
                deps[("d", id(d.dsem))] = d
            else:
                if d.eng == eng and eng == "pe" and dsem is None:
                    return
                k = ("c", d.eng)
                if k not in deps or deps[k].idx < d.idx:
                    deps[k] = d
        for t in reads:
            add(t.res.w)
        for t in writes:
            add(t.res.w)
            for d in t.res.r.values():
                add(d)
        waits = []
        for d in deps.values():
            if d.dsem is not None:
                waits.append((d.dsem, d.dsem.tot))
            else:
                d.sig = True
                waits.append(d)
        ins.waits = waits
        if dsem is not None:
            dsem.tot += 16
            ins.dval = dsem.tot
        else:
            self.lastc[eng] = ins
        key = ("d", id(dsem)) if dsem is not None else ("c", eng)
        for t in reads:
            t.res.r[key] = ins
        for t in writes:
            t.res.w = ins
            t.res.r = {}
        self.lists[eng].append(ins)
        return ins

    def dma(self, q, out, in_, tile, load, deps=None):
        if tile.ds is None:
            tile.ds = self.get_ds()
        fn = lambda e: e.dma_start(out=out, in_=in_)
        dl = [tile] if deps is None else list(deps)
        if load:
            return self.op(q, fn, writes=dl, dsem=tile.ds)
        return self.op(q, fn, reads=dl, dsem=tile.ds)

    def barrier(self, tiles=()):
        last = [self.lastc[e] for e in self.ENGS if self.lastc[e] is not None]
        dtot = [(d, d.tot) for d in self.dsems if d.tot > 0]
        for d in last:
            d.sig = True
        for e in self.ENGS:
            ins = self.op(e, lambda eng: eng.nop())
            ins.waits = list(last) + list(dtot)
        for t in tiles:
            if t.ds is not None:
                self.free_ds.append(t.ds)
                t.ds = None

    def emit(self):
        nc = self.nc
        for e in self.ENGS:
            c = 0
            for ins in self.lists[e]:
                if ins.dsem is None and ins.sig:
                    c += 1
                    ins.sigval = c
        with nc.Block() as block:
            def run(e):
                def body(eng):
                    waited = {}
                    cur = None
                    for ins in self.lists[e]:
                        if self.scopes and ins.ph != cur:
                            if cur is not None:
                                scope.__exit__(None, None, None)
                            scope = nc.named_scope(ins.ph)
                            scope.__enter__()
                            cur = ins.ph
                        for w in ins.waits:
                            if isinstance(w, tuple):
                                sem, val = w[0].sem, w[1]
                            else:
                                sem, val = self.esem[w.eng], w.sigval
                            k = id(sem)
                            if waited.get(k, 0) >= val:
                                continue
                            waited[k] = val
                            eng.wait_ge(sem, val)
                        bi = ins.fn(eng)
                        if ins.dsem is not None:
                            bi.then_inc(ins.dsem.sem, 16)
                        elif ins.sig:
                            bi.then_inc(self.esem[e], 1)
                    if e == "sp":
                        for d in self.dsems:
                            if d.tot > 0:
                                eng.wait_ge(d.sem, d.tot)
                    if cur is not None:
                        scope.__exit__(None, None, None)
                return body
            block.tensor(run("pe"))
            block.scalar(run("act"))
            block.vector(run("dve"))
            block.gpsimd(run("pool"))
            block.sync(run("sp"))


def build(D, NCTX, NOWN, dbg=False, scopes=False):
    KC = D // 128
    T = NCTX * 128
    TO = NOWN * 128
    OWN0 = NCTX - NOWN
    NE = NK * NK
    nc = bass.Bass("TRN2", target_bir_lowering=False)

    def din(name, shape, dt=F32):
        return nc.dram_tensor(name, shape, dt, kind="ExternalInput").ap()

    def dscr(name, shape, dt=BF16):
        return nc.dram_tensor(name, shape, dt, kind="ExternalOutput" if dbg else "Internal").ap()

    ctx_x = din("ctx_x", [T, D])
    valid_d = din("valid", [128, NCTX])
    cs_d = din("cossin", [T, 128])
    ktab_d = din("ktab", [T, H])
    qtab_d = din("qtab", [TO, H])
    gmix_d = din("norm_mix_g", [D])
    w_in = din("w_in", [D, 26640 - 8192 + 2 * D])
    bf_d = din("b_forget", [H])
    qg_d = din("q_norm_g", [HD])
    kg_d = din("k_norm_g", [HD])
    rg_d = din("ret_norm_g", [H * DV])
    wpf = din("w_proj_fox", [H * HD, D])
    wpr = din("w_proj_ret", [H * DV, D])
    wout = din("w_out", [D, D])
    gffn_d = din("norm_ffn_g", [D])
    wq = din("peer_w_q", [D, PH * 256])
    k1t = din("k1t", [PH, 128, NK])
    k2t = din("k2t", [PH, 128, NK])
    ut_d = din("ut", [NE // 256, 128, KC, 256])
    pv_d = din("peer_v", [NE, D])
    out_d = nc.dram_tensor("out", [TO, D], F32, kind="ExternalOutput").ap()

    uT_d = dscr("uT", [128, KC, T])
    kT_d = dscr("kT", [H, 128, T])
    vf_d = dscr("vf", [T, H, 129])
    krw_d = dscr("krw", [T, H, 128])
    krT_d = dscr("krT", [H, 128, TO])
    vr_d = dscr("vr", [T, H * DV])
    qT_d = dscr("qT", [H, 128, TO])
    qrT_d = dscr("qrT", [H, 128, TO])
    sg_d = dscr("sg", [TO, H * DV])
    ga_d = dscr("ga", [TO, D])
    gr_d = dscr("gr", [TO, D])
    ya_d = dscr("ya", [TO, H * HD])
    yr_d = dscr("yr", [TO, H * DV])
    h1_d = dscr("h1", [TO, D], F32)
    xnT_d = dscr("xnT", [128, KC, TO])
    qpT_d = dscr("qpT", [2 * PH, 128, TO])
    G_d = dscr("G", [NK, 128, TO])
    oraw_d = dscr("oraw", [H, TO, DV], F32) if dbg else None
    stt_d = dscr("sttd", [H, 128, DV], BF16) if dbg else None

    c_qa, c_ka, c_va, c_fa = 0, 2048, 4096, 6144
    c_qr = 6160
    c_kr = c_qr + 2048
    c_vr = c_kr + 2048
    c_gr = c_vr + 4096
    c_ga = c_gr + 4096
    c_gtr = c_ga + D

    with ExitStack() as st:
        S = Sched(nc, st)
        S.scopes = scopes
        psf = [Tile(st.enter_context(nc.psum_tensor("psf%d" % i, [128, 512], F32)), "psf%d" % i) for i in range(6)]
        psb = [Tile(st.enter_context(nc.psum_tensor("psb%d" % i, [128, 1024], BF16)), "psb%d" % i) for i in range(2)]
        PSA = Pool(psf[0:4])
        PSACC = Pool(psf[4:6])
        PSB = Pool(psb)

        cnt = [0]

        class Phase:
            def __init__(self):
                self.st = ExitStack()
                self.tiles = []

            def sb(self, name, shape, dt):
                cnt[0] += 1
                name = "s%d_%s" % (cnt[0], name)
                t = Tile(self.st.enter_context(nc.sbuf_tensor(name, shape, dt)), name)
                self.tiles.append(t)
                return t

            def pool(self, name, n, shape, dt):
                return Pool([self.sb("%s%d" % (name, i), shape, dt) for i in range(n)])

            def close(self):
                S.barrier(self.tiles + psf + psb)
                self.st.close()

        P0 = Phase()
        ident = P0.sb("ident", [128, 128], BF16)
        identf = P0.sb("identf", [128, 128], F32)
        tri = P0.sb("tri", [128, 128], BF16)
        trif = P0.sb("trif", [128, 128], F32)
        onesf = P0.sb("onesf", [128, 128], F32)
        iotar = P0.sb("iotar", [128, 128], F32)
        ctmp = P0.sb("ctmp", [128, 128], F32)
        mhalf = P0.sb("mhalf", [128, 8], F32)
        S.op("pool", lambda e: e.memset(mhalf.t[:], -0.5), writes=[mhalf])
        PA = Phase()
        valid = PA.sb("valid", [128, NCTX], F32)
        ktab = PA.sb("ktab", [128, NCTX, H], F32)
        qtab = PA.sb("qtab", [128, NOWN, H], F32)
        Lfull = PA.sb("Lfull", [128, NCTX, H], F32)
        Lb = PA.sb("Lb", [128, NCTX + 1, H], F32)
        S.op("pool", lambda e: e.iota(ctmp.t[:], pattern=[[1, 128]], base=0, channel_multiplier=-1,
                                      allow_small_or_imprecise_dtypes=True), writes=[ctmp])
        S.op("dve", lambda e: e.tensor_single_scalar(out=identf.t[:], in_=ctmp.t[:], scalar=0.0, op=ALU.is_equal),
             reads=[ctmp], writes=[identf])
        S.op("dve", lambda e: e.tensor_copy(out=ident.t[:], in_=identf.t[:]), reads=[identf], writes=[ident])
        S.op("dve", lambda e: e.tensor_single_scalar(out=trif.t[:], in_=ctmp.t[:], scalar=0.0, op=ALU.is_ge),
             reads=[ctmp], writes=[trif])
        S.op("dve", lambda e: e.tensor_copy(out=tri.t[:], in_=trif.t[:]), reads=[trif], writes=[tri])
        S.op("pool", lambda e: e.memset(onesf.t[:], 1.0), writes=[onesf])
        S.op("pool", lambda e: e.iota(iotar.t[:], pattern=[[1, 128]], base=0, channel_multiplier=0,
                                      allow_small_or_imprecise_dtypes=True), writes=[iotar])
        S.op("pool", lambda e: e.memset(Lb.t[:, 0, :], 0.0), writes=[Lb])
        S.dma("sp", valid.t[:], valid_d, valid, True)
        S.dma("sp", ktab.t[:], ktab_d.rearrange("(b p) h -> p b h", p=128), ktab, True)
        S.dma("sp", qtab.t[:], qtab_d.rearrange("(b p) h -> p b h", p=128), qtab, True)

        def bcast_load(ph, name, src, n, scale=None):
            t = ph.sb(name, [128, n], F32)
            S.dma("sp", t.t[:], src.partition_broadcast(128), t, True)
            if scale is not None:
                S.op("dve", lambda e: e.tensor_scalar(out=t.t[:], in0=t.t[:], scalar1=float(scale), scalar2=None,
                                                      op0=ALU.mult), reads=[t], writes=[t])
            return t

        def rmsnorm_T(ph, nblk, src_rows, g_bc, dstT, xpool, jpool, npool, opool):
            ss = ph.pool("ss", 2, [128, 2], F32)
            xq = []

            def xload(b):
                xt_ = xpool.next()
                S.dma("sp", xt_.t[:], src_rows(b), xt_, True)
                xq.append(xt_)
            for b in range(min(2, nblk)):
                xload(b)
            for blk in range(nblk):
                xt = xq[blk]
                if blk + 2 < nblk:
                    xload(blk + 2)
                jk = jpool.next()
                s1 = ss.next()
                S.op("pool", lambda e, s1=s1: e.memset(s1.t[:], 0.0), writes=[s1])
                S.op("act", lambda e, xt=xt, jk=jk, s1=s1: e.activation(out=jk.t[:], in_=xt.t[:], func=AF.Square,
                                                                         accum_out=s1.t[:, 0:1]),
                     reads=[xt], writes=[jk, s1])
                S.op("dve", lambda e, s1=s1: e.tensor_scalar(out=s1.t[:, 1:2], in0=s1.t[:, 0:1], scalar1=1.0 / D,
                                                              scalar2=EPS, op0=ALU.mult, op1=ALU.add),
                     reads=[s1], writes=[s1])
                S.op("pool", lambda e, s1=s1: e.tensor_tensor(out=s1.t[:, 1:2], in0=s1.t[:, 1:2], in1=mhalf.t[:, 0:1], op=ALU.pow),
                     reads=[s1, mhalf], writes=[s1])
                xn = npool.next()
                S.op("dve", lambda e, xt=xt, xn=xn, s1=s1: e.scalar_tensor_tensor(
                    out=xn.t[:], in0=xt.t[:], scalar=s1.t[:, 1:2], in1=g_bc.t[:], op0=ALU.mult, op1=ALU.mult),
                    reads=[xt, s1, g_bc], writes=[xn])
                ot = opool.next()
                for k8 in range(0, KC, 8):
                    pb = PSB.next()
                    n8 = min(8, KC - k8)
                    for j in range(n8):
                        S.op("pe", lambda e, pb=pb, xn=xn, j=j, k8=k8: e.transpose(
                            pb.t[:, j * 128:(j + 1) * 128], xn.t[:, (k8 + j) * 128:(k8 + j + 1) * 128], ident.t[:]),
                            reads=[xn, ident], writes=[pb])
                    eng = "act" if (k8 // 8) % 2 == 0 else "dve"
                    if eng == "act":
                        S.op("act", lambda e, pb=pb, ot=ot, k8=k8, n8=n8: e.copy(
                            out=ot.t[:, k8:k8 + n8, :], in_=pb.t[:, 0:n8 * 128].rearrange("p (k t) -> p k t", t=128)),
                            reads=[pb], writes=[ot])
                    else:
                        S.op("dve", lambda e, pb=pb, ot=ot, k8=k8, n8=n8: e.tensor_copy(
                            out=ot.t[:, k8:k8 + n8, :], in_=pb.t[:, 0:n8 * 128].rearrange("p (k t) -> p k t", t=128)),
                            reads=[pb], writes=[ot])
                S.dma("sp", dstT[:, :, blk * 128:(blk + 1) * 128], ot.t[:], ot, False)

        S.cur_phase = "p1"
        P1 = Phase()
        g_bc = bcast_load(P1, "gmix", gmix_d, D)
        rmsnorm_T(P1, NCTX, lambda blk: ctx_x[blk * 128:(blk + 1) * 128, :], g_bc, uT_d,
                  P1.pool("x", 3, [128, D], F32), P1.pool("jk", 1, [128, D], BF16),
                  P1.pool("xn", 2, [128, D], BF16), P1.pool("uo", 2, [128, KC, 128], BF16))
        P1.close()

        def gemm_multi(xT, nblk, kcs, wpool, jobs):
            nkc = len(kcs)
            tiles = []
            for (W, N, evac) in jobs:
                for nt in range((N + 511) // 512):
                    tiles.append((W, nt, min(512, N - nt * 512), evac))

            def load(ti):
                W, nt, nw, _ = tiles[ti]
                wt = wpool.next()
                n0 = nt * 512
                for k4 in range(0, nkc, 8):
                    k5 = min(nkc, k4 + 8)
                    S.dma("pool", wt.t[:, k4:k5, 0:nw],
                          W[k4 * 128:k5 * 128, n0:n0 + nw].rearrange("(kc p) n -> p kc n", p=128), wt, True)
                return wt
            nxt = load(0)
            for ti, (W, nt, nw, evac) in enumerate(tiles):
                wt = nxt
                if ti + 1 < len(tiles):
                    nxt = load(ti + 1)
                for tb in range(nblk):
                    ps = PSA.next()
                    for i, kc in enumerate(kcs):
                        S.op("pe", lambda e, ps=ps, wt=wt, i=i, kc=kc, tb=tb, nw=nw: e.matmul(
                            ps.t[:, 0:nw], lhsT=xT.t[:, kc, tb * 128:(tb + 1) * 128], rhs=wt.t[:, i, 0:nw],
                            start=(i == 0), stop=(i == nkc - 1)), reads=[xT, wt], writes=[ps])
                    evac(tb, nt, ps, nw)

        def gemm(xT, nblk, kcs, W, N, wpool, evac, pre=None):
            gemm_multi(xT, nblk, kcs, wpool, [(W, N, evac)])

        def load_xT(xT, srcT, b0, nb):
            for k4 in range(0, KC, 8):
                k5 = min(KC, k4 + 8)
                S.dma("sp", xT.t[:, k4:k5, 0:nb * 128], srcT[:, k4:k5, b0 * 128:(b0 + nb) * 128], xT, True)

        def qknorm_T(ph, ps, g_t, dst, h0, col0, ssp, knp, ktp):
            s4 = ssp.next()
            kn = knp.next()
            S.op("pool", lambda e: e.memset(s4.t[:], 0.0), writes=[s4])
            for h in range(4):
                S.op("act", lambda e, h=h: e.activation(out=kn.t[:, h * 128:(h + 1) * 128], in_=ps.t[:, h * 128:(h + 1) * 128],
                                                         func=AF.Square, accum_out=s4.t[:, h:h + 1]),
                     reads=[ps], writes=[kn, s4])
            S.op("dve", lambda e: e.tensor_scalar(out=s4.t[:, 4:8], in0=s4.t[:, 0:4], scalar1=1.0 / HD, scalar2=EPS,
                                                  op0=ALU.mult, op1=ALU.add), reads=[s4], writes=[s4])
            S.op("pool", lambda e: e.tensor_tensor(out=s4.t[:, 4:8], in0=s4.t[:, 4:8], in1=mhalf.t[:, 0:4], op=ALU.pow),
                 reads=[s4, mhalf], writes=[s4])
            for h in range(4):
                S.op("dve", lambda e, h=h: e.scalar_tensor_tensor(
                    out=kn.t[:, h * 128:(h + 1) * 128], in0=ps.t[:, h * 128:(h + 1) * 128], scalar=s4.t[:, 4 + h:5 + h],
                    in1=g_t.t[:], op0=ALU.mult, op1=ALU.mult), reads=[ps, s4, g_t], writes=[kn])
            transpose4(kn, dst, h0, col0, ktp)

        def transpose4(kn, dst, h0, col0, ktp):
            pb = PSB.next()
            for h in range(4):
                S.op("pe", lambda e, h=h: e.transpose(pb.t[:, h * 128:(h + 1) * 128], kn.t[:, h * 128:(h + 1) * 128],
                                                      ident.t[:]), reads=[kn, ident], writes=[pb])
            kt = ktp.next()
            S.op("act", lambda e: e.copy(out=kt.t[:], in_=pb.t[:, 0:512]), reads=[pb], writes=[kt])
            S.dma("sp", dst[h0:h0 + 4, :, col0:col0 + 128].rearrange("h p t -> p h t"),
                  kt.t[:].rearrange("p (h t) -> p h t", t=128), kt, False)

        def rotary(ph, ps, cs, tabcol, ro_p, kn_p):
            ro = ro_p.next()
            pv = ps.t[:, 0:512].rearrange("p (h two f) -> p h two f", h=4, two=2)
            rv = ro.t[:].rearrange("p (a h f) -> p a h f", a=4, h=4)
            cosb = cs.t[:, 0:64].unsqueeze(1).to_broadcast([128, 4, 64])
            sinb = cs.t[:, 64:128].unsqueeze(1).to_broadcast([128, 4, 64])
            S.op("dve", lambda e: e.tensor_tensor(out=rv[:, 0], in0=pv[:, :, 0, :], in1=cosb, op=ALU.mult), reads=[ps, cs], writes=[ro])
            S.op("dve", lambda e: e.tensor_tensor(out=rv[:, 1], in0=pv[:, :, 1, :], in1=sinb, op=ALU.mult), reads=[ps, cs], writes=[ro])
            S.op("dve", lambda e: e.tensor_tensor(out=rv[:, 2], in0=pv[:, :, 0, :], in1=sinb, op=ALU.mult), reads=[ps, cs], writes=[ro])
            S.op("dve", lambda e: e.tensor_tensor(out=rv[:, 3], in0=pv[:, :, 1, :], in1=cosb, op=ALU.mult), reads=[ps, cs], writes=[ro])
            S.op("pool", lambda e: e.tensor_tensor(out=rv[:, 0], in0=rv[:, 0], in1=rv[:, 1], op=ALU.subtract), reads=[ro], writes=[ro])
            S.op("pool", lambda e: e.tensor_tensor(out=rv[:, 2], in0=rv[:, 2], in1=rv[:, 3], op=ALU.add), reads=[ro], writes=[ro])
            kn = kn_p.next()
            kv = kn.t[:].rearrange("p (h two f) -> p h two f", h=4, two=2)
            tb_ = tabcol.unsqueeze(2).to_broadcast([128, 4, 64])
            S.op("dve", lambda e: e.tensor_tensor(out=kv[:, :, 0, :], in0=rv[:, 0], in1=tb_, op=ALU.mult), reads=[ro, ktab, qtab], writes=[kn])
            S.op("dve", lambda e: e.tensor_tensor(out=kv[:, :, 1, :], in0=rv[:, 2], in1=tb_, op=ALU.mult), reads=[ro, ktab, qtab], writes=[kn])
            return kn

        TT2 = min(NCTX, 11)
        S.cur_phase = "p2"
        P2 = Phase()
        xT2 = P2.sb("xT2", [128, KC, TT2 * 128], BF16)
        wp2 = P2.pool("w", 2, [128, KC, 512], BF16)
        kg_bc = bcast_load(P2, "kg", kg_d, HD)
        bf_bc = bcast_load(P2, "bfb", bf_d, H)
        ss4 = P2.pool("s4", 3, [128, 8], F32)
        knp = P2.pool("kn", 3, [128, 512], BF16)
        ktp = P2.pool("kt", 3, [128, 512], BF16)
        vsp = P2.pool("vs", 3, [128, 4, 129], BF16)
        rop = P2.pool("ro", 2, [128, 1024], F32)
        csp = P2.pool("cs", 3, [128, 128], F32)
        fz = P2.pool("fz", 2, [128, 48], F32)
        for b0 in range(0, NCTX, TT2):
            nb = min(TT2, NCTX - b0)
            load_xT(xT2, uT_d, b0, nb)

            def ev_ka(tb, nt, ps, nw, b0=b0):
                qknorm_T(P2, ps, kg_bc, kT_d, nt * 4, (b0 + tb) * 128, ss4, knp, ktp)

            def ev_va(tb, nt, ps, nw, b0=b0):
                blk = b0 + tb
                vs = vsp.next()
                S.op("dve", lambda e: e.tensor_scalar(out=vs.t[:, :, 0:128], in0=ps.t[:, 0:512].rearrange("p (h f) -> p h f", h=4),
                                                      scalar1=valid.t[:, blk:blk + 1], scalar2=None, op0=ALU.mult),
                     reads=[ps, valid], writes=[vs])
                S.op("pool", lambda e: e.tensor_copy(out=vs.t[:, :, 128:129],
                                                     in_=valid.t[:, blk:blk + 1].unsqueeze(1).to_broadcast([128, 4, 1])),
                     reads=[valid], writes=[vs])
                S.dma("sp", vf_d[blk * 128:(blk + 1) * 128, nt * 4:nt * 4 + 4, :], vs.t[:], vs, False)

            def ev_fa(tb, nt, ps, nw, b0=b0):
                blk = b0 + tb
                z = fz.next()
                S.op("dve", lambda e: e.tensor_tensor(out=z.t[:, 0:16], in0=ps.t[:, 0:16], in1=bf_bc.t[:], op=ALU.add),
                     reads=[ps, bf_bc], writes=[z])
                S.op("act", lambda e: e.activation(out=z.t[:, 16:32], in_=z.t[:, 0:16], func=AF.Exp, scale=-1.0), reads=[z], writes=[z])
                S.op("act", lambda e: e.activation(out=z.t[:, 32:48], in_=z.t[:, 16:32], func=AF.Ln, bias=1.0), reads=[z], writes=[z])
                p2 = PSA.next()
                S.op("pe", lambda e: e.matmul(p2.t[:, 0:16], lhsT=trif.t[:], rhs=z.t[:, 32:48], start=True, stop=True),
                     reads=[trif, z], writes=[p2])
                S.op("pe", lambda e: e.matmul(p2.t[:, 16:32], lhsT=onesf.t[:], rhs=z.t[:, 32:48], start=True, stop=True),
                     reads=[onesf, z], writes=[p2])
                S.op("dve", lambda e: e.tensor_tensor(out=Lfull.t[:, blk, :], in0=p2.t[:, 0:16], in1=Lb.t[:, blk, :], op=ALU.add),
                     reads=[p2, Lb], writes=[Lfull])
                S.op("dve", lambda e: e.tensor_tensor(out=Lb.t[:, blk + 1, :], in0=p2.t[:, 16:32], in1=Lb.t[:, blk, :], op=ALU.add),
                     reads=[p2, Lb], writes=[Lb])

            def ev_kr(tb, nt, ps, nw, b0=b0):
                blk = b0 + tb
                cs = csp.next()
                S.dma("sp", cs.t[:], cs_d[blk * 128:(blk + 1) * 128, :], cs, True)
                kn = rotary(P2, ps, cs, ktab.t[:, blk, nt * 4:nt * 4 + 4], rop, knp)
                if blk < OWN0:
                    S.dma("sp", krw_d[blk * 128:(blk + 1) * 128, nt * 4:nt * 4 + 4, :],
                          kn.t[:].rearrange("p (h f) -> p h f", h=4), kn, False)
                else:
                    transpose4(kn, krT_d, nt * 4, (blk - OWN0) * 128, ktp)

            def ev_vr(tb, nt, ps, nw, b0=b0):
                blk = b0 + tb
                vs = knp.next()
                S.op("act", lambda e: e.activation(out=vs.t[:], in_=ps.t[:, 0:512], func=AF.Copy,
                                                   scale=valid.t[:, blk:blk + 1]), reads=[ps, valid], writes=[vs])
                S.dma("sp", vr_d[blk * 128:(blk + 1) * 128, nt * 512:(nt + 1) * 512], vs.t[:], vs, False)

            kcs = list(range(KC))
            gemm_multi(xT2, nb, kcs, wp2, [
                (w_in[:, c_fa:c_fa + 16], 16, ev_fa),
                (w_in[:, c_ka:c_ka + 2048], 2048, ev_ka),
                (w_in[:, c_va:c_va + 2048], 2048, ev_va),
                (w_in[:, c_kr:c_kr + 2048], 2048, ev_kr),
                (w_in[:, c_vr:c_vr + 4096], 4096, ev_vr)])
        P2.close()

        TT3 = min(NOWN, 8)
        S.cur_phase = "p3"
        P3 = Phase()
        xT3 = P3.sb("xT3", [128, KC, TT3 * 128], BF16)
        wp3 = P3.pool("w", 2, [128, KC, 512], BF16)
        qg_bc = bcast_load(P3, "qg", qg_d, HD, scale=HD ** -0.5)
        ss4 = P3.pool("s4", 3, [128, 8], F32)
        knp = P3.pool("kn", 3, [128, 512], BF16)
        ktp = P3.pool("kt", 3, [128, 512], BF16)
        rop = P3.pool("ro", 2, [128, 1024], F32)
        csp = P3.pool("cs", 3, [128, 128], F32)
        for b0 in range(0, NOWN, TT3):
            nb = min(TT3, NOWN - b0)
            load_xT(xT3, uT_d, OWN0 + b0, nb)

            def ev_qa(tb, nt, ps, nw, b0=b0):
                qknorm_T(P3, ps, qg_bc, qT_d, nt * 4, (b0 + tb) * 128, ss4, knp, ktp)

            def ev_qr(tb, nt, ps, nw, b0=b0):
                blk = b0 + tb
                cs = csp.next()
                S.dma("sp", cs.t[:], cs_d[(OWN0 + blk) * 128:(OWN0 + blk + 1) * 128, :], cs, True)
                kn = rotary(P3, ps, cs, qtab.t[:, blk, nt * 4:nt * 4 + 4], rop, knp)
                transpose4(kn, qrT_d, nt * 4, blk * 128, ktp)

            def ev_act(dst, func):
                def ev(tb, nt, ps, nw, b0=b0):
                    blk = b0 + tb
                    vs = knp.next()
                    S.op("act", lambda e: e.activation(out=vs.t[:], in_=ps.t[:, 0:512], func=func), reads=[ps], writes=[vs])
                    S.dma("sp", dst[blk * 128:(blk + 1) * 128, nt * 512:(nt + 1) * 512], vs.t[:], vs, False)
                return ev
            kcs = list(range(KC))
            gemm_multi(xT3, nb, kcs, wp3, [
                (w_in[:, c_qa:c_qa + 2048], 2048, ev_qa),
                (w_in[:, c_qr:c_qr + 2048], 2048, ev_qr),
                (w_in[:, c_gr:c_gr + 4096], 4096, ev_act(sg_d, AF.Silu)),
                (w_in[:, c_ga:c_ga + D], D, ev_act(ga_d, AF.Sigmoid)),
                (w_in[:, c_gtr:c_gtr + D], D, ev_act(gr_d, AF.Sigmoid))])
        P3.close()

        S.cur_phase = "p4"
        P4 = Phase()
        qTp = P4.pool("qT", 2, [128, TO], BF16)
        kTp = P4.pool("kT", 2, [128, T], BF16)
        vp = P4.pool("v", 2, [128, NCTX, 129], BF16)
        bip = P4.pool("bias", 2, [128, NOWN, NCTX], F32)
        ptp = P4.pool("pt", 6, [128, 128], BF16)
        yap = P4.pool("yas", 2, [128, NOWN, 128], BF16)
        rcp = P4.pool("rc", 2, [128, 1], F32)
        def p4load(h):
            qT = qTp.next(); kT = kTp.next(); vv = vp.next()
            S.dma("sp", qT.t[:], qT_d[h], qT, True)
            S.dma("sp", kT.t[:], kT_d[h], kT, True)
            S.dma("sp", vv.t[:], vf_d[:, h, :].rearrange("(b p) f -> p b f", p=128), vv, True)
            return qT, kT, vv
        nx4 = p4load(0)
        for h in range(H):
            qT, kT, vv = nx4
            if h + 1 < H:
                nx4 = p4load(h + 1)
            yas = yap.next()
            biasT = bip.next()
            for i in range(NOWN):
                gi = OWN0 + i
                S.op("dve", lambda e, i=i, gi=gi: e.tensor_scalar(
                    out=biasT.t[:, i, 0:gi + 1], in0=Lfull.t[:, 0:gi + 1, h], scalar1=Lb.t[:, gi, h:h + 1], scalar2=None,
                    op0=ALU.subtract), reads=[Lfull, Lb], writes=[biasT])
            pairs = [(i, j) for i in range(NOWN) for j in range(OWN0 + i + 1)]
            pss = {}

            def qk(p):
                i, j = pairs[p]
                ps = PSA.next()
                pss[p] = ps
                S.op("pe", lambda e: e.matmul(ps.t[:, 0:128], lhsT=kT.t[:, j * 128:(j + 1) * 128], rhs=qT.t[:, i * 128:(i + 1) * 128],
                                              start=True, stop=True), reads=[kT, qT], writes=[ps])
            LA = 3
            for p in range(min(LA, len(pairs))):
                qk(p)
            acc = None
            for p, (i, j) in enumerate(pairs):
                if p + LA < len(pairs):
                    qk(p + LA)
                gi = OWN0 + i
                if j == 0:
                    acc = PSACC.next()
                ps = pss.pop(p)
                pt = ptp.next()
                S.op("act", lambda e: e.activation(out=pt.t[:], in_=ps.t[:, 0:128], func=AF.Exp, bias=biasT.t[:, i, j:j + 1], scale=1.0),
                     reads=[ps, biasT], writes=[pt])
                if j == gi:
                    S.op("pool", lambda e: e.tensor_tensor(out=pt.t[:], in0=pt.t[:], in1=tri.t[:], op=ALU.mult),
                         reads=[pt, tri], writes=[pt])
                S.op("pe", lambda e: e.matmul(acc.t[:, 0:129], lhsT=pt.t[:], rhs=vv.t[:, j, :], start=(j == 0), stop=(j == gi)),
                     reads=[pt, vv], writes=[acc])
                if j == gi:
                    rc = rcp.next()
                    S.op("dve", lambda e: e.reciprocal(out=rc.t[:], in_=acc.t[:, 128:129]), reads=[acc], writes=[rc])
                    S.op("dve", lambda e: e.tensor_scalar(out=yas.t[:, i, :], in0=acc.t[:, 0:128], scalar1=rc.t[:, 0:1], scalar2=None,
                                                          op0=ALU.mult), reads=[acc, rc], writes=[yas])
            S.dma("sp", ya_d[:, h * 128:(h + 1) * 128].rearrange("(b p) f -> p b f", p=128), yas.t[:], yas, False)
        P4.close()

        S.cur_phase = "p5"
        P5 = Phase()
        rg_bc = bcast_load(P5, "rg", rg_d, H * DV)
        NPV = max(OWN0, 1)
        krwp = P5.pool("krw", 2, [128, NPV, 128], BF16)
        vrp = P5.pool("vr", 2, [128, NCTX, DV], BF16)
        qrp = P5.pool("qrT", 2, [128, TO], BF16)
        krp = P5.pool("krT", 2, [128, TO], BF16)
        sgp = P5.pool("sg", 2, [128, NOWN, DV], BF16)
        stp = P5.pool("st", 2, [128, DV], BF16)
        ptp = P5.pool("pt", 6, [128, 128], BF16)
        yrp = P5.pool("yrs", 1, [128, NOWN, DV], BF16)
        smp = P5.pool("sm", 3, [128, 8], F32)
        jkp = P5.pool("jk", 2, [128, DV], F32)
        onp = P5.pool("on", 2, [128, DV], F32)
        def p5load(h):
            krw = krwp.next(); vr = vrp.next(); qr = qrp.next(); kr = krp.next(); sg = sgp.next()
            S.dma("sp", krw.t[:, 0:OWN0, :], krw_d[0:OWN0 * 128, h, :].rearrange("(b p) f -> p b f", p=128), krw, True)
            S.dma("sp", vr.t[:], vr_d[:, h * DV:(h + 1) * DV].rearrange("(b p) f -> p b f", p=128), vr, True)
            S.dma("sp", qr.t[:], qrT_d[h], qr, True)
            S.dma("sp", kr.t[:], krT_d[h], kr, True)
            S.dma("sp", sg.t[:], sg_d[:, h * DV:(h + 1) * DV].rearrange("(b p) f -> p b f", p=128), sg, True)
            return krw, vr, qr, kr, sg
        nx5 = p5load(0)
        for h in range(H):
            gam = 1.0 - 2.0 ** (-5.0 - h)
            krw, vr, qr, kr, sg = nx5
            if h + 1 < H:
                nx5 = p5load(h + 1)
            yrs = yrp.next()
            sp_ = PSA.next()
            for j in range(OWN0):
                S.op("pe", lambda e, j=j: e.matmul(sp_.t[:, 0:DV], lhsT=krw.t[:, j, :], rhs=vr.t[:, j, :],
                                                   start=(j == 0), stop=(j == OWN0 - 1)), reads=[krw, vr], writes=[sp_])
            stt = stp.next()
            S.op("act", lambda e, stt=stt, sp_=sp_, gam=gam: e.activation(out=stt.t[:], in_=sp_.t[:, 0:DV], func=AF.Copy, scale=float(gam)),
                 reads=[sp_], writes=[stt])
            if dbg:
                S.dma("sp", stt_d[h], stt.t[:], stt, False)
            pairs = [(i, j) for i in range(NOWN) for j in range(i + 1)]
            pss = {}

            def sk(p):
                i, j = pairs[p]
                ps = PSA.next()
                pss[p] = ps
                S.op("pe", lambda e: e.matmul(ps.t[:, 0:128], lhsT=kr.t[:, j * 128:(j + 1) * 128], rhs=qr.t[:, i * 128:(i + 1) * 128],
                                              start=True, stop=True), reads=[kr, qr], writes=[ps])
            LA = 3
            for p in range(min(LA, len(pairs))):
                sk(p)
            acc = None
            for p, (i, j) in enumerate(pairs):
                if p + LA < len(pairs):
                    sk(p + LA)
                if j == 0:
                    acc = PSACC.next()
                    S.op("pe", lambda e: e.matmul(acc.t[:, 0:DV], lhsT=qr.t[:, i * 128:(i + 1) * 128], rhs=stt.t[:],
                                                  start=True, stop=False), reads=[qr, stt], writes=[acc])
                ps = pss.pop(p)
                pt = ptp.next()
                if j == i:
                    S.op("dve", lambda e: e.tensor_tensor(out=pt.t[:], in0=ps.t[:, 0:128], in1=tri.t[:], op=ALU.mult),
                         reads=[ps, tri], writes=[pt])
                else:
                    S.op("act", lambda e: e.copy(out=pt.t[:], in_=ps.t[:, 0:128]), reads=[ps], writes=[pt])
                S.op("pe", lambda e: e.matmul(acc.t[:, 0:DV], lhsT=pt.t[:], rhs=vr.t[:, OWN0 + j, :], start=False, stop=(j == i)),
                     reads=[pt, vr], writes=[acc])
                if j != i:
                    continue
                sm = smp.next(); jk = jkp.next(); on = onp.next()
                S.op("pool", lambda e, sm=sm: e.memset(sm.t[:], 0.0), writes=[sm])
                S.op("act", lambda e, acc=acc, jk=jk, sm=sm: e.activation(out=jk.t[:], in_=acc.t[:, 0:DV], func=AF.Copy,
                                                                         accum_out=sm.t[:, 0:1]), reads=[acc], writes=[jk, sm])
                if dbg:
                    S.dma("sp", oraw_d[h, i * 128:(i + 1) * 128, :], jk.t[:], jk, False)
                S.op("act", lambda e, acc=acc, jk=jk, sm=sm: e.activation(out=jk.t[:], in_=acc.t[:, 0:DV], func=AF.Square,
                                                                         accum_out=sm.t[:, 1:2]), reads=[acc], writes=[jk, sm])
                S.op("dve", lambda e, sm=sm: e.tensor_scalar(out=sm.t[:, 2:4], in0=sm.t[:, 0:2], scalar1=1.0 / DV, scalar2=None, op0=ALU.mult),
                     reads=[sm], writes=[sm])
                S.op("dve", lambda e, sm=sm: e.tensor_tensor(out=sm.t[:, 4:5], in0=sm.t[:, 2:3], in1=sm.t[:, 2:3], op=ALU.mult),
                     reads=[sm], writes=[sm])
                S.op("dve", lambda e, sm=sm: e.tensor_tensor(out=sm.t[:, 5:6], in0=sm.t[:, 3:4], in1=sm.t[:, 4:5], op=ALU.subtract),
                     reads=[sm], writes=[sm])
                S.op("dve", lambda e, sm=sm: e.tensor_scalar(out=sm.t[:, 6:7], in0=sm.t[:, 5:6], scalar1=EPS, scalar2=None,
                                                              op0=ALU.add), reads=[sm], writes=[sm])
                S.op("pool", lambda e, sm=sm: e.tensor_tensor(out=sm.t[:, 6:7], in0=sm.t[:, 6:7], in1=mhalf.t[:, 0:1], op=ALU.pow),
                     reads=[sm, mhalf], writes=[sm])
                S.op("dve", lambda e, sm=sm, acc=acc, on=on: e.tensor_scalar(
                    out=on.t[:], in0=acc.t[:, 0:DV], scalar1=sm.t[:, 2:3], scalar2=sm.t[:, 6:7], op0=ALU.subtract, op1=ALU.mult),
                    reads=[acc, sm], writes=[on])
                S.op("pool", lambda e, on=on, h=h: e.tensor_tensor(out=on.t[:], in0=on.t[:], in1=rg_bc.t[:, h * DV:(h + 1) * DV], op=ALU.mult),
                     reads=[on, rg_bc], writes=[on])
                S.op("pool", lambda e, on=on, sg=sg, yrs=yrs, i=i: e.tensor_tensor(out=yrs.t[:, i, :], in0=on.t[:], in1=sg.t[:, i, :], op=ALU.mult),
                     reads=[on, sg], writes=[yrs])
            S.dma("sp", yr_d[:, h * DV:(h + 1) * DV].rearrange("(b p) f -> p b f", p=128), yrs.t[:], yrs, False)
        P5.close()
        PA.close()

        TT6 = min(NOWN, 4)
        KA = H * HD // 128
        KR = H * DV // 128
        S.cur_phase = "p6"
        P6 = Phase()
        yT = P6.sb("yT", [128, KA + KR, TT6 * 128], BF16)
        mbf = P6.sb("mbf", [128, TT6, D], BF16)
        mT = yT
        wp6 = P6.pool("w", 2, [128, 32, 512], BF16)
        wap6 = P6.pool("wa", 2, [128, KA, 512], BF16)
        ysp = P6.pool("ys", 2, [128, 2048], BF16)
        gtp = P6.pool("gt", 4, [128, 512], BF16)
        xp6 = P6.pool("x6", 2, [128, 512], F32)
        tp6 = P6.pool("t6", 3, [128, 512], F32)
        for b0 in range(0, NOWN, TT6):
            nb = min(TT6, NOWN - b0)
            for tb in range(nb):
                blk = b0 + tb
                for part in range((KA + KR) // 16):
                    ys = ysp.next()
                    if part == 0:
                        S.dma("sp", ys.t[:], ya_d[blk * 128:(blk + 1) * 128, :], ys, True)
                    else:
                        S.dma("sp", ys.t[:], yr_d[blk * 128:(blk + 1) * 128, (part - 1) * 2048:part * 2048], ys, True)
                    for k8 in range(0, 16, 8):
                        pb = PSB.next()
                        for j in range(8):
                            S.op("pe", lambda e: e.transpose(pb.t[:, j * 128:(j + 1) * 128], ys.t[:, (k8 + j) * 128:(k8 + j + 1) * 128], ident.t[:]),
                                 reads=[ys, ident], writes=[pb])
                        kk0 = part * 16 + k8
                        if (k8 // 8) % 2 == 0:
                            S.op("act", lambda e: e.copy(out=yT.t[:, kk0:kk0 + 8, tb * 128:(tb + 1) * 128],
                                                         in_=pb.t[:].rearrange("p (k t) -> p k t", t=128)), reads=[pb], writes=[yT])
                        else:
                            S.op("dve", lambda e: e.tensor_copy(out=yT.t[:, kk0:kk0 + 8, tb * 128:(tb + 1) * 128],
                                                                in_=pb.t[:].rearrange("p (k t) -> p k t", t=128)), reads=[pb], writes=[yT])
            def p6load(nt):
                wa = wap6.next(); wr = wp6.next()
                for k4 in range(0, KA, 8):
                    S.dma("pool", wa.t[:, k4:k4 + 8, :], wpf[k4 * 128:(k4 + 8) * 128, nt * 512:(nt + 1) * 512].rearrange("(kc p) n -> p kc n", p=128), wa, True)
                for k4 in range(0, KR, 8):
                    S.dma("pool", wr.t[:, k4:k4 + 8, :], wpr[k4 * 128:(k4 + 8) * 128, nt * 512:(nt + 1) * 512].rearrange("(kc p) n -> p kc n", p=128), wr, True)
                return wa, wr
            nx6 = p6load(0)
            for nt in range(D // 512):
                wa, wr = nx6
                if nt + 1 < D // 512:
                    nx6 = p6load(nt + 1)
                for tb in range(nb):
                    blk = b0 + tb
                    pa = PSA.next(); pr = PSA.next()
                    for kc in range(KA):
                        S.op("pe", lambda e, pa=pa, wa=wa, kc=kc, tb=tb: e.matmul(pa.t[:], lhsT=yT.t[:, kc, tb * 128:(tb + 1) * 128],
                                                                                 rhs=wa.t[:, kc, :], start=(kc == 0), stop=(kc == KA - 1)),
                             reads=[yT, wa], writes=[pa])
                    for kc in range(KR):
                        S.op("pe", lambda e, pr=pr, wr=wr, kc=kc, tb=tb: e.matmul(pr.t[:], lhsT=yT.t[:, KA + kc, tb * 128:(tb + 1) * 128],
                                                                                 rhs=wr.t[:, kc, :], start=(kc == 0), stop=(kc == KR - 1)),
                             reads=[yT, wr], writes=[pr])
                    g1 = gtp.next(); g2 = gtp.next(); t1 = tp6.next(); t2 = tp6.next()
                    S.dma("sp", g1.t[:], ga_d[blk * 128:(blk + 1) * 128, nt * 512:(nt + 1) * 512], g1, True)
                    S.dma("sp", g2.t[:], gr_d[blk * 128:(blk + 1) * 128, nt * 512:(nt + 1) * 512], g2, True)
                    S.op("dve", lambda e, t1=t1, pa=pa, g1=g1: e.tensor_tensor(out=t1.t[:], in0=pa.t[:], in1=g1.t[:], op=ALU.mult), reads=[pa, g1], writes=[t1])
                    S.op("dve", lambda e, t2=t2, pr=pr, g2=g2: e.tensor_tensor(out=t2.t[:], in0=pr.t[:], in1=g2.t[:], op=ALU.mult), reads=[pr, g2], writes=[t2])
                    S.op("dve", lambda e, t1=t1, t2=t2, tb=tb, nt=nt: e.tensor_tensor(out=mbf.t[:, tb, nt * 512:(nt + 1) * 512], in0=t1.t[:], in1=t2.t[:], op=ALU.add),
                         reads=[t1, t2], writes=[mbf])
            for tb in range(nb):
                for k8 in range(0, KC, 8):
                    pb = PSB.next()
                    n8 = min(8, KC - k8)
                    for j in range(n8):
                        S.op("pe", lambda e, pb=pb, j=j, k8=k8, tb=tb: e.transpose(
                            pb.t[:, j * 128:(j + 1) * 128], mbf.t[:, tb, (k8 + j) * 128:(k8 + j + 1) * 128], ident.t[:]),
                            reads=[mbf, ident], writes=[pb])
                    S.op("act", lambda e, pb=pb, k8=k8, n8=n8, tb=tb: e.copy(out=mT.t[:, k8:k8 + n8, tb * 128:(tb + 1) * 128],
                                                                            in_=pb.t[:, 0:n8 * 128].rearrange("p (k t) -> p k t", t=128)),
                         reads=[pb], writes=[mT])

            def ev_out(tb, nt, ps, nw, b0=b0):
                blk = b0 + tb
                xt = xp6.next()
                S.dma("sp", xt.t[:], ctx_x[(OWN0 + blk) * 128:(OWN0 + blk + 1) * 128, nt * 512:(nt + 1) * 512], xt, True)
                S.op("dve", lambda e: e.tensor_tensor(out=xt.t[:], in0=ps.t[:], in1=xt.t[:], op=ALU.add), reads=[ps, xt], writes=[xt])
                S.dma("sp", h1_d[blk * 128:(blk + 1) * 128, nt * 512:(nt + 1) * 512], xt.t[:], xt, False)
            gemm(mT, nb, list(range(KC)), wout, D, wp6, ev_out)
        P6.close()

        S.cur_phase = "p7"
        P7 = Phase()
        gf_bc = bcast_load(P7, "gffn", gffn_d, D)
        rmsnorm_T(P7, NOWN, lambda blk: h1_d[blk * 128:(blk + 1) * 128, :], gf_bc, xnT_d,
                  P7.pool("x", 3, [128, D], F32), P7.pool("jk", 1, [128, D], BF16),
                  P7.pool("xn", 2, [128, D], BF16), P7.pool("uo", 2, [128, KC, 128], BF16))
        P7.close()

        TT7 = min(NOWN, 8)
        S.cur_phase = "p7b"
        P7b = Phase()
        xT7 = P7b.sb("xT7", [128, KC, TT7 * 128], BF16)
        wp7 = P7b.pool("w", 2, [128, KC, 512], BF16)
        knp = P7b.pool("kn", 3, [128, 512], BF16)
        ktp = P7b.pool("kt", 3, [128, 512], BF16)
        for b0 in range(0, NOWN, TT7):
            nb = min(TT7, NOWN - b0)
            load_xT(xT7, xnT_d, b0, nb)

            def ev_q(tb, nt, ps, nw, b0=b0):
                kn = knp.next()
                S.op("act", lambda e: e.copy(out=kn.t[:], in_=ps.t[:, 0:512]), reads=[ps], writes=[kn])
                transpose4(kn, qpT_d, nt * 4, (b0 + tb) * 128, ktp)
            gemm(xT7, nb, list(range(KC)), wq, PH * 256, wp7, ev_q)
        P7b.close()

        S.cur_phase = "p7c"
        P7c = Phase()
        kk = P7c.sb("kk", [128, 2 * PH, NK], BF16)
        for hh in range(PH):
            S.dma("pool", kk.t[:, 2 * hh, :], k1t[hh], kk, True)
            S.dma("pool", kk.t[:, 2 * hh + 1, :], k2t[hh], kk, True)
        qpp = P7c.pool("qp", 2, [128, 2 * PH, 128], BF16)
        scp = P7c.pool("sc", 2, [128, 2 * PH, NK], F32)
        tmpp = P7c.pool("tmp", 2, [128, 256], F32)
        t16p = P7c.pool("t16", 1, [128, 2 * PH, NK], F32)
        t2p = P7c.pool("t2", 1, [128, PH, 256], F32)
        v12p = P7c.pool("v12", 2, [128, 2 * PH, 16], F32)
        idxp = P7c.pool("idx", 2, [128, PH, 16], U32)
        idfp = P7c.pool("idf", 2, [128, PH * 16], F32)
        candp = P7c.pool("cand", 1, [128, PH, 256], F32)
        tvp = P7c.pool("tv", 2, [128, PH, 16], F32)
        smp = P7c.pool("sm", 2, [128, 4, PH], F32)
        pp_ = P7c.pool("pp", 2, [128, 16, NK], F32)
        ep_ = P7c.pool("ep", 2, [128, 16, NK], F32)
        Rp = P7c.pool("R", 1, [128, 128, NK], BF16)
        Rtp = P7c.pool("Rt", 1, [128, 128, NK], BF16)
        OHp = P7c.pool("OH", 1, [128, 128, NK], BF16)
        itp = P7c.pool("it", 2, [128, 128], F32)
        for blk in range(NOWN):
            qp = qpp.next()
            S.dma("sp", qp.t[:], qpT_d[:, :, blk * 128:(blk + 1) * 128].rearrange("c p t -> p c t"), qp, True)
            sc = scp.next()
            for c4 in range(0, 2 * PH, 4):
                ps = PSA.next()
                for j in range(4):
                    S.op("pe", lambda e, ps=ps, qp=qp, c4=c4, j=j: e.matmul(
                        ps.t[:, j * 128:(j + 1) * 128], lhsT=qp.t[:, c4 + j, :], rhs=kk.t[:, c4 + j, :], start=True, stop=True),
                        reads=[qp, kk], writes=[ps])
                S.op("act", lambda e, ps=ps, sc=sc, c4=c4: e.copy(out=sc.t[:, c4:c4 + 4, :], in_=ps.t[:].rearrange("p (c n) -> p c n", n=NK)),
                     reads=[ps], writes=[sc])
            v12 = v12p.next(); idx = idxp.next(); t16 = t16p.next(); t2 = t2p.next()
            v12s = v12.subs(2 * PH); idxs = idx.subs(PH); t16s = t16.subs(2 * PH); t2s = t2.subs(PH)
            for c in range(2 * PH):
                S.op("dve", lambda e: e.max(out=v12.t[:, c, 0:8], in_=sc.t[:, c, :]), reads=[sc], writes=[v12s[c]])
            for c in range(2 * PH):
                S.op("dve", lambda e: e.match_replace(out=t16.t[:, c, :], in_to_replace=v12.t[:, c, 0:8], in_values=sc.t[:, c, :],
                                                      imm_value=-1e30), reads=[sc, v12s[c]], writes=[t16s[c]])
            for c in range(2 * PH):
                S.op("dve", lambda e: e.max(out=v12.t[:, c, 8:16], in_=t16.t[:, c, :]), reads=[t16s[c]], writes=[v12s[c]])
            for hh in range(PH):
                c = 2 * hh
                S.op("dve", lambda e: e.max_index(out=idx.t[:, hh, 0:8], in_max=v12.t[:, c, 0:8], in_values=sc.t[:, c, :]),
                     reads=[sc, v12s[c]], writes=[idxs[hh]])
            for hh in range(PH):
                c = 2 * hh
                S.op("dve", lambda e: e.max_index(out=idx.t[:, hh, 8:16], in_max=v12.t[:, c, 8:16], in_values=t16.t[:, c, :]),
                     reads=[t16s[c], v12s[c]], writes=[idxs[hh]])
            cand = candp.next(); tv = tvp.next(); sm = smp.next()
            tvs = tv.subs(PH)
            vv = v12.t[:].rearrange("p (h two) k -> p h two k", two=2)
            S.op("pool", lambda e: e.tensor_tensor(
                out=cand.t[:].rearrange("p h (a b) -> p h a b", a=16),
                in0=vv[:, :, 0, :].unsqueeze(3).to_broadcast([128, PH, 16, 16]),
                in1=vv[:, :, 1, :].unsqueeze(2).to_broadcast([128, PH, 16, 16]), op=ALU.add), reads=v12s, writes=[cand])
            for hh in range(PH):
                S.op("dve", lambda e: e.max(out=tv.t[:, hh, 0:8], in_=cand.t[:, hh, :]), reads=[cand], writes=[tvs[hh]])
            for hh in range(PH):
                S.op("dve", lambda e: e.match_replace(out=t2.t[:, hh, :], in_to_replace=tv.t[:, hh, 0:8], in_values=cand.t[:, hh, :],
                                                      imm_value=-1e30), reads=[cand, tvs[hh]], writes=[t2s[hh]])
            for hh in range(PH):
                S.op("dve", lambda e: e.max(out=tv.t[:, hh, 8:16], in_=t2.t[:, hh, :]), reads=[t2s[hh]], writes=[tvs[hh]])
            S.op("dve", lambda e, sm=sm, tv=tv: e.tensor_scalar(out=sm.t[:, 0, :], in0=tv.t[:, :, 0], scalar1=-1.0, scalar2=None, op0=ALU.mult),
                 reads=tvs, writes=[sm])
            S.op("dve", lambda e, sm=sm, tv=tv: e.tensor_copy(out=sm.t[:, 3, :], in_=tv.t[:, :, 15]), reads=tvs, writes=[sm])
            ex = tmpp.next()
            S.op("dve", lambda e, sm=sm, tv=tv, ex=ex: e.tensor_tensor(
                out=ex.t[:, 0:PH * 16].rearrange("p (h k) -> p h k", k=16), in0=tv.t[:],
                in1=sm.t[:, 0, :].unsqueeze(2).to_broadcast([128, PH, 16]), op=ALU.add), reads=tvs + [sm], writes=[ex])
            S.op("act", lambda e, ex=ex: e.activation(out=ex.t[:, 0:PH * 16], in_=ex.t[:, 0:PH * 16], func=AF.Exp), reads=[ex], writes=[ex])
            S.op("dve", lambda e, sm=sm, ex=ex: e.tensor_reduce(out=sm.t[:, 1, :], in_=ex.t[:, 0:PH * 16].rearrange("p (h k) -> p h k", k=16),
                                                               axis=AX.X, op=ALU.add), reads=[ex], writes=[sm])
            S.op("act", lambda e, sm=sm: e.activation(out=sm.t[:, 2, :], in_=sm.t[:, 1, :], func=AF.Ln), reads=[sm], writes=[sm])
            S.op("dve", lambda e, sm=sm: e.tensor_tensor(out=sm.t[:, 2, :], in0=sm.t[:, 0, :], in1=sm.t[:, 2, :], op=ALU.subtract),
                 reads=[sm], writes=[sm])
            R = Rp.next()
            Gs = R.subs(32)
            for hh in range(PH):
                pp = pp_.next(); ep = ep_.next()
                S.op("pool", lambda e, pp=pp, hh=hh: e.tensor_tensor(
                    out=pp.t[:], in0=sc.t[:, 2 * hh + 1, :].unsqueeze(1).to_broadcast([128, 16, NK]),
                    in1=v12.t[:, 2 * hh, :].unsqueeze(2).to_broadcast([128, 16, NK]), op=ALU.add), reads=[sc, v12s[2 * hh]], writes=[pp])
                S.op("act", lambda e, pp=pp, ep=ep, hh=hh, sm=sm: e.activation(out=ep.t[:], in_=pp.t[:], func=AF.Exp,
                                                                              bias=sm.t[:, 2, hh:hh + 1], scale=1.0),
                     reads=[pp, sm], writes=[ep])
                S.op("dve", lambda e, pp=pp, ep=ep, hh=hh, sm=sm, R=R: e.scalar_tensor_tensor(
                    out=R.t[:, hh * 16:(hh + 1) * 16, :], in0=pp.t[:], scalar=sm.t[:, 3, hh:hh + 1], in1=ep.t[:],
                    op0=ALU.is_ge, op1=ALU.mult), reads=[pp, ep, sm], writes=Gs)
            Rt = Rtp.next()
            Rts = Rt.subs(NK // 8)
            for i8 in range(0, NK, 8):
                pb = PSB.next()
                for j in range(8):
                    S.op("pe", lambda e, pb=pb, R=R, i8=i8, j=j: e.transpose(pb.t[:, j * 128:(j + 1) * 128], R.t[:, :, i8 + j], ident.t[:]),
                         reads=Gs + [ident], writes=[pb])
                eng = "act" if (i8 // 8) % 2 == 0 else "dve"
                outv = Rt.t[:, :, i8:i8 + 8].rearrange("p t i -> p i t")
                if eng == "act":
                    S.op("act", lambda e, pb=pb, outv=outv: e.copy(out=outv, in_=pb.t[:].rearrange("p (i t) -> p i t", t=128)), reads=[pb], writes=[Rts[i8 // 8]])
                else:
                    S.op("dve", lambda e, pb=pb, outv=outv: e.tensor_copy(out=outv, in_=pb.t[:].rearrange("p (i t) -> p i t", t=128)), reads=[pb], writes=[Rts[i8 // 8]])
            idf = idfp.next()
            S.op("dve", lambda e, idf=idf, idx=idx: e.tensor_copy(out=idf.t[:], in_=idx.t[:].rearrange("p h k -> p (h k)")), reads=idxs, writes=[idf])
            pt_ = PSA.next()
            S.op("pe", lambda e, pt_=pt_, idf=idf: e.transpose(pt_.t[:, 0:128], idf.t[:], identf.t[:]), reads=[idf, identf], writes=[pt_])
            it = itp.next()
            S.op("act", lambda e, it=it, pt_=pt_: e.copy(out=it.t[:], in_=pt_.t[:, 0:128]), reads=[pt_], writes=[it])
            OH = OHp.next()
            S.op("dve", lambda e, OH=OH, it=it: e.tensor_tensor(
                out=OH.t[:], in0=iotar.t[:].unsqueeze(1).to_broadcast([128, 128, NK]),
                in1=it.t[:].unsqueeze(2).to_broadcast([128, 128, NK]), op=ALU.is_equal), reads=[iotar, it], writes=[OH])
            G = R
            for t4 in range(0, 128, 4):
                ps = PSA.next()
                for j in range(4):
                    S.op("pe", lambda e, ps=ps, t4=t4, j=j: e.matmul(ps.t[:, j * 128:(j + 1) * 128], lhsT=Rt.t[:, t4 + j, :], rhs=OH.t[:, t4 + j, :],
                                                                      start=True, stop=True), reads=Rts + [OH], writes=[ps])
                outv = G.t[:, :, t4:t4 + 4].rearrange("p i t -> p t i")
                if (t4 // 4) % 2 == 0:
                    S.op("act", lambda e, ps=ps, outv=outv: e.copy(out=outv, in_=ps.t[:].rearrange("p (t i) -> p t i", i=NK)), reads=[ps], writes=[Gs[t4 // 4]])
                else:
                    S.op("dve", lambda e, ps=ps, outv=outv: e.tensor_copy(out=outv, in_=ps.t[:].rearrange("p (t i) -> p t i", i=NK)), reads=[ps], writes=[Gs[t4 // 4]])
            for i16 in range(0, NK, 16):
                S.dma("sp", G_d[i16:i16 + 16, :, blk * 128:(blk + 1) * 128].rearrange("i p t -> p i t"), G.t[:, i16:i16 + 16, :], G, False, deps=Gs)
        P7c.close()

        TT8 = min(NOWN, 4)
        S.cur_phase = "p8"
        P8 = Phase()
        xT8 = P8.sb("xT8", [128, KC, TT8 * 128], BF16)
        yacc = P8.sb("yacc", [128, TT8, D], F32)
        utp = P8.pool("ut", 2, [128, KC, 256], BF16)
        vtp = P8.pool("vt", 4, [128, D], BF16)
        gp8 = P8.pool("g8", 4, [128, TT8 * 128], BF16)
        gep = P8.pool("ge", 3, [128, TT8 * 128], F32)
        atp = P8.pool("at", 4, [128, TT8 * 128], BF16)
        for b0 in range(0, NOWN, TT8):
            nb = min(TT8, NOWN - b0)
            ntok = nb * 128
            load_xT(xT8, xnT_d, b0, nb)
            ND8 = D // 512
            yss = yacc.subs(TT8 * ND8)
            for tb in range(nb):
                S.dma("sp", yacc.t[:, tb, :], h1_d[(b0 + tb) * 128:(b0 + tb + 1) * 128, :], yacc, True, deps=yss[tb * ND8:(tb + 1) * ND8])
            def p8load(grp):
                ut = utp.next()
                for k4 in range(0, KC, 8):
                    k5 = min(KC, k4 + 8)
                    S.dma("pool", ut.t[:, k4:k5, :], ut_d[grp, :, k4:k5, :], ut, True)
                vts_ = []
                g8s_ = []
                for cl in range(2):
                    c = grp * 2 + cl
                    vt = vtp.next()
                    for d4 in range(0, D, 2048):
                        d5 = min(D, d4 + 2048)
                        S.dma("pool", vt.t[:, d4:d5], pv_d[c * 128:(c + 1) * 128, d4:d5], vt, True)
                    g8 = gp8.next()
                    S.dma("sp", g8.t[:, 0:ntok], G_d[c, :, b0 * 128:b0 * 128 + ntok], g8, True)
                    vts_.append(vt); g8s_.append(g8)
                return ut, vts_, g8s_
            nx8 = p8load(0)
            for grp in range(NE // 256):
                ut, vts, g8s = nx8
                if grp + 1 < NE // 256:
                    nx8 = p8load(grp + 1)
                ats = []
                for cl in range(2):
                    g8 = g8s[cl]
                    ps = PSA.next()
                    for kc in range(KC):
                        S.op("pe", lambda e, ps=ps, ut=ut, kc=kc, cl=cl, ntok=ntok: e.matmul(
                            ps.t[:, 0:ntok], lhsT=ut.t[:, kc, cl * 128:(cl + 1) * 128], rhs=xT8.t[:, kc, 0:ntok],
                            start=(kc == 0), stop=(kc == KC - 1)), reads=[ut, xT8], writes=[ps])
                    ge = gep.next(); at = atp.next()
                    S.op("act", lambda e, ps=ps, ge=ge, ntok=ntok: e.activation(out=ge.t[:, 0:ntok], in_=ps.t[:, 0:ntok], func=AF.Gelu),
                         reads=[ps], writes=[ge])
                    S.op("dve", lambda e, ge=ge, g8=g8, at=at, ntok=ntok: e.tensor_tensor(out=at.t[:, 0:ntok], in0=ge.t[:, 0:ntok], in1=g8.t[:, 0:ntok], op=ALU.mult),
                         reads=[ge, g8], writes=[at])
                    ats.append(at)
                for tb in range(nb):
                    for dt in range(D // 512):
                        ps = PSA.next()
                        for cl in range(2):
                            S.op("pe", lambda e, ps=ps, cl=cl, tb=tb, dt=dt, at=ats[cl], vt=vts[cl]: e.matmul(
                                ps.t[:], lhsT=at.t[:, tb * 128:(tb + 1) * 128], rhs=vt.t[:, dt * 512:(dt + 1) * 512],
                                start=(cl == 0), stop=(cl == 1)), reads=[ats[cl], vts[cl]], writes=[ps])
                        S.op("dve", lambda e, ps=ps, tb=tb, dt=dt: e.tensor_tensor(
                            out=yacc.t[:, tb, dt * 512:(dt + 1) * 512], in0=ps.t[:], in1=yacc.t[:, tb, dt * 512:(dt + 1) * 512], op=ALU.add),
                            reads=[ps, yss[tb * ND8 + dt]], writes=[yss[tb * ND8 + dt]])
            for tb in range(nb):
                S.dma("sp", out_d[(b0 + tb) * 128:(b0 + tb + 1) * 128, :], yacc.t[:, tb, :], yacc, False, deps=yss[tb * ND8:(tb + 1) * ND8])
        P8.close()
        P0.close()
        S.emit()
    return nc


_CACHE = {}


def kernel(x, meta_tokens, norm_mix_g, w_in, b_forget, q_norm_g, k_norm_g, ret_norm_g, w_proj_fox, w_proj_ret,
           w_out, norm_ffn_g, peer_w_q, peer_keys_1, peer_keys_2, peer_u, peer_v, _dbg=False):
    f = lambda a: np.ascontiguousarray(np.asarray(a, dtype=np.float32))
    x = f(x)
    B, SEQ, D = x.shape
    NB = SEQ // 128
    NOWN = NB // 4
    NCTX = 1 + NB
    T = NCTX * 128
    KC = D // 128
    key = (D, NCTX, NOWN)
    key = (D, NCTX, NOWN, _dbg)
    if key not in _CACHE:
        _CACHE[key] = build(D, NCTX, NOWN, _dbg)
    nc = _CACHE[key]
    meta = f(meta_tokens)
    pu = f(peer_u)[0]
    NE = pu.shape[0]
    ut = np.ascontiguousarray(pu.reshape(NE // 256, 256, KC, 128).transpose(0, 3, 2, 1))
    shared = {
        "norm_mix_g": f(norm_mix_g)[0], "w_in": f(w_in)[0], "b_forget": f(b_forget)[0], "q_norm_g": f(q_norm_g)[0],
        "k_norm_g": f(k_norm_g)[0], "ret_norm_g": f(ret_norm_g)[0], "w_proj_fox": f(w_proj_fox)[0],
        "w_proj_ret": f(w_proj_ret)[0], "w_out": f(w_out)[0], "norm_ffn_g": f(norm_ffn_g)[0],
        "peer_w_q": f(peer_w_q)[0],
        "k1t": np.ascontiguousarray(f(peer_keys_1)[0].transpose(0, 2, 1)),
        "k2t": np.ascontiguousarray(f(peer_keys_2)[0].transpose(0, 2, 1)),
        "ut": ut, "peer_v": f(peer_v)[0],
    }
    gam = 1.0 - 2.0 ** (-5.0 - np.arange(H, dtype=np.float64))
    inv = ROPE_BASE ** (-np.arange(64, dtype=np.float64) / 64)
    in_maps = []
    for c in range(8):
        b, g = c // 4, c % 4
        ndum = (3 - g) * NOWN * 128
        nprev = g * NOWN * 128
        ctx = np.zeros((T, D), np.float32)
        ctx[ndum + PAD:ndum + 128] = meta
        ctx[ndum + 128:ndum + 128 + nprev] = x[b, :nprev]
        ctx[T - NOWN * 128:] = x[b, nprev:nprev + NOWN * 128]
        n = np.arange(T) - ndum
        valid = (n >= PAD).astype(np.float32)
        pos = (n - PAD).astype(np.float64)
        ang = pos[:, None] * inv[None, :]
        cossin = np.concatenate([np.cos(ang), np.sin(ang)], axis=1).astype(np.float32)
        start = 128 + nprev
        lpos = (n - start).astype(np.float64)
        own = n >= start
        ktab = np.zeros((T, H), np.float64)
        with np.errstate(over="ignore", under="ignore"):
            ktab[~own] = np.exp(np.log(gam)[None, :] * (start - 1 - n[~own])[:, None])
            ktab[own] = np.exp(-np.log(gam)[None, :] * lpos[own][:, None])
            qtab = np.exp(np.log(gam)[None, :] * lpos[own][:, None])
        ktab = ktab * (128.0 ** -0.5) * valid[:, None]
        m = dict(shared)
        m.update({"ctx_x": ctx, "valid": np.ascontiguousarray(valid.reshape(NCTX, 128).T), "cossin": cossin, "ktab": ktab.astype(np.float32),
                  "qtab": qtab.astype(np.float32)})
        in_maps.append(m)
    res = run_bass_kernel_spmd(nc, in_maps, core_ids=list(range(8)))
    if _dbg:
        return res.results, in_maps
    out = np.zeros((B, SEQ, D), np.float32)
    for c in range(8):
        b, g = c // 4, c % 4
        out[b, g * NOWN * 128:(g + 1) * NOWN * 128] = res.results[c]["out"]
    return out
```

```python
import numpy as np
from contextlib import ExitStack
import concourse.bass as bass
import concourse.mybir as mybir
from concourse.bass_utils import run_bass_kernel_spmd

F32 = mybir.dt.float32
BF16 = mybir.dt.bfloat16
U32 = mybir.dt.uint32
AF = mybir.ActivationFunctionType
ALU = mybir.AluOpType
AX = mybir.AxisListType

N_META = 16
PAD = 112
EPS = 1e-6
H = 16
HD = 128
DV = 256
PH = 8
NK = 128
TOPK = 16
ROPE_BASE = 10000.0


class Res:
    __slots__ = ("w", "r", "name")

    def __init__(self, name=""):
        self.w = None
        self.r = {}
        self.name = name


class DSem:
    __slots__ = ("sem", "tot")


class Ins:
    __slots__ = ("eng", "fn", "waits", "sig", "sigval", "dsem", "dval", "idx", "ph")


class Tile:
    __slots__ = ("t", "res", "ds", "_subs")

    def __init__(self, t, name):
        self.t = t
        self.res = Res(name)
        self.ds = None
        self._subs = None

    def subs(self, n):
        if self._subs is None:
            self._subs = [Tile(self.t, "%s.%d" % (self.res.name, i)) for i in range(n)]
        return self._subs


class Pool:
    def __init__(self, tiles):
        self.tiles = tiles
        self.i = 0

    def next(self):
        t = self.tiles[self.i % len(self.tiles)]
        self.i += 1
        return t


class _Rec:
    def __getattr__(self, name):
        def f(*a, **k):
            self.call = (name, a, k)
            return self
        return f


class Sched:
    ENGS = ("pe", "act", "dve", "pool", "sp")

    def __init__(self, nc, stack):
        self.nc = nc
        self.stack = stack
        self.lists = {e: [] for e in self.ENGS}
        self.esem = {e: stack.enter_context(nc.semaphore("es_" + e)) for e in self.ENGS}
        self.dsems = []
        self.free_ds = []
        self.lastc = {e: None for e in self.ENGS}
        self.cur_phase = "p0"
        self.scopes = False

    def get_ds(self):
        if self.free_ds:
            return self.free_ds.pop()
        d = DSem()
        d.sem = self.stack.enter_context(self.nc.semaphore("ds%d" % len(self.dsems)))
        d.tot = 0
        self.dsems.append(d)
        return d

    def op(self, eng, fn, reads=(), writes=(), dsem=None):
        rec = _Rec()
        fn(rec)
        name, a, k = rec.call
        ins = Ins()
        ins.eng = eng
        ins.fn = lambda e: getattr(e, name)(*a, **k)
        ins.sig = False
        ins.sigval = 0
        ins.dsem = dsem
        ins.idx = len(self.lists[eng])
        ins.ph = self.cur_phase
        deps = {}

        def add(d):
            if d is None:
                return
            if d.dsem is not None:
                deps[("d", id(d.dsem))] = d
            else:
                if d.eng == eng and eng == "pe" and dsem is None:
                    return
                k = ("c", d.eng)
                if k not in deps or deps[k].idx < d.idx:
                    deps[k] = d
        for t in reads:
            add(t.res.w)
        for t in writes:
            add(t.res.w)
            for d in t.res.r.values():
                add(d)
        waits = []
        for d in deps.values():
            if d.dsem is not None:
                waits.append((d.dsem, d.dsem.tot))
            else:
                d.sig = True
                waits.append(d)
        ins.waits = waits
        if dsem is not None:
            dsem.tot += 16
            ins.dval = dsem.tot
        else:
            self.lastc[eng] = ins
        key = ("d", id(dsem)) if dsem is not None else ("c", eng)
        for t in reads:
            t.res.r[key] = ins
        for t in writes:
            t.res.w = ins
            t.res.r = {}
        self.lists[eng].append(ins)
        return ins

    def dma(self, q, out, in_, tile, load, deps=None):
        if tile.ds is None:
            tile.ds = self.get_ds()
        fn = lambda e: e.dma_start(out=out, in_=in_)
        dl = [tile] if deps is None else list(deps)
        if load:
            return self.op(q, fn, writes=dl, dsem=tile.ds)
        return self.op(q, fn, reads=dl, dsem=tile.ds)

    def barrier(self, tiles=()):
        last = [self.lastc[e] for e in self.ENGS if self.lastc[e] is not None]
        dtot = [(d, d.tot) for d in self.dsems if d.tot > 0]
        for d in last:
            d.sig = True
        for e in self.ENGS:
            ins = self.op(e, lambda eng: eng.nop())
            ins.waits = list(last) + list(dtot)
        for t in tiles:
            if t.ds is not None:
                self.free_ds.append(t.ds)
                t.ds = None

    def emit(self):
        nc = self.nc
        for e in self.ENGS:
            c = 0
            for ins in self.lists[e]:
                if ins.dsem is None and ins.sig:
                    c += 1
                    ins.sigval = c
        with nc.Block() as block:
            def run(e):
                def body(eng):
                    waited = {}
                    cur = None
                    for ins in self.lists[e]:
                        if self.scopes and ins.ph != cur:
                            if cur is not None:
                                scope.__exit__(None, None, None)
                            scope = nc.named_scope(ins.ph)
                            scope.__enter__()
                            cur = ins.ph
                        for w in ins.waits:
                            if isinstance(w, tuple):
                                sem, val = w[0].sem, w[1]
                            else:
                                sem, val = self.esem[w.eng], w.sigval
                            k = id(sem)
                            if waited.get(k, 0) >= val:
                                continue
                            waited[k] = val
                            eng.wait_ge(sem, val)
                        bi = ins.fn(eng)
                        if ins.dsem is not None:
                            bi.then_inc(ins.dsem.sem, 16)
                        elif ins.sig:
                            bi.then_inc(self.esem[e], 1)
                    if e == "sp":
                        for d in self.dsems:
                            if d.tot > 0:
                                eng.wait_ge(d.sem, d.tot)
                    if cur is not None:
                        scope.__exit__(None, None, None)
                return body
            block.tensor(run("pe"))
            block.scalar(run("act"))
            block.vector(run("dve"))
            block.gpsimd(run("pool"))
            block.sync(run("sp"))


def build(D, NCTX, NOWN, dbg=False, scopes=False):
    KC = D // 128
    T = NCTX * 128
    TO = NOWN * 128
    OWN0 = NCTX - NOWN
    NE = NK * NK
    nc = bass.Bass("TRN2", target_bir_lowering=False)

    def din(name, shape, dt=F32):
        return nc.dram_tensor(name, shape, dt, kind="ExternalInput").ap()

    def dscr(name, shape, dt=BF16):
        return nc.dram_tensor(name, shape, dt, kind="ExternalOutput" if dbg else "Internal").ap()

    ctx_x = din("ctx_x", [T, D])
    valid_d = din("valid", [128, NCTX])
    cs_d = din("cossin", [T, 128])
    ktab_d = din("ktab", [T, H])
    qtab_d = din("qtab", [TO, H])
    gmix_d = din("norm_mix_g", [D])
    w_in = din("w_in", [D, 26640 - 8192 + 2 * D])
    bf_d = din("b_forget", [H])
    qg_d = din("q_norm_g", [HD])
    kg_d = din("k_norm_g", [HD])
    rg_d = din("ret_norm_g", [H * DV])
    wpf = din("w_proj_fox", [H * HD, D])
    wpr = din("w_proj_ret", [H * DV, D])
    wout = din("w_out", [D, D])
    gffn_d = din("norm_ffn_g", [D])
    wq = din("peer_w_q", [D, PH * 256])
    k1t = din("k1t", [PH, 128, NK])
    k2t = din("k2t", [PH, 128, NK])
    ut_d = din("ut", [NE // 256, 128, KC, 256])
    pv_d = din("peer_v", [NE, D])
    out_d = nc.dram_tensor("out", [TO, D], F32, kind="ExternalOutput").ap()

    uT_d = dscr("uT", [128, KC, T])
    kT_d = dscr("kT", [H, 128, T])
    vf_d = dscr("vf", [T, H, 129])
    krw_d = dscr("krw", [T, H, 128])
    krT_d = dscr("krT", [H, 128, TO])
    vr_d = dscr("vr", [T, H * DV])
    qT_d = dscr("qT", [H, 128, TO])
    qrT_d = dscr("qrT", [H, 128, TO])
    sg_d = dscr("sg", [TO, H * DV])
    ga_d = dscr("ga", [TO, D])
    gr_d = dscr("gr", [TO, D])
    ya_d = dscr("ya", [TO, H * HD])
    yr_d = dscr("yr", [TO, H * DV])
    h1_d = dscr("h1", [TO, D], F32)
    xnT_d = dscr("xnT", [128, KC, TO])
    qpT_d = dscr("qpT", [2 * PH, 128, TO])
    G_d = dscr("G", [NK, 128, TO])
    oraw_d = dscr("oraw", [H, TO, DV], F32) if dbg else None
    stt_d = dscr("sttd", [H, 128, DV], BF16) if dbg else None

    c_qa, c_ka, c_va, c_fa = 0, 2048, 4096, 6144
    c_qr = 6160
    c_kr = c_qr + 2048
    c_vr = c_kr + 2048
    c_gr = c_vr + 4096
    c_ga = c_gr + 4096
    c_gtr = c_ga + D

    with ExitStack() as st:
        S = Sched(nc, st)
        S.scopes = scopes
        psf = [Tile(st.enter_context(nc.psum_tensor("psf%d" % i, [128, 512], F32)), "psf%d" % i) for i in range(6)]
        psb = [Tile(st.enter_context(nc.psum_tensor("psb%d" % i, [128, 1024], BF16)), "psb%d" % i) for i in range(2)]
        PSA = Pool(psf[0:4])
        PSACC = Pool(psf[4:6])
        PSB = Pool(psb)

        cnt = [0]

        class Phase:
            def __init__(self):
                self.st = ExitStack()
                self.tiles = []

            def sb(self, name, shape, dt):
                cnt[0] += 1
                name = "s%d_%s" % (cnt[0], name)
                t = Tile(self.st.enter_context(nc.sbuf_tensor(name, shape, dt)), name)
                self.tiles.append(t)
                return t

            def pool(self, name, n, shape, dt):
                return Pool([self.sb("%s%d" % (name, i), shape, dt) for i in range(n)])

            def close(self):
                S.barrier(self.tiles + psf + psb)
                self.st.close()

        P0 = Phase()
        ident = P0.sb("ident", [128, 128], BF16)
        identf = P0.sb("identf", [128, 128], F32)
        tri = P0.sb("tri", [128, 128], BF16)
        trif = P0.sb("trif", [128, 128], F32)
        onesf = P0.sb("onesf", [128, 128], F32)
        iotar = P0.sb("iotar", [128, 128], F32)
        ctmp = P0.sb("ctmp", [128, 128], F32)
        mhalf = P0.sb("mhalf", [128, 8], F32)
        S.op("pool", lambda e: e.memset(mhalf.t[:], -0.5), writes=[mhalf])
        PA = Phase()
        valid = PA.sb("valid", [128, NCTX], F32)
        ktab = PA.sb("ktab", [128, NCTX, H], F32)
        qtab = PA.sb("qtab", [128, NOWN, H], F32)
        Lfull = PA.sb("Lfull", [128, NCTX, H], F32)
        Lb = PA.sb("Lb", [128, NCTX + 1, H], F32)
        S.op("pool", lambda e: e.iota(ctmp.t[:], pattern=[[1, 128]], base=0, channel_multiplier=-1,
                                      allow_small_or_imprecise_dtypes=True), writes=[ctmp])
        S.op("dve", lambda e: e.tensor_single_scalar(out=identf.t[:], in_=ctmp.t[:], scalar=0.0, op=ALU.is_equal),
             reads=[ctmp], writes=[identf])
        S.op("dve", lambda e: e.tensor_copy(out=ident.t[:], in_=identf.t[:]), reads=[identf], writes=[ident])
        S.op("dve", lambda e: e.tensor_single_scalar(out=trif.t[:], in_=ctmp.t[:], scalar=0.0, op=ALU.is_ge),
             reads=[ctmp], writes=[trif])
        S.op("dve", lambda e: e.tensor_copy(out=tri.t[:], in_=trif.t[:]), reads=[trif], writes=[tri])
        S.op("pool", lambda e: e.memset(onesf.t[:], 1.0), writes=[onesf])
        S.op("pool", lambda e: e.iota(iotar.t[:], pattern=[[1, 128]], base=0, channel_multiplier=0,
                                      allow_small_or_imprecise_dtypes=True), writes=[iotar])
        S.op("pool", lambda e: e.memset(Lb.t[:, 0, :], 0.0), writes=[Lb])
        S.dma("sp", valid.t[:], valid_d, valid, True)
        S.dma("sp", ktab.t[:], ktab_d.rearrange("(b p) h -> p b h", p=128), ktab, True)
        S.dma("sp", qtab.t[:], qtab_d.rearrange("(b p) h -> p b h", p=128), qtab, True)

        def bcast_load(ph, name, src, n, scale=None):
            t = ph.sb(name, [128, n], F32)
            S.dma("sp", t.t[:], src.partition_broadcast(128), t, True)
            if scale is not None:
                S.op("dve", lambda e: e.tensor_scalar(out=t.t[:], in0=t.t[:], scalar1=float(scale), scalar2=None,
                                                      op0=ALU.mult), reads=[t], writes=[t])
            return t

        def rmsnorm_T(ph, nblk, src_rows, g_bc, dstT, xpool, jpool, npool, opool):
            ss = ph.pool("ss", 2, [128, 2], F32)
            xq = []

            def xload(b):
                xt_ = xpool.next()
                S.dma("sp", xt_.t[:], src_rows(b), xt_, True)
                xq.append(xt_)
            for b in range(min(2, nblk)):
                xload(b)
            for blk in range(nblk):
                xt = xq[blk]
                if blk + 2 < nblk:
                    xload(blk + 2)
                jk = jpool.next()
                s1 = ss.next()
                S.op("pool", lambda e, s1=s1: e.memset(s1.t[:], 0.0), writes=[s1])
                S.op("act", lambda e, xt=xt, jk=jk, s1=s1: e.activation(out=jk.t[:], in_=xt.t[:], func=AF.Square,
                                                                         accum_out=s1.t[:, 0:1]),
                     reads=[xt], writes=[jk, s1])
                S.op("dve", lambda e, s1=s1: e.tensor_scalar(out=s1.t[:, 1:2], in0=s1.t[:, 0:1], scalar1=1.0 / D,
                                                              scalar2=EPS, op0=ALU.mult, op1=ALU.add),
                     reads=[s1], writes=[s1])
                S.op("pool", lambda e, s1=s1: e.tensor_tensor(out=s1.t[:, 1:2], in0=s1.t[:, 1:2], in1=mhalf.t[:, 0:1], op=ALU.pow),
                     reads=[s1, mhalf], writes=[s1])
                xn = npool.next()
                S.op("dve", lambda e, xt=xt, xn=xn, s1=s1: e.scalar_tensor_tensor(
                    out=xn.t[:], in0=xt.t[:], scalar=s1.t[:, 1:2], in1=g_bc.t[:], op0=ALU.mult, op1=ALU.mult),
                    reads=[xt, s1, g_bc], writes=[xn])
                ot = opool.next()
                for k8 in range(0, KC, 8):
                    pb = PSB.next()
                    n8 = min(8, KC - k8)
                    for j in range(n8):
                        S.op("pe", lambda e, pb=pb, xn=xn, j=j, k8=k8: e.transpose(
                            pb.t[:, j * 128:(j + 1) * 128], xn.t[:, (k8 + j) * 128:(k8 + j + 1) * 128], ident.t[:]),
                            reads=[xn, ident], writes=[pb])
                    eng = "act" if (k8 // 8) % 2 == 0 else "dve"
                    if eng == "act":
                        S.op("act", lambda e, pb=pb, ot=ot, k8=k8, n8=n8: e.copy(
                            out=ot.t[:, k8:k8 + n8, :], in_=pb.t[:, 0:n8 * 128].rearrange("p (k t) -> p k t", t=128)),
                            reads=[pb], writes=[ot])
                    else:
                        S.op("dve", lambda e, pb=pb, ot=ot, k8=k8, n8=n8: e.tensor_copy(
                            out=ot.t[:, k8:k8 + n8, :], in_=pb.t[:, 0:n8 * 128].rearrange("p (k t) -> p k t", t=128)),
                            reads=[pb], writes=[ot])
                S.dma("sp", dstT[:, :, blk * 128:(blk + 1) * 128], ot.t[:], ot, False)

        S.cur_phase = "p1"
        P1 = Phase()
        g_bc = bcast_load(P1, "gmix", gmix_d, D)
        rmsnorm_T(P1, NCTX, lambda blk: ctx_x[blk * 128:(blk + 1) * 128, :], g_bc, uT_d,
                  P1.pool("x", 3, [128, D], F32), P1.pool("jk", 1, [128, D], BF16),
                  P1.pool("xn", 2, [128, D], BF16), P1.pool("uo", 2, [128, KC, 128], BF16))
        P1.close()

        def gemm_multi(xT, nblk, kcs, wpool, jobs):
            nkc = len(kcs)
            tiles = []
            for (W, N, evac) in jobs:
                for nt in range((N + 511) // 512):
                    tiles.append((W, nt, min(512, N - nt * 512), evac))

            def load(ti):
                W, nt, nw, _ = tiles[ti]
                wt = wpool.next()
                n0 = nt * 512
                for k4 in range(0, nkc, 8):
                    k5 = min(nkc, k4 + 8)
                    S.dma("pool", wt.t[:, k4:k5, 0:nw],
                          W[k4 * 128:k5 * 128, n0:n0 + nw].rearrange("(kc p) n -> p kc n", p=128), wt, True)
                return wt
            nxt = load(0)
            pending = None
            for ti, (W, nt, nw, evac) in enumerate(tiles):
                wt = nxt
                if ti + 1 < len(tiles):
                    nxt = load(ti + 1)
                for tb in range(nblk):
                    ps = PSA.next()
                    for i, kc in enumerate(kcs):
                        S.op("pe", lambda e, ps=ps, wt=wt, i=i, kc=kc, tb=tb, nw=nw: e.matmul(
                            ps.t[:, 0:nw], lhsT=xT.t[:, kc, tb * 128:(tb + 1) * 128], rhs=wt.t[:, i, 0:nw],
                            start=(i == 0), stop=(i == nkc - 1)), reads=[xT, wt], writes=[ps])
                    if pending is not None:
                        pending()
                    pending = evac(tb, nt, ps, nw)
            if pending is not None:
                pending()

        def gemm(xT, nblk, kcs, W, N, wpool, evac, pre=None):
            gemm_multi(xT, nblk, kcs, wpool, [(W, N, evac)])

        def load_xT(xT, srcT, b0, nb):
            for k4 in range(0, KC, 8):
                k5 = min(KC, k4 + 8)
                S.dma("sp", xT.t[:, k4:k5, 0:nb * 128], srcT[:, k4:k5, b0 * 128:(b0 + nb) * 128], xT, True)

        def qknorm_T(ph, ps, g_t, dst, h0, col0, ssp, knp, ktp):
            s4 = ssp.next()
            kn = knp.next()
            S.op("pool", lambda e: e.memset(s4.t[:], 0.0), writes=[s4])
            for h in range(4):
                S.op("act", lambda e, h=h: e.activation(out=kn.t[:, h * 128:(h + 1) * 128], in_=ps.t[:, h * 128:(h + 1) * 128],
                                                         func=AF.Square, accum_out=s4.t[:, h:h + 1]),
                     reads=[ps], writes=[kn, s4])
            S.op("dve", lambda e: e.tensor_scalar(out=s4.t[:, 4:8], in0=s4.t[:, 0:4], scalar1=1.0 / HD, scalar2=EPS,
                                                  op0=ALU.mult, op1=ALU.add), reads=[s4], writes=[s4])
            S.op("pool", lambda e: e.tensor_tensor(out=s4.t[:, 4:8], in0=s4.t[:, 4:8], in1=mhalf.t[:, 0:4], op=ALU.pow),
                 reads=[s4, mhalf], writes=[s4])
            for h in range(4):
                S.op("dve", lambda e, h=h: e.scalar_tensor_tensor(
                    out=kn.t[:, h * 128:(h + 1) * 128], in0=ps.t[:, h * 128:(h + 1) * 128], scalar=s4.t[:, 4 + h:5 + h],
                    in1=g_t.t[:], op0=ALU.mult, op1=ALU.mult), reads=[ps, s4, g_t], writes=[kn])
            return lambda: transpose4(kn, dst, h0, col0, ktp)

        def transpose4(kn, dst, h0, col0, ktp):
            pb = PSB.next()
            for h in range(4):
                S.op("pe", lambda e, h=h: e.transpose(pb.t[:, h * 128:(h + 1) * 128], kn.t[:, h * 128:(h + 1) * 128],
                                                      ident.t[:]), reads=[kn, ident], writes=[pb])
            kt = ktp.next()
            S.op("act", lambda e: e.copy(out=kt.t[:], in_=pb.t[:, 0:512]), reads=[pb], writes=[kt])
            S.dma("sp", dst[h0:h0 + 4, :, col0:col0 + 128].rearrange("h p t -> p h t"),
                  kt.t[:].rearrange("p (h t) -> p h t", t=128), kt, False)

        def rotary(ph, ps, cs, tabcol, ro_p, kn_p):
            ro = ro_p.next()
            pv = ps.t[:, 0:512].rearrange("p (h two f) -> p h two f", h=4, two=2)
            rv = ro.t[:].rearrange("p (a h f) -> p a h f", a=4, h=4)
            cosb = cs.t[:, 0:64].unsqueeze(1).to_broadcast([128, 4, 64])
            sinb = cs.t[:, 64:128].unsqueeze(1).to_broadcast([128, 4, 64])
            S.op("dve", lambda e: e.tensor_tensor(out=rv[:, 0], in0=pv[:, :, 0, :], in1=cosb, op=ALU.mult), reads=[ps, cs], writes=[ro])
            S.op("dve", lambda e: e.tensor_tensor(out=rv[:, 1], in0=pv[:, :, 1, :], in1=sinb, op=ALU.mult), reads=[ps, cs], writes=[ro])
            S.op("dve", lambda e: e.tensor_tensor(out=rv[:, 2], in0=pv[:, :, 0, :], in1=sinb, op=ALU.mult), reads=[ps, cs], writes=[ro])
            S.op("dve", lambda e: e.tensor_tensor(out=rv[:, 3], in0=pv[:, :, 1, :], in1=cosb, op=ALU.mult), reads=[ps, cs], writes=[ro])
            S.op("pool", lambda e: e.tensor_tensor(out=rv[:, 0], in0=rv[:, 0], in1=rv[:, 1], op=ALU.subtract), reads=[ro], writes=[ro])
            S.op("pool", lambda e: e.tensor_tensor(out=rv[:, 2], in0=rv[:, 2], in1=rv[:, 3], op=ALU.add), reads=[ro], writes=[ro])
            kn = kn_p.next()
            kv = kn.t[:].rearrange("p (h two f) -> p h two f", h=4, two=2)
            tb_ = tabcol.unsqueeze(2).to_broadcast([128, 4, 64])
            S.op("dve", lambda e: e.tensor_tensor(out=kv[:, :, 0, :], in0=rv[:, 0], in1=tb_, op=ALU.mult), reads=[ro, ktab, qtab], writes=[kn])
            S.op("dve", lambda e: e.tensor_tensor(out=kv[:, :, 1, :], in0=rv[:, 2], in1=tb_, op=ALU.mult), reads=[ro, ktab, qtab], writes=[kn])
            return kn

        TT2 = min(NCTX, 11)
        S.cur_phase = "p2"
        P2 = Phase()
        xT2 = P2.sb("xT2", [128, KC, TT2 * 128], BF16)
        wp2 = P2.pool("w", 2, [128, KC, 512], BF16)
        kg_bc = bcast_load(P2, "kg", kg_d, HD)
        bf_bc = bcast_load(P2, "bfb", bf_d, H)
        ss4 = P2.pool("s4", 3, [128, 8], F32)
        knp = P2.pool("kn", 4, [128, 512], BF16)
        ktp = P2.pool("kt", 3, [128, 512], BF16)
        vsp = P2.pool("vs", 3, [128, 4, 129], BF16)
        rop = P2.pool("ro", 2, [128, 1024], F32)
        csp = P2.pool("cs", 3, [128, 128], F32)
        fz = P2.pool("fz", 2, [128, 48], F32)
        for b0 in range(0, NCTX, TT2):
            nb = min(TT2, NCTX - b0)
            load_xT(xT2, uT_d, b0, nb)

            def ev_ka(tb, nt, ps, nw, b0=b0):
                return qknorm_T(P2, ps, kg_bc, kT_d, nt * 4, (b0 + tb) * 128, ss4, knp, ktp)

            def ev_va(tb, nt, ps, nw, b0=b0):
                blk = b0 + tb
                vs = vsp.next()
                S.op("dve", lambda e: e.tensor_scalar(out=vs.t[:, :, 0:128], in0=ps.t[:, 0:512].rearrange("p (h f) -> p h f", h=4),
                                                      scalar1=valid.t[:, blk:blk + 1], scalar2=None, op0=ALU.mult),
                     reads=[ps, valid], writes=[vs])
                S.op("pool", lambda e: e.tensor_copy(out=vs.t[:, :, 128:129],
                                                     in_=valid.t[:, blk:blk + 1].unsqueeze(1).to_broadcast([128, 4, 1])),
                     reads=[valid], writes=[vs])
                S.dma("sp", vf_d[blk * 128:(blk + 1) * 128, nt * 4:nt * 4 + 4, :], vs.t[:], vs, False)

            def ev_fa(tb, nt, ps, nw, b0=b0):
                blk = b0 + tb
                z = fz.next()
                S.op("dve", lambda e: e.tensor_tensor(out=z.t[:, 0:16], in0=ps.t[:, 0:16], in1=bf_bc.t[:], op=ALU.add),
                     reads=[ps, bf_bc], writes=[z])
                S.op("act", lambda e: e.activation(out=z.t[:, 16:32], in_=z.t[:, 0:16], func=AF.Exp, scale=-1.0), reads=[z], writes=[z])
                S.op("act", lambda e: e.activation(out=z.t[:, 32:48], in_=z.t[:, 16:32], func=AF.Ln, bias=1.0), reads=[z], writes=[z])
                p2 = PSA.next()
                S.op("pe", lambda e: e.matmul(p2.t[:, 0:16], lhsT=trif.t[:], rhs=z.t[:, 32:48], start=True, stop=True),
                     reads=[trif, z], writes=[p2])
                S.op("pe", lambda e: e.matmul(p2.t[:, 16:32], lhsT=onesf.t[:], rhs=z.t[:, 32:48], start=True, stop=True),
                     reads=[onesf, z], writes=[p2])
                S.op("dve", lambda e: e.tensor_tensor(out=Lfull.t[:, blk, :], in0=p2.t[:, 0:16], in1=Lb.t[:, blk, :], op=ALU.add),
                     reads=[p2, Lb], writes=[Lfull])
                S.op("dve", lambda e: e.tensor_tensor(out=Lb.t[:, blk + 1, :], in0=p2.t[:, 16:32], in1=Lb.t[:, blk, :], op=ALU.add),
                     reads=[p2, Lb], writes=[Lb])

            def ev_kr(tb, nt, ps, nw, b0=b0):
                blk = b0 + tb
                cs = csp.next()
                S.dma("sp", cs.t[:], cs_d[blk * 128:(blk + 1) * 128, :], cs, True)
                kn = rotary(P2, ps, cs, ktab.t[:, blk, nt * 4:nt * 4 + 4], rop, knp)
                if blk < OWN0:
                    S.dma("sp", krw_d[blk * 128:(blk + 1) * 128, nt * 4:nt * 4 + 4, :],
                          kn.t[:].rearrange("p (h f) -> p h f", h=4), kn, False)
                else:
                    return lambda: transpose4(kn, krT_d, nt * 4, (blk - OWN0) * 128, ktp)

            def ev_vr(tb, nt, ps, nw, b0=b0):
                blk = b0 + tb
                vs = knp.next()
                S.op("act", lambda e: e.activation(out=vs.t[:], in_=ps.t[:, 0:512], func=AF.Copy,
                                                   scale=valid.t[:, blk:blk + 1]), reads=[ps, valid], writes=[vs])
                S.dma("sp", vr_d[blk * 128:(blk + 1) * 128, nt * 512:(nt + 1) * 512], vs.t[:], vs, False)

            kcs = list(range(KC))
            gemm_multi(xT2, nb, kcs, wp2, [
                (w_in[:, c_fa:c_fa + 16], 16, ev_fa),
                (w_in[:, c_ka:c_ka + 2048], 2048, ev_ka),
                (w_in[:, c_va:c_va + 2048], 2048, ev_va),
                (w_in[:, c_kr:c_kr + 2048], 2048, ev_kr),
                (w_in[:, c_vr:c_vr + 4096], 4096, ev_vr)])
        P2.close()

        TT3 = min(NOWN, 8)
        S.cur_phase = "p3"
        P3 = Phase()
        xT3 = P3.sb("xT3", [128, KC, TT3 * 128], BF16)
        wp3 = P3.pool("w", 2, [128, KC, 512], BF16)
        qg_bc = bcast_load(P3, "qg", qg_d, HD, scale=HD ** -0.5)
        ss4 = P3.pool("s4", 3, [128, 8], F32)
        knp = P3.pool("kn", 4, [128, 512], BF16)
        ktp = P3.pool("kt", 3, [128, 512], BF16)
        rop = P3.pool("ro", 2, [128, 1024], F32)
        csp = P3.pool("cs", 3, [128, 128], F32)
        for b0 in range(0, NOWN, TT3):
            nb = min(TT3, NOWN - b0)
            load_xT(xT3, uT_d, OWN0 + b0, nb)

            def ev_qa(tb, nt, ps, nw, b0=b0):
                return qknorm_T(P3, ps, qg_bc, qT_d, nt * 4, (b0 + tb) * 128, ss4, knp, ktp)

            def ev_qr(tb, nt, ps, nw, b0=b0):
                blk = b0 + tb
                cs = csp.next()
                S.dma("sp", cs.t[:], cs_d[(OWN0 + blk) * 128:(OWN0 + blk + 1) * 128, :], cs, True)
                kn = rotary(P3, ps, cs, qtab.t[:, blk, nt * 4:nt * 4 + 4], rop, knp)
                return lambda: transpose4(kn, qrT_d, nt * 4, blk * 128, ktp)

            def ev_act(dst, func):
                def ev(tb, nt, ps, nw, b0=b0):
                    blk = b0 + tb
                    vs = knp.next()
                    S.op("act", lambda e: e.activation(out=vs.t[:], in_=ps.t[:, 0:512], func=func), reads=[ps], writes=[vs])
                    S.dma("sp", dst[blk * 128:(blk + 1) * 128, nt * 512:(nt + 1) * 512], vs.t[:], vs, False)
                return ev
            kcs = list(range(KC))
            gemm_multi(xT3, nb, kcs, wp3, [
                (w_in[:, c_qa:c_qa + 2048], 2048, ev_qa),
                (w_in[:, c_qr:c_qr + 2048], 2048, ev_qr),
                (w_in[:, c_gr:c_gr + 4096], 4096, ev_act(sg_d, AF.Silu)),
                (w_in[:, c_ga:c_ga + D], D, ev_act(ga_d, AF.Sigmoid)),
                (w_in[:, c_gtr:c_gtr + D], D, ev_act(gr_d, AF.Sigmoid))])
        P3.close()

        S.cur_phase = "p4"
        P4 = Phase()
        qTp = P4.pool("qT", 2, [128, TO], BF16)
        kTp = P4.pool("kT", 2, [128, T], BF16)
        vp = P4.pool("v", 2, [128, NCTX, 129], BF16)
        bip = P4.pool("bias", 2, [128, NOWN, NCTX], F32)
        ptp = P4.pool("pt", 6, [128, 128], BF16)
        yap = P4.pool("yas", 2, [128, NOWN, 128], BF16)
        rcp = P4.pool("rc", 2, [128, 1], F32)
        def p4load(h):
            qT = qTp.next(); kT = kTp.next(); vv = vp.next()
            S.dma("sp", qT.t[:], qT_d[h], qT, True)
            S.dma("sp", kT.t[:], kT_d[h], kT, True)
            S.dma("sp", vv.t[:], vf_d[:, h, :].rearrange("(b p) f -> p b f", p=128), vv, True)
            return qT, kT, vv
        nx4 = p4load(0)
        for h in range(H):
            qT, kT, vv = nx4
            if h + 1 < H:
                nx4 = p4load(h + 1)
            yas = yap.next()
            biasT = bip.next()
            for i in range(NOWN):
                gi = OWN0 + i
                S.op("dve", lambda e, i=i, gi=gi: e.tensor_scalar(
                    out=biasT.t[:, i, 0:gi + 1], in0=Lfull.t[:, 0:gi + 1, h], scalar1=Lb.t[:, gi, h:h + 1], scalar2=None,
                    op0=ALU.subtract), reads=[Lfull, Lb], writes=[biasT])
            pairs = [(i, j) for i in range(NOWN) for j in range(OWN0 + i + 1)]
            pss = {}

            def qk(p):
                i, j = pairs[p]
                ps = PSA.next()
                pss[p] = ps
                S.op("pe", lambda e: e.matmul(ps.t[:, 0:128], lhsT=kT.t[:, j * 128:(j + 1) * 128], rhs=qT.t[:, i * 128:(i + 1) * 128],
                                              start=True, stop=True), reads=[kT, qT], writes=[ps])
            LA = 3
            for p in range(min(LA, len(pairs))):
                qk(p)
            acc = None
            for p, (i, j) in enumerate(pairs):
                if p + LA < len(pairs):
                    qk(p + LA)
                gi = OWN0 + i
                if j == 0:
                    acc = PSACC.next()
                ps = pss.pop(p)
                pt = ptp.next()
                S.op("act", lambda e: e.activation(out=pt.t[:], in_=ps.t[:, 0:128], func=AF.Exp, bias=biasT.t[:, i, j:j + 1], scale=1.0),
                     reads=[ps, biasT], writes=[pt])
                if j == gi:
                    S.op("pool", lambda e: e.tensor_tensor(out=pt.t[:], in0=pt.t[:], in1=tri.t[:], op=ALU.mult),
                         reads=[pt, tri], writes=[pt])
                S.op("pe", lambda e: e.matmul(acc.t[:, 0:129], lhsT=pt.t[:], rhs=vv.t[:, j, :], start=(j == 0), stop=(j == gi)),
                     reads=[pt, vv], writes=[acc])
                if j == gi:
                    rc = rcp.next()
                    S.op("dve", lambda e: e.reciprocal(out=rc.t[:], in_=acc.t[:, 128:129]), reads=[acc], writes=[rc])
                    S.op("dve", lambda e: e.tensor_scalar(out=yas.t[:, i, :], in0=acc.t[:, 0:128], scalar1=rc.t[:, 0:1], scalar2=None,
                                                          op0=ALU.mult), reads=[acc, rc], writes=[yas])
            S.dma("sp", ya_d[:, h * 128:(h + 1) * 128].rearrange("(b p) f -> p b f", p=128), yas.t[:], yas, False)
        P4.close()

        S.cur_phase = "p5"
        P5 = Phase()
        rg_bc = bcast_load(P5, "rg", rg_d, H * DV)
        NPV = max(OWN0, 1)
        krwp = P5.pool("krw", 2, [128, NPV, 128], BF16)
        vrp = P5.pool("vr", 2, [128, NCTX, DV], BF16)
        qrp = P5.pool("qrT", 2, [128, TO], BF16)
        krp = P5.pool("krT", 2, [128, TO], BF16)
        sgp = P5.pool("sg", 2, [128, NOWN, DV], BF16)
        stp = P5.pool("st", 2, [128, DV], BF16)
        ptp = P5.pool("pt", 6, [128, 128], BF16)
        yrp = P5.pool("yrs", 1, [128, NOWN, DV], BF16)
        smp = P5.pool("sm", 3, [128, 8], F32)
        jkp = P5.pool("jk", 2, [128, DV], F32)
        onp = P5.pool("on", 2, [128, DV], F32)
        def p5load(h):
            krw = krwp.next(); vr = vrp.next(); qr = qrp.next(); kr = krp.next(); sg = sgp.next()
            S.dma("sp", krw.t[:, 0:OWN0, :], krw_d[0:OWN0 * 128, h, :].rearrange("(b p) f -> p b f", p=128), krw, True)
            S.dma("sp", vr.t[:], vr_d[:, h * DV:(h + 1) * DV].rearrange("(b p) f -> p b f", p=128), vr, True)
            S.dma("sp", qr.t[:], qrT_d[h], qr, True)
            S.dma("sp", kr.t[:], krT_d[h], kr, True)
            S.dma("sp", sg.t[:], sg_d[:, h * DV:(h + 1) * DV].rearrange("(b p) f -> p b f", p=128), sg, True)
            return krw, vr, qr, kr, sg
        nx5 = p5load(0)
        for h in range(H):
            gam = 1.0 - 2.0 ** (-5.0 - h)
            krw, vr, qr, kr, sg = nx5
            if h + 1 < H:
                nx5 = p5load(h + 1)
            yrs = yrp.next()
            sp_ = PSA.next()
            for j in range(OWN0):
                S.op("pe", lambda e, j=j: e.matmul(sp_.t[:, 0:DV], lhsT=krw.t[:, j, :], rhs=vr.t[:, j, :],
                                                   start=(j == 0), stop=(j == OWN0 - 1)), reads=[krw, vr], writes=[sp_])
            stt = stp.next()
            S.op("act", lambda e, stt=stt, sp_=sp_, gam=gam: e.activation(out=stt.t[:], in_=sp_.t[:, 0:DV], func=AF.Copy, scale=float(gam)),
                 reads=[sp_], writes=[stt])
            if dbg:
                S.dma("sp", stt_d[h], stt.t[:], stt, False)
            pairs = [(i, j) for i in range(NOWN) for j in range(i + 1)]
            pss = {}

            def sk(p):
                i, j = pairs[p]
                ps = PSA.next()
                pss[p] = ps
                S.op("pe", lambda e: e.matmul(ps.t[:, 0:128], lhsT=kr.t[:, j * 128:(j + 1) * 128], rhs=qr.t[:, i * 128:(i + 1) * 128],
                                              start=True, stop=True), reads=[kr, qr], writes=[ps])
            LA = 3
            for p in range(min(LA, len(pairs))):
                sk(p)
            acc = None
            for p, (i, j) in enumerate(pairs):
                if p + LA < len(pairs):
                    sk(p + LA)
                if j == 0:
                    acc = PSACC.next()
                    S.op("pe", lambda e: e.matmul(acc.t[:, 0:DV], lhsT=qr.t[:, i * 128:(i + 1) * 128], rhs=stt.t[:],
                                                  start=True, stop=False), reads=[qr, stt], writes=[acc])
                ps = pss.pop(p)
                pt = ptp.next()
                if j == i:
                    S.op("dve", lambda e: e.tensor_tensor(out=pt.t[:], in0=ps.t[:, 0:128], in1=tri.t[:], op=ALU.mult),
                         reads=[ps, tri], writes=[pt])
                else:
                    S.op("act", lambda e: e.copy(out=pt.t[:], in_=ps.t[:, 0:128]), reads=[ps], writes=[pt])
                S.op("pe", lambda e: e.matmul(acc.t[:, 0:DV], lhsT=pt.t[:], rhs=vr.t[:, OWN0 + j, :], start=False, stop=(j == i)),
                     reads=[pt, vr], writes=[acc])
                if j != i:
                    continue
                sm = smp.next(); jk = jkp.next(); on = onp.next()
                S.op("pool", lambda e, sm=sm: e.memset(sm.t[:], 0.0), writes=[sm])
                S.op("act", lambda e, acc=acc, jk=jk, sm=sm: e.activation(out=jk.t[:], in_=acc.t[:, 0:DV], func=AF.Copy,
                                                                         accum_out=sm.t[:, 0:1]), reads=[acc], writes=[jk, sm])
                if dbg:
                    S.dma("sp", oraw_d[h, i * 128:(i + 1) * 128, :], jk.t[:], jk, False)
                S.op("act", lambda e, acc=acc, jk=jk, sm=sm: e.activation(out=jk.t[:], in_=acc.t[:, 0:DV], func=AF.Square,
                                                                         accum_out=sm.t[:, 1:2]), reads=[acc], writes=[jk, sm])
                S.op("dve", lambda e, sm=sm: e.tensor_scalar(out=sm.t[:, 2:4], in0=sm.t[:, 0:2], scalar1=1.0 / DV, scalar2=None, op0=ALU.mult),
                     reads=[sm], writes=[sm])
                S.op("dve", lambda e, sm=sm: e.tensor_tensor(out=sm.t[:, 4:5], in0=sm.t[:, 2:3], in1=sm.t[:, 2:3], op=ALU.mult),
                     reads=[sm], writes=[sm])
                S.op("dve", lambda e, sm=sm: e.tensor_tensor(out=sm.t[:, 5:6], in0=sm.t[:, 3:4], in1=sm.t[:, 4:5], op=ALU.subtract),
                     reads=[sm], writes=[sm])
                S.op("dve", lambda e, sm=sm: e.tensor_scalar(out=sm.t[:, 6:7], in0=sm.t[:, 5:6], scalar1=EPS, scalar2=None,
                                                              op0=ALU.add), reads=[sm], writes=[sm])
                S.op("pool", lambda e, sm=sm: e.tensor_tensor(out=sm.t[:, 6:7], in0=sm.t[:, 6:7], in1=mhalf.t[:, 0:1], op=ALU.pow),
                     reads=[sm, mhalf], writes=[sm])
                S.op("dve", lambda e, sm=sm, acc=acc, on=on: e.tensor_scalar(
                    out=on.t[:], in0=acc.t[:, 0:DV], scalar1=sm.t[:, 2:3], scalar2=sm.t[:, 6:7], op0=ALU.subtract, op1=ALU.mult),
                    reads=[acc, sm], writes=[on])
                S.op("pool", lambda e, on=on, h=h: e.tensor_tensor(out=on.t[:], in0=on.t[:], in1=rg_bc.t[:, h * DV:(h + 1) * DV], op=ALU.mult),
                     reads=[on, rg_bc], writes=[on])
                S.op("pool", lambda e, on=on, sg=sg, yrs=yrs, i=i: e.tensor_tensor(out=yrs.t[:, i, :], in0=on.t[:], in1=sg.t[:, i, :], op=ALU.mult),
                     reads=[on, sg], writes=[yrs])
            S.dma("sp", yr_d[:, h * DV:(h + 1) * DV].rearrange("(b p) f -> p b f", p=128), yrs.t[:], yrs, False)
        P5.close()
        PA.close()

        TT6 = min(NOWN, 4)
        KA = H * HD // 128
        KR = H * DV // 128
        S.cur_phase = "p6"
        P6 = Phase()
        yT = P6.sb("yT", [128, KA + KR, TT6 * 128], BF16)
        mbf = P6.sb("mbf", [128, TT6, D], BF16)
        mT = yT
        wp6 = P6.pool("w", 2, [128, 32, 512], BF16)
        wap6 = P6.pool("wa", 2, [128, KA, 512], BF16)
        ysp = P6.pool("ys", 2, [128, 2048], BF16)
        gtp = P6.pool("gt", 4, [128, 512], BF16)
        xp6 = P6.pool("x6", 2, [128, 512], F32)
        tp6 = P6.pool("t6", 3, [128, 512], F32)
        for b0 in range(0, NOWN, TT6):
            nb = min(TT6, NOWN - b0)
            for tb in range(nb):
                blk = b0 + tb
                for part in range((KA + KR) // 16):
                    ys = ysp.next()
                    if part == 0:
                        S.dma("sp", ys.t[:], ya_d[blk * 128:(blk + 1) * 128, :], ys, True)
                    else:
                        S.dma("sp", ys.t[:], yr_d[blk * 128:(blk + 1) * 128, (part - 1) * 2048:part * 2048], ys, True)
                    for k8 in range(0, 16, 8):
                        pb = PSB.next()
                        for j in range(8):
                            S.op("pe", lambda e: e.transpose(pb.t[:, j * 128:(j + 1) * 128], ys.t[:, (k8 + j) * 128:(k8 + j + 1) * 128], ident.t[:]),
                                 reads=[ys, ident], writes=[pb])
                        kk0 = part * 16 + k8
                        if (k8 // 8) % 2 == 0:
                            S.op("act", lambda e: e.copy(out=yT.t[:, kk0:kk0 + 8, tb * 128:(tb + 1) * 128],
                                                         in_=pb.t[:].rearrange("p (k t) -> p k t", t=128)), reads=[pb], writes=[yT])
                        else:
                            S.op("dve", lambda e: e.tensor_copy(out=yT.t[:, kk0:kk0 + 8, tb * 128:(tb + 1) * 128],
                                                                in_=pb.t[:].rearrange("p (k t) -> p k t", t=128)), reads=[pb], writes=[yT])
            def p6load(nt):
                wa = wap6.next(); wr = wp6.next()
                for k4 in range(0, KA, 8):
                    S.dma("pool", wa.t[:, k4:k4 + 8, :], wpf[k4 * 128:(k4 + 8) * 128, nt * 512:(nt + 1) * 512].rearrange("(kc p) n -> p kc n", p=128), wa, True)
                for k4 in range(0, KR, 8):
                    S.dma("pool", wr.t[:, k4:k4 + 8, :], wpr[k4 * 128:(k4 + 8) * 128, nt * 512:(nt + 1) * 512].rearrange("(kc p) n -> p kc n", p=128), wr, True)
                return wa, wr
            nx6 = p6load(0)
            for nt in range(D // 512):
                wa, wr = nx6
                if nt + 1 < D // 512:
                    nx6 = p6load(nt + 1)
                for tb in range(nb):
                    blk = b0 + tb
                    pa = PSA.next(); pr = PSA.next()
                    for kc in range(KA):
                        S.op("pe", lambda e, pa=pa, wa=wa, kc=kc, tb=tb: e.matmul(pa.t[:], lhsT=yT.t[:, kc, tb * 128:(tb + 1) * 128],
                                                                                 rhs=wa.t[:, kc, :], start=(kc == 0), stop=(kc == KA - 1)),
                             reads=[yT, wa], writes=[pa])
                    for kc in range(KR):
                        S.op("pe", lambda e, pr=pr, wr=wr, kc=kc, tb=tb: e.matmul(pr.t[:], lhsT=yT.t[:, KA + kc, tb * 128:(tb + 1) * 128],
                                                                                 rhs=wr.t[:, kc, :], start=(kc == 0), stop=(kc == KR - 1)),
                             reads=[yT, wr], writes=[pr])
                    g1 = gtp.next(); g2 = gtp.next(); t1 = tp6.next(); t2 = tp6.next()
                    S.dma("sp", g1.t[:], ga_d[blk * 128:(blk + 1) * 128, nt * 512:(nt + 1) * 512], g1, True)
                    S.dma("sp", g2.t[:], gr_d[blk * 128:(blk + 1) * 128, nt * 512:(nt + 1) * 512], g2, True)
                    S.op("dve", lambda e, t1=t1, pa=pa, g1=g1: e.tensor_tensor(out=t1.t[:], in0=pa.t[:], in1=g1.t[:], op=ALU.mult), reads=[pa, g1], writes=[t1])
                    S.op("dve", lambda e, t2=t2, pr=pr, g2=g2: e.tensor_tensor(out=t2.t[:], in0=pr.t[:], in1=g2.t[:], op=ALU.mult), reads=[pr, g2], writes=[t2])
                    S.op("dve", lambda e, t1=t1, t2=t2, tb=tb, nt=nt: e.tensor_tensor(out=mbf.t[:, tb, nt * 512:(nt + 1) * 512], in0=t1.t[:], in1=t2.t[:], op=ALU.add),
                         reads=[t1, t2], writes=[mbf])
            for tb in range(nb):
                for k8 in range(0, KC, 8):
                    pb = PSB.next()
                    n8 = min(8, KC - k8)
                    for j in range(n8):
                        S.op("pe", lambda e, pb=pb, j=j, k8=k8, tb=tb: e.transpose(
                            pb.t[:, j * 128:(j + 1) * 128], mbf.t[:, tb, (k8 + j) * 128:(k8 + j + 1) * 128], ident.t[:]),
                            reads=[mbf, ident], writes=[pb])
                    S.op("act", lambda e, pb=pb, k8=k8, n8=n8, tb=tb: e.copy(out=mT.t[:, k8:k8 + n8, tb * 128:(tb + 1) * 128],
                                                                            in_=pb.t[:, 0:n8 * 128].rearrange("p (k t) -> p k t", t=128)),
                         reads=[pb], writes=[mT])

            def ev_out(tb, nt, ps, nw, b0=b0):
                blk = b0 + tb
                xt = xp6.next()
                S.dma("sp", xt.t[:], ctx_x[(OWN0 + blk) * 128:(OWN0 + blk + 1) * 128, nt * 512:(nt + 1) * 512], xt, True)
                S.op("dve", lambda e: e.tensor_tensor(out=xt.t[:], in0=ps.t[:], in1=xt.t[:], op=ALU.add), reads=[ps, xt], writes=[xt])
                S.dma("sp", h1_d[blk * 128:(blk + 1) * 128, nt * 512:(nt + 1) * 512], xt.t[:], xt, False)
            gemm(mT, nb, list(range(KC)), wout, D, wp6, ev_out)
        P6.close()

        S.cur_phase = "p7"
        P7 = Phase()
        gf_bc = bcast_load(P7, "gffn", gffn_d, D)
        rmsnorm_T(P7, NOWN, lambda blk: h1_d[blk * 128:(blk + 1) * 128, :], gf_bc, xnT_d,
                  P7.pool("x", 3, [128, D], F32), P7.pool("jk", 1, [128, D], BF16),
                  P7.pool("xn", 2, [128, D], BF16), P7.pool("uo", 2, [128, KC, 128], BF16))
        P7.close()

        TT7 = min(NOWN, 8)
        S.cur_phase = "p7b"
        P7b = Phase()
        xT7 = P7b.sb("xT7", [128, KC, TT7 * 128], BF16)
        wp7 = P7b.pool("w", 2, [128, KC, 512], BF16)
        knp = P7b.pool("kn", 4, [128, 512], BF16)
        ktp = P7b.pool("kt", 3, [128, 512], BF16)
        for b0 in range(0, NOWN, TT7):
            nb = min(TT7, NOWN - b0)
            load_xT(xT7, xnT_d, b0, nb)

            def ev_q(tb, nt, ps, nw, b0=b0):
                kn = knp.next()
                S.op("act", lambda e: e.copy(out=kn.t[:], in_=ps.t[:, 0:512]), reads=[ps], writes=[kn])
                return lambda: transpose4(kn, qpT_d, nt * 4, (b0 + tb) * 128, ktp)
            gemm(xT7, nb, list(range(KC)), wq, PH * 256, wp7, ev_q)
        P7b.close()

        S.cur_phase = "p7c"
        P7c = Phase()
        kk = P7c.sb("kk", [128, 2 * PH, NK], BF16)
        for hh in range(PH):
            S.dma("pool", kk.t[:, 2 * hh, :], k1t[hh], kk, True)
            S.dma("pool", kk.t[:, 2 * hh + 1, :], k2t[hh], kk, True)
        qpp = P7c.pool("qp", 2, [128, 2 * PH, 128], BF16)
        scp = P7c.pool("sc", 2, [128, 2 * PH, NK], F32)
        tmpp = P7c.pool("tmp", 2, [128, 256], F32)
        t16p = P7c.pool("t16", 1, [128, 2 * PH, NK], F32)
        t2p = P7c.pool("t2", 1, [128, PH, 256], F32)
        v12p = P7c.pool("v12", 2, [128, 2 * PH, 16], F32)
        idxp = P7c.pool("idx", 2, [128, PH, 16], U32)
        idfp = P7c.pool("idf", 2, [128, PH * 16], F32)
        candp = P7c.pool("cand", 1, [128, PH, 256], F32)
        tvp = P7c.pool("tv", 2, [128, PH, 16], F32)
        smp = P7c.pool("sm", 2, [128, 4, PH], F32)
        pp_ = P7c.pool("pp", 2, [128, 16, NK], F32)
        ep_ = P7c.pool("ep", 2, [128, 16, NK], F32)
        Rp = P7c.pool("R", 1, [128, 128, NK], BF16)
        Rtp = P7c.pool("Rt", 1, [128, 128, NK], BF16)
        OHp = P7c.pool("OH", 1, [128, 128, NK], BF16)
        itp = P7c.pool("it", 2, [128, 128], F32)
        for blk in range(NOWN):
            qp = qpp.next()
            S.dma("sp", qp.t[:], qpT_d[:, :, blk * 128:(blk + 1) * 128].rearrange("c p t -> p c t"), qp, True)
            sc = scp.next()
            for c4 in range(0, 2 * PH, 4):
                ps = PSA.next()
                for j in range(4):
                    S.op("pe", lambda e, ps=ps, qp=qp, c4=c4, j=j: e.matmul(
                        ps.t[:, j * 128:(j + 1) * 128], lhsT=qp.t[:, c4 + j, :], rhs=kk.t[:, c4 + j, :], start=True, stop=True),
                        reads=[qp, kk], writes=[ps])
                S.op("act", lambda e, ps=ps, sc=sc, c4=c4: e.copy(out=sc.t[:, c4:c4 + 4, :], in_=ps.t[:].rearrange("p (c n) -> p c n", n=NK)),
                     reads=[ps], writes=[sc])
            v12 = v12p.next(); idx = idxp.next(); t16 = t16p.next(); t2 = t2p.next()
            v12s = v12.subs(2 * PH); idxs = idx.subs(PH); t16s = t16.subs(2 * PH); t2s = t2.subs(PH)
            for c in range(2 * PH):
                S.op("dve", lambda e: e.max(out=v12.t[:, c, 0:8], in_=sc.t[:, c, :]), reads=[sc], writes=[v12s[c]])
            for c in range(2 * PH):
                S.op("dve", lambda e: e.match_replace(out=t16.t[:, c, :], in_to_replace=v12.t[:, c, 0:8], in_values=sc.t[:, c, :],
                                                      imm_value=-1e30), reads=[sc, v12s[c]], writes=[t16s[c]])
            for c in range(2 * PH):
                S.op("dve", lambda e: e.max(out=v12.t[:, c, 8:16], in_=t16.t[:, c, :]), reads=[t16s[c]], writes=[v12s[c]])
            for hh in range(PH):
                c = 2 * hh
                S.op("dve", lambda e: e.max_index(out=idx.t[:, hh, 0:8], in_max=v12.t[:, c, 0:8], in_values=sc.t[:, c, :]),
                     reads=[sc, v12s[c]], writes=[idxs[hh]])
            for hh in range(PH):
                c = 2 * hh
                S.op("dve", lambda e: e.max_index(out=idx.t[:, hh, 8:16], in_max=v12.t[:, c, 8:16], in_values=t16.t[:, c, :]),
                     reads=[t16s[c], v12s[c]], writes=[idxs[hh]])
            cand = candp.next(); tv = tvp.next(); sm = smp.next()
            tvs = tv.subs(PH)
            vv = v12.t[:].rearrange("p (h two) k -> p h two k", two=2)
            S.op("pool", lambda e: e.tensor_tensor(
                out=cand.t[:].rearrange("p h (a b) -> p h a b", a=16),
                in0=vv[:, :, 0, :].unsqueeze(3).to_broadcast([128, PH, 16, 16]),
                in1=vv[:, :, 1, :].unsqueeze(2).to_broadcast([128, PH, 16, 16]), op=ALU.add), reads=v12s, writes=[cand])
            for hh in range(PH):
                S.op("dve", lambda e: e.max(out=tv.t[:, hh, 0:8], in_=cand.t[:, hh, :]), reads=[cand], writes=[tvs[hh]])
            for hh in range(PH):
                S.op("dve", lambda e: e.match_replace(out=t2.t[:, hh, :], in_to_replace=tv.t[:, hh, 0:8], in_values=cand.t[:, hh, :],
                                                      imm_value=-1e30), reads=[cand, tvs[hh]], writes=[t2s[hh]])
            for hh in range(PH):
                S.op("dve", lambda e: e.max(out=tv.t[:, hh, 8:16], in_=t2.t[:, hh, :]), reads=[t2s[hh]], writes=[tvs[hh]])
            S.op("dve", lambda e, sm=sm, tv=tv: e.tensor_scalar(out=sm.t[:, 0, :], in0=tv.t[:, :, 0], scalar1=-1.0, scalar2=None, op0=ALU.mult),
                 reads=tvs, writes=[sm])
            S.op("dve", lambda e, sm=sm, tv=tv: e.tensor_copy(out=sm.t[:, 3, :], in_=tv.t[:, :, 15]), reads=tvs, writes=[sm])
            ex = tmpp.next()
            S.op("dve", lambda e, sm=sm, tv=tv, ex=ex: e.tensor_tensor(
                out=ex.t[:, 0:PH * 16].rearrange("p (h k) -> p h k", k=16), in0=tv.t[:],
                in1=sm.t[:, 0, :].unsqueeze(2).to_broadcast([128, PH, 16]), op=ALU.add), reads=tvs + [sm], writes=[ex])
            S.op("act", lambda e, ex=ex: e.activation(out=ex.t[:, 0:PH * 16], in_=ex.t[:, 0:PH * 16], func=AF.Exp), reads=[ex], writes=[ex])
            S.op("dve", lambda e, sm=sm, ex=ex: e.tensor_reduce(out=sm.t[:, 1, :], in_=ex.t[:, 0:PH * 16].rearrange("p (h k) -> p h k", k=16),
                                                               axis=AX.X, op=ALU.add), reads=[ex], writes=[sm])
            S.op("act", lambda e, sm=sm: e.activation(out=sm.t[:, 2, :], in_=sm.t[:, 1, :], func=AF.Ln), reads=[sm], writes=[sm])
            S.op("dve", lambda e, sm=sm: e.tensor_tensor(out=sm.t[:, 2, :], in0=sm.t[:, 0, :], in1=sm.t[:, 2, :], op=ALU.subtract),
                 reads=[sm], writes=[sm])
            R = Rp.next()
            Gs = R.subs(32)
            for hh in range(PH):
                pp = pp_.next(); ep = ep_.next()
                S.op("pool", lambda e, pp=pp, hh=hh: e.tensor_tensor(
                    out=pp.t[:], in0=sc.t[:, 2 * hh + 1, :].unsqueeze(1).to_broadcast([128, 16, NK]),
                    in1=v12.t[:, 2 * hh, :].unsqueeze(2).to_broadcast([128, 16, NK]), op=ALU.add), reads=[sc, v12s[2 * hh]], writes=[pp])
                S.op("act", lambda e, pp=pp, ep=ep, hh=hh, sm=sm: e.activation(out=ep.t[:], in_=pp.t[:], func=AF.Exp,
                                                                              bias=sm.t[:, 2, hh:hh + 1], scale=1.0),
                     reads=[pp, sm], writes=[ep])
                S.op("dve", lambda e, pp=pp, ep=ep, hh=hh, sm=sm, R=R: e.scalar_tensor_tensor(
                    out=R.t[:, hh * 16:(hh + 1) * 16, :], in0=pp.t[:], scalar=sm.t[:, 3, hh:hh + 1], in1=ep.t[:],
                    op0=ALU.is_ge, op1=ALU.mult), reads=[pp, ep, sm], writes=Gs)
            Rt = Rtp.next()
            Rts = Rt.subs(NK // 8)
            for i8 in range(0, NK, 8):
                pb = PSB.next()
                for j in range(8):
                    S.op("pe", lambda e, pb=pb, R=R, i8=i8, j=j: e.transpose(pb.t[:, j * 128:(j + 1) * 128], R.t[:, :, i8 + j], ident.t[:]),
                         reads=Gs + [ident], writes=[pb])
                eng = "act" if (i8 // 8) % 2 == 0 else "dve"
                outv = Rt.t[:, :, i8:i8 + 8].rearrange("p t i -> p i t")
                if eng == "act":
                    S.op("act", lambda e, pb=pb, outv=outv: e.copy(out=outv, in_=pb.t[:].rearrange("p (i t) -> p i t", t=128)), reads=[pb], writes=[Rts[i8 // 8]])
                else:
                    S.op("dve", lambda e, pb=pb, outv=outv: e.tensor_copy(out=outv, in_=pb.t[:].rearrange("p (i t) -> p i t", t=128)), reads=[pb], writes=[Rts[i8 // 8]])
            idf = idfp.next()
            S.op("dve", lambda e, idf=idf, idx=idx: e.tensor_copy(out=idf.t[:], in_=idx.t[:].rearrange("p h k -> p (h k)")), reads=idxs, writes=[idf])
            pt_ = PSA.next()
            S.op("pe", lambda e, pt_=pt_, idf=idf: e.transpose(pt_.t[:, 0:128], idf.t[:], identf.t[:]), reads=[idf, identf], writes=[pt_])
            it = itp.next()
            S.op("act", lambda e, it=it, pt_=pt_: e.copy(out=it.t[:], in_=pt_.t[:, 0:128]), reads=[pt_], writes=[it])
            OH = OHp.next()
            S.op("dve", lambda e, OH=OH, it=it: e.tensor_tensor(
                out=OH.t[:], in0=iotar.t[:].unsqueeze(1).to_broadcast([128, 128, NK]),
                in1=it.t[:].unsqueeze(2).to_broadcast([128, 128, NK]), op=ALU.is_equal), reads=[iotar, it], writes=[OH])
            G = R
            for t4 in range(0, 128, 4):
                ps = PSA.next()
                for j in range(4):
                    S.op("pe", lambda e, ps=ps, t4=t4, j=j: e.matmul(ps.t[:, j * 128:(j + 1) * 128], lhsT=Rt.t[:, t4 + j, :], rhs=OH.t[:, t4 + j, :],
                                                                      start=True, stop=True), reads=Rts + [OH], writes=[ps])
                outv = G.t[:, :, t4:t4 + 4].rearrange("p i t -> p t i")
                if (t4 // 4) % 2 == 0:
                    S.op("act", lambda e, ps=ps, outv=outv: e.copy(out=outv, in_=ps.t[:].rearrange("p (t i) -> p t i", i=NK)), reads=[ps], writes=[Gs[t4 // 4]])
                else:
                    S.op("dve", lambda e, ps=ps, outv=outv: e.tensor_copy(out=outv, in_=ps.t[:].rearrange("p (t i) -> p t i", i=NK)), reads=[ps], writes=[Gs[t4 // 4]])
            for i16 in range(0, NK, 16):
                S.dma("sp", G_d[i16:i16 + 16, :, blk * 128:(blk + 1) * 128].rearrange("i p t -> p i t"), G.t[:, i16:i16 + 16, :], G, False, deps=Gs)
        P7c.close()

        TT8 = min(NOWN, 4)
        S.cur_phase = "p8"
        P8 = Phase()
        xT8 = P8.sb("xT8", [128, KC, TT8 * 128], BF16)
        yacc = P8.sb("yacc", [128, TT8, D], F32)
        utp = P8.pool("ut", 2, [128, KC, 256], BF16)
        vtp = P8.pool("vt", 6, [128, D], BF16)
        gp8 = P8.pool("g8", 4, [128, TT8 * 128], BF16)
        gep = P8.pool("ge", 3, [128, TT8 * 128], F32)
        atp = P8.pool("at", 6, [128, TT8 * 128], BF16)
        for b0 in range(0, NOWN, TT8):
            nb = min(TT8, NOWN - b0)
            ntok = nb * 128
            load_xT(xT8, xnT_d, b0, nb)
            ND8 = D // 512
            yss = yacc.subs(TT8 * ND8)
            for tb in range(nb):
                S.dma("sp", yacc.t[:, tb, :], h1_d[(b0 + tb) * 128:(b0 + tb + 1) * 128, :], yacc, True, deps=yss[tb * ND8:(tb + 1) * ND8])
            def p8load(grp):
                ut = utp.next()
                for k4 in range(0, KC, 8):
                    k5 = min(KC, k4 + 8)
                    S.dma("pool", ut.t[:, k4:k5, :], ut_d[grp, :, k4:k5, :], ut, True)
                vts_ = []
                g8s_ = []
                for cl in range(2):
                    c = grp * 2 + cl
                    vt = vtp.next()
                    for d4 in range(0, D, 2048):
                        d5 = min(D, d4 + 2048)
                        S.dma("pool", vt.t[:, d4:d5], pv_d[c * 128:(c + 1) * 128, d4:d5], vt, True)
                    g8 = gp8.next()
                    S.dma("sp", g8.t[:, 0:ntok], G_d[c, :, b0 * 128:b0 * 128 + ntok], g8, True)
                    vts_.append(vt); g8s_.append(g8)
                return ut, vts_, g8s_
            def second8(ats, vts):
                for tb in range(nb):
                    for dt in range(D // 512):
                        ps = PSA.next()
                        for cl in range(2):
                            S.op("pe", lambda e: e.matmul(ps.t[:], lhsT=ats[cl].t[:, tb * 128:(tb + 1) * 128],
                                                          rhs=vts[cl].t[:, dt * 512:(dt + 1) * 512], start=(cl == 0), stop=(cl == 1)),
                                 reads=[ats[cl], vts[cl]], writes=[ps])
                        S.op("dve", lambda e: e.tensor_tensor(
                            out=yacc.t[:, tb, dt * 512:(dt + 1) * 512], in0=ps.t[:], in1=yacc.t[:, tb, dt * 512:(dt + 1) * 512], op=ALU.add),
                            reads=[ps, yss[tb * ND8 + dt]], writes=[yss[tb * ND8 + dt]])
            prev8 = None
            nx8 = p8load(0)
            for grp in range(NE // 256):
                ut, vts, g8s = nx8
                if grp + 1 < NE // 256:
                    nx8 = p8load(grp + 1)
                ats = []
                for cl in range(2):
                    g8 = g8s[cl]
                    ps = PSA.next()
                    for kc in range(KC):
                        S.op("pe", lambda e, ps=ps, ut=ut, kc=kc, cl=cl, ntok=ntok: e.matmul(
                            ps.t[:, 0:ntok], lhsT=ut.t[:, kc, cl * 128:(cl + 1) * 128], rhs=xT8.t[:, kc, 0:ntok],
                            start=(kc == 0), stop=(kc == KC - 1)), reads=[ut, xT8], writes=[ps])
                    ge = gep.next(); at = atp.next()
                    S.op("act", lambda e, ps=ps, ge=ge, ntok=ntok: e.activation(out=ge.t[:, 0:ntok], in_=ps.t[:, 0:ntok], func=AF.Gelu),
                         reads=[ps], writes=[ge])
                    S.op("dve", lambda e, ge=ge, g8=g8, at=at, ntok=ntok: e.tensor_tensor(out=at.t[:, 0:ntok], in0=ge.t[:, 0:ntok], in1=g8.t[:, 0:ntok], op=ALU.mult),
                         reads=[ge, g8], writes=[at])
                    ats.append(at)
                if prev8 is not None:
                    second8(*prev8)
                prev8 = (ats, vts)
            second8(*prev8)
            for tb in range(nb):
                S.dma("sp", out_d[(b0 + tb) * 128:(b0 + tb + 1) * 128, :], yacc.t[:, tb, :], yacc, False, deps=yss[tb * ND8:(tb + 1) * ND8])
        P8.close()
        P0.close()
        S.emit()
    return nc


_CACHE = {}


def kernel(x, meta_tokens, norm_mix_g, w_in, b_forget, q_norm_g, k_norm_g, ret_norm_g, w_proj_fox, w_proj_ret,
           w_out, norm_ffn_g, peer_w_q, peer_keys_1, peer_keys_2, peer_u, peer_v, _dbg=False):
    f = lambda a: np.ascontiguousarray(np.asarray(a, dtype=np.float32))
    x = f(x)
    B, SEQ, D = x.shape
    NB = SEQ // 128
    NOWN = NB // 4
    NCTX = 1 + NB
    T = NCTX * 128
    KC = D // 128
    key = (D, NCTX, NOWN)
    key = (D, NCTX, NOWN, _dbg)
    if key not in _CACHE:
        _CACHE[key] = build(D, NCTX, NOWN, _dbg)
    nc = _CACHE[key]
    meta = f(meta_tokens)
    pu = f(peer_u)[0]
    NE = pu.shape[0]
    ut = np.ascontiguousarray(pu.reshape(NE // 256, 256, KC, 128).transpose(0, 3, 2, 1))
    shared = {
        "norm_mix_g": f(norm_mix_g)[0], "w_in": f(w_in)[0], "b_forget": f(b_forget)[0], "q_norm_g": f(q_norm_g)[0],
        "k_norm_g": f(k_norm_g)[0], "ret_norm_g": f(ret_norm_g)[0], "w_proj_fox": f(w_proj_fox)[0],
        "w_proj_ret": f(w_proj_ret)[0], "w_out": f(w_out)[0], "norm_ffn_g": f(norm_ffn_g)[0],
        "peer_w_q": f(peer_w_q)[0],
        "k1t": np.ascontiguousarray(f(peer_keys_1)[0].transpose(0, 2, 1)),
        "k2t": np.ascontiguousarray(f(peer_keys_2)[0].transpose(0, 2, 1)),
        "ut": ut, "peer_v": f(peer_v)[0],
    }
    gam = 1.0 - 2.0 ** (-5.0 - np.arange(H, dtype=np.float64))
    inv = ROPE_BASE ** (-np.arange(64, dtype=np.float64) / 64)
    in_maps = []
    for c in range(8):
        b, g = c // 4, c % 4
        ndum = (3 - g) * NOWN * 128
        nprev = g * NOWN * 128
        ctx = np.zeros((T, D), np.float32)
        ctx[ndum + PAD:ndum + 128] = meta
        ctx[ndum + 128:ndum + 128 + nprev] = x[b, :nprev]
        ctx[T - NOWN * 128:] = x[b, nprev:nprev + NOWN * 128]
        n = np.arange(T) - ndum
        valid = (n >= PAD).astype(np.float32)
        pos = (n - PAD).astype(np.float64)
        ang = pos[:, None] * inv[None, :]
        cossin = np.concatenate([np.cos(ang), np.sin(ang)], axis=1).astype(np.float32)
        start = 128 + nprev
        lpos = (n - start).astype(np.float64)
        own = n >= start
        ktab = np.zeros((T, H), np.float64)
        with np.errstate(over="ignore", under="ignore"):
            ktab[~own] = np.exp(np.log(gam)[None, :] * (start - 1 - n[~own])[:, None])
            ktab[own] = np.exp(-np.log(gam)[None, :] * lpos[own][:, None])
            qtab = np.exp(np.log(gam)[None, :] * lpos[own][:, None])
        ktab = ktab * (128.0 ** -0.5) * valid[:, None]
        m = dict(shared)
        m.update({"ctx_x": ctx, "valid": np.ascontiguousarray(valid.reshape(NCTX, 128).T), "cossin": cossin, "ktab": ktab.astype(np.float32),
                  "qtab": qtab.astype(np.float32)})
        in_maps.append(m)
    res = run_bass_kernel_spmd(nc, in_maps, core_ids=list(range(8)))
    if _dbg:
        return res.results, in_maps
    out = np.zeros((B, SEQ, D), np.float32)
    for c in range(8):
        b, g = c // 4, c % 4
        out[b, g * NOWN * 128:(g + 1) * NOWN * 128] = res.results[c]["out"]
    return out
```

```python
import numpy as np
from contextlib import ExitStack
import concourse.bass as bass
import concourse.mybir as mybir
from concourse.bass_utils import run_bass_kernel_spmd

F32 = mybir.dt.float32
BF16 = mybir.dt.bfloat16
U32 = mybir.dt.uint32
AF = mybir.ActivationFunctionType
ALU = mybir.AluOpType
AX = mybir.AxisListType

N_META = 16
PAD = 112
EPS = 1e-6
H = 16
HD = 128
DV = 256
PH = 8
NK = 128
TOPK = 16
ROPE_BASE = 10000.0


class Res:
    __slots__ = ("w", "r", "name")

    def __init__(self, name=""):
        self.w = None
        self.r = {}
        self.name = name


class DSem:
    __slots__ = ("sem", "tot")


class Ins:
    __slots__ = ("eng", "fn", "waits", "sig", "sigval", "dsem", "dval", "idx", "ph")


class Tile:
    __slots__ = ("t", "res", "ds", "_subs")

    def __init__(self, t, name):
        self.t = t
        self.res = Res(name)
        self.ds = None
        self._subs = None

    def subs(self, n):
        if self._subs is None:
            self._subs = [Tile(self.t, "%s.%d" % (self.res.name, i)) for i in range(n)]
        return self._subs


class Pool:
    def __init__(self, tiles):
        self.tiles = tiles
        self.i = 0

    def next(self):
        t = self.tiles[self.i % len(self.tiles)]
        self.i += 1
        return t


class _Rec:
    def __getattr__(self, name):
        def f(*a, **k):
            self.call = (name, a, k)
            return self
        return f


class Sched:
    ENGS = ("pe", "act", "dve", "pool", "sp")

    def __init__(self, nc, stack):
        self.nc = nc
        self.stack = stack
        self.lists = {e: [] for e in self.ENGS}
        self.esem = {e: stack.enter_context(nc.semaphore("es_" + e)) for e in self.ENGS}
        self.dsems = []
        self.free_ds = []
        self.lastc = {e: None for e in self.ENGS}
        self.cur_phase = "p0"
        self.scopes = False

    def get_ds(self):
        if self.free_ds:
            return self.free_ds.pop()
        d = DSem()
        d.sem = self.stack.enter_context(self.nc.semaphore("ds%d" % len(self.dsems)))
        d.tot = 0
        self.dsems.append(d)
        return d

    def op(self, eng, fn, reads=(), writes=(), dsem=None):
        rec = _Rec()
        fn(rec)
        name, a, k = rec.call
        ins = Ins()
        ins.eng = eng
        ins.fn = lambda e: getattr(e, name)(*a, **k)
        ins.sig = False
        ins.sigval = 0
        ins.dsem = dsem
        ins.idx = len(self.lists[eng])
        ins.ph = self.cur_phase
        deps = {}

        def add(d):
            if d is None:
                return
            if d.dsem is not None:
                deps[("d", id(d.dsem))] = d
            else:
                if d.eng == eng and eng == "pe" and dsem is None:
                    return
                k = ("c", d.eng)
                if k not in deps or deps[k].idx < d.idx:
                    deps[k] = d
        for t in reads:
            add(t.res.w)
        for t in writes:
            add(t.res.w)
            for d in t.res.r.values():
                add(d)
        waits = []
        for d in deps.values():
            if d.dsem is not None:
                waits.append((d.dsem, d.dsem.tot))
            else:
                d.sig = True
                waits.append(d)
        ins.waits = waits
        if dsem is not None:
            dsem.tot += 16
            ins.dval = dsem.tot
        else:
            self.lastc[eng] = ins
        key = ("d", id(dsem)) if dsem is not None else ("c", eng)
        for t in reads:
            t.res.r[key] = ins
        for t in writes:
            t.res.w = ins
            t.res.r = {}
        self.lists[eng].append(ins)
        return ins

    def dma(self, q, out, in_, tile, load, deps=None):
        if tile.ds is None:
            tile.ds = self.get_ds()
        fn = lambda e: e.dma_start(out=out, in_=in_)
        dl = [tile] if deps is None else list(deps)
        if load:
            return self.op(q, fn, writes=dl, dsem=tile.ds)
        return self.op(q, fn, reads=dl, dsem=tile.ds)

    def barrier(self, tiles=()):
        last = [self.lastc[e] for e in self.ENGS if self.lastc[e] is not None]
        dtot = [(d, d.tot) for d in self.dsems if d.tot > 0]
        for d in last:
            d.sig = True
        for e in self.ENGS:
            ins = self.op(e, lambda eng: eng.nop())
            ins.waits = list(last) + list(dtot)
        for t in tiles:
            if t.ds is not None:
                self.free_ds.append(t.ds)
                t.ds = None

    def emit(self):
        nc = self.nc
        for e in self.ENGS:
            c = 0
            for ins in self.lists[e]:
                if ins.dsem is None and ins.sig:
                    c += 1
                    ins.sigval = c
        with nc.Block() as block:
            def run(e):
                def body(eng):
                    waited = {}
                    cur = None
                    for ins in self.lists[e]:
                        if self.scopes and ins.ph != cur:
                            if cur is not None:
                                scope.__exit__(None, None, None)
                            scope = nc.named_scope(ins.ph)
                            scope.__enter__()
                            cur = ins.ph
                        for w in ins.waits:
                            if isinstance(w, tuple):
                                sem, val = w[0].sem, w[1]
                            else:
                                sem, val = self.esem[w.eng], w.sigval
                            k = id(sem)
                            if waited.get(k, 0) >= val:
                                continue
                            waited[k] = val
                            eng.wait_ge(sem, val)
                        bi = ins.fn(eng)
                        if ins.dsem is not None:
                            bi.then_inc(ins.dsem.sem, 16)
                        elif ins.sig:
                            bi.then_inc(self.esem[e], 1)
                    if e == "sp":
                        for d in self.dsems:
                            if d.tot > 0:
                                eng.wait_ge(d.sem, d.tot)
                    if cur is not None:
                        scope.__exit__(None, None, None)
                return body
            block.tensor(run("pe"))
            block.scalar(run("act"))
            block.vector(run("dve"))
            block.gpsimd(run("pool"))
            block.sync(run("sp"))


def build(D, NCTX, NOWN, dbg=False, scopes=False):
    KC = D // 128
    T = NCTX * 128
    TO = NOWN * 128
    OWN0 = NCTX - NOWN
    NE = NK * NK
    nc = bass.Bass("TRN2", target_bir_lowering=False)

    def din(name, shape, dt=F32):
        return nc.dram_tensor(name, shape, dt, kind="ExternalInput").ap()

    def dscr(name, shape, dt=BF16):
        return nc.dram_tensor(name, shape, dt, kind="ExternalOutput" if dbg else "Internal").ap()

    ctx_x = din("ctx_x", [T, D])
    valid_d = din("valid", [128, NCTX])
    cs_d = din("cossin", [T, 128])
    ktab_d = din("ktab", [T, H])
    qtab_d = din("qtab", [TO, H])
    gmix_d = din("norm_mix_g", [D])
    w_in = din("w_in", [D, 26640 - 8192 + 2 * D])
    bf_d = din("b_forget", [H])
    qg_d = din("q_norm_g", [HD])
    kg_d = din("k_norm_g", [HD])
    rg_d = din("ret_norm_g", [H * DV])
    wpf = din("w_proj_fox", [H * HD, D])
    wpr = din("w_proj_ret", [H * DV, D])
    wout = din("w_out", [D, D])
    gffn_d = din("norm_ffn_g", [D])
    wq = din("peer_w_q", [D, PH * 256])
    k1t = din("k1t", [PH, 128, NK])
    k2t = din("k2t", [PH, 128, NK])
    ut_d = din("ut", [NE // 256, 128, KC, 256])
    pv_d = din("peer_v", [NE, D])
    out_d = nc.dram_tensor("out", [TO, D], F32, kind="ExternalOutput").ap()

    uT_d = dscr("uT", [128, KC, T])
    kT_d = dscr("kT", [H, 128, T])
    vf_d = dscr("vf", [T, H, 129])
    krw_d = dscr("krw", [T, H, 128])
    krT_d = dscr("krT", [H, 128, TO])
    vr_d = dscr("vr", [T, H * DV])
    qT_d = dscr("qT", [H, 128, TO])
    qrT_d = dscr("qrT", [H, 128, TO])
    sg_d = dscr("sg", [TO, H * DV])
    ga_d = dscr("ga", [TO, D])
    gr_d = dscr("gr", [TO, D])
    ya_d = dscr("ya", [TO, H * HD])
    yr_d = dscr("yr", [TO, H * DV])
    h1_d = dscr("h1", [TO, D], F32)
    xnT_d = dscr("xnT", [128, KC, TO])
    qpT_d = dscr("qpT", [2 * PH, 128, TO])
    G_d = dscr("G", [NK, 128, TO])
    oraw_d = dscr("oraw", [H, TO, DV], F32) if dbg else None
    stt_d = dscr("sttd", [H, 128, DV], BF16) if dbg else None

    c_qa, c_ka, c_va, c_fa = 0, 2048, 4096, 6144
    c_qr = 6160
    c_kr = c_qr + 2048
    c_vr = c_kr + 2048
    c_gr = c_vr + 4096
    c_ga = c_gr + 4096
    c_gtr = c_ga + D

    with ExitStack() as st:
        S = Sched(nc, st)
        S.scopes = scopes
        psf = [Tile(st.enter_context(nc.psum_tensor("psf%d" % i, [128, 512], F32)), "psf%d" % i) for i in range(6)]
        psb = [Tile(st.enter_context(nc.psum_tensor("psb%d" % i, [128, 1024], BF16)), "psb%d" % i) for i in range(2)]
        PSA = Pool(psf[0:4])
        PSACC = Pool(psf[4:6])
        PSB = Pool(psb)

        cnt = [0]

        class Phase:
            def __init__(self):
                self.st = ExitStack()
                self.tiles = []

            def sb(self, name, shape, dt):
                cnt[0] += 1
                name = "s%d_%s" % (cnt[0], name)
                t = Tile(self.st.enter_context(nc.sbuf_tensor(name, shape, dt)), name)
                self.tiles.append(t)
                return t

            def pool(self, name, n, shape, dt):
                return Pool([self.sb("%s%d" % (name, i), shape, dt) for i in range(n)])

            def close(self):
                S.barrier(self.tiles + psf + psb)
                self.st.close()

        P0 = Phase()
        ident = P0.sb("ident", [128, 128], BF16)
        identf = P0.sb("identf", [128, 128], F32)
        tri = P0.sb("tri", [128, 128], BF16)
        trif = P0.sb("trif", [128, 128], F32)
        onesf = P0.sb("onesf", [128, 128], F32)
        iotar = P0.sb("iotar", [128, 128], F32)
        ctmp = P0.sb("ctmp", [128, 128], F32)
        mhalf = P0.sb("mhalf", [128, 8], F32)
        S.op("pool", lambda e: e.memset(mhalf.t[:], -0.5), writes=[mhalf])
        PA = Phase()
        valid = PA.sb("valid", [128, NCTX], F32)
        ktab = PA.sb("ktab", [128, NCTX, H], F32)
        qtab = PA.sb("qtab", [128, NOWN, H], F32)
        Lfull = PA.sb("Lfull", [128, NCTX, H], F32)
        Lb = PA.sb("Lb", [128, NCTX + 1, H], F32)
        S.op("pool", lambda e: e.iota(ctmp.t[:], pattern=[[1, 128]], base=0, channel_multiplier=-1,
                                      allow_small_or_imprecise_dtypes=True), writes=[ctmp])
        S.op("dve", lambda e: e.tensor_single_scalar(out=identf.t[:], in_=ctmp.t[:], scalar=0.0, op=ALU.is_equal),
             reads=[ctmp], writes=[identf])
        S.op("dve", lambda e: e.tensor_copy(out=ident.t[:], in_=identf.t[:]), reads=[identf], writes=[ident])
        S.op("dve", lambda e: e.tensor_single_scalar(out=trif.t[:], in_=ctmp.t[:], scalar=0.0, op=ALU.is_ge),
             reads=[ctmp], writes=[trif])
        S.op("dve", lambda e: e.tensor_copy(out=tri.t[:], in_=trif.t[:]), reads=[trif], writes=[tri])
        S.op("pool", lambda e: e.memset(onesf.t[:], 1.0), writes=[onesf])
        S.op("pool", lambda e: e.iota(iotar.t[:], pattern=[[1, 128]], base=0, channel_multiplier=0,
                                      allow_small_or_imprecise_dtypes=True), writes=[iotar])
        S.op("pool", lambda e: e.memset(Lb.t[:, 0, :], 0.0), writes=[Lb])
        S.dma("sp", valid.t[:], valid_d, valid, True)
        S.dma("sp", ktab.t[:], ktab_d.rearrange("(b p) h -> p b h", p=128), ktab, True)
        S.dma("sp", qtab.t[:], qtab_d.rearrange("(b p) h -> p b h", p=128), qtab, True)

        def bcast_load(ph, name, src, n, scale=None):
            t = ph.sb(name, [128, n], F32)
            S.dma("sp", t.t[:], src.partition_broadcast(128), t, True)
            if scale is not None:
                S.op("dve", lambda e: e.tensor_scalar(out=t.t[:], in0=t.t[:], scalar1=float(scale), scalar2=None,
                                                      op0=ALU.mult), reads=[t], writes=[t])
            return t

        def rmsnorm_T(ph, nblk, src_rows, g_bc, dstT, xpool, jpool, npool, opool):
            ss = ph.pool("ss", 2, [128, 2], F32)
            xq = []

            def xload(b):
                xt_ = xpool.next()
                S.dma("sp", xt_.t[:], src_rows(b), xt_, True)
                xq.append(xt_)
            for b in range(min(2, nblk)):
                xload(b)
            for blk in range(nblk):
                xt = xq[blk]
                if blk + 2 < nblk:
                    xload(blk + 2)
                jk = jpool.next()
                s1 = ss.next()
                S.op("pool", lambda e, s1=s1: e.memset(s1.t[:], 0.0), writes=[s1])
                S.op("act", lambda e, xt=xt, jk=jk, s1=s1: e.activation(out=jk.t[:], in_=xt.t[:], func=AF.Square,
                                                                         accum_out=s1.t[:, 0:1]),
                     reads=[xt], writes=[jk, s1])
                S.op("dve", lambda e, s1=s1: e.tensor_scalar(out=s1.t[:, 1:2], in0=s1.t[:, 0:1], scalar1=1.0 / D,
                                                              scalar2=EPS, op0=ALU.mult, op1=ALU.add),
                     reads=[s1], writes=[s1])
                S.op("pool", lambda e, s1=s1: e.tensor_tensor(out=s1.t[:, 1:2], in0=s1.t[:, 1:2], in1=mhalf.t[:, 0:1], op=ALU.pow),
                     reads=[s1, mhalf], writes=[s1])
                xn = npool.next()
                S.op("dve", lambda e, xt=xt, xn=xn, s1=s1: e.scalar_tensor_tensor(
                    out=xn.t[:], in0=xt.t[:], scalar=s1.t[:, 1:2], in1=g_bc.t[:], op0=ALU.mult, op1=ALU.mult),
                    reads=[xt, s1, g_bc], writes=[xn])
                ot = opool.next()
                for k8 in range(0, KC, 8):
                    pb = PSB.next()
                    n8 = min(8, KC - k8)
                    for j in range(n8):
                        S.op("pe", lambda e, pb=pb, xn=xn, j=j, k8=k8: e.transpose(
                            pb.t[:, j * 128:(j + 1) * 128], xn.t[:, (k8 + j) * 128:(k8 + j + 1) * 128], ident.t[:]),
                            reads=[xn, ident], writes=[pb])
                    eng = "act" if (k8 // 8) % 2 == 0 else "dve"
                    if eng == "act":
                        S.op("act", lambda e, pb=pb, ot=ot, k8=k8, n8=n8: e.copy(
                            out=ot.t[:, k8:k8 + n8, :], in_=pb.t[:, 0:n8 * 128].rearrange("p (k t) -> p k t", t=128)),
                            reads=[pb], writes=[ot])
                    else:
                        S.op("dve", lambda e, pb=pb, ot=ot, k8=k8, n8=n8: e.tensor_copy(
                            out=ot.t[:, k8:k8 + n8, :], in_=pb.t[:, 0:n8 * 128].rearrange("p (k t) -> p k t", t=128)),
                            reads=[pb], writes=[ot])
                S.dma("sp", dstT[:, :, blk * 128:(blk + 1) * 128], ot.t[:], ot, False)

        S.cur_phase = "p1"
        P1 = Phase()
        g_bc = bcast_load(P1, "gmix", gmix_d, D)
        rmsnorm_T(P1, NCTX, lambda blk: ctx_x[blk * 128:(blk + 1) * 128, :], g_bc, uT_d,
                  P1.pool("x", 3, [128, D], F32), P1.pool("jk", 1, [128, D], BF16),
                  P1.pool("xn", 2, [128, D], BF16), P1.pool("uo", 2, [128, KC, 128], BF16))
        P1.close()

        def gemm_multi(xT, nblk, kcs, wpool, jobs):
            nkc = len(kcs)
            tiles = []
            for (W, N, evac) in jobs:
                for nt in range((N + 511) // 512):
                    tiles.append((W, nt, min(512, N - nt * 512), evac))

            def load(ti):
                W, nt, nw, _ = tiles[ti]
                wt = wpool.next()
                n0 = nt * 512
                for k4 in range(0, nkc, 8):
                    k5 = min(nkc, k4 + 8)
                    S.dma("pool", wt.t[:, k4:k5, 0:nw],
                          W[k4 * 128:k5 * 128, n0:n0 + nw].rearrange("(kc p) n -> p kc n", p=128), wt, True)
                return wt
            nxt = load(0)
            pending = None
            for ti, (W, nt, nw, evac) in enumerate(tiles):
                wt = nxt
                if ti + 1 < len(tiles):
                    nxt = load(ti + 1)
                for tb in range(nblk):
                    ps = PSA.next()
                    for i, kc in enumerate(kcs):
                        S.op("pe", lambda e, ps=ps, wt=wt, i=i, kc=kc, tb=tb, nw=nw: e.matmul(
                            ps.t[:, 0:nw], lhsT=xT.t[:, kc, tb * 128:(tb + 1) * 128], rhs=wt.t[:, i, 0:nw],
                            start=(i == 0), stop=(i == nkc - 1)), reads=[xT, wt], writes=[ps])
                    if pending is not None:
                        pending()
                    pending = evac(tb, nt, ps, nw)
            if pending is not None:
                pending()

        def gemm(xT, nblk, kcs, W, N, wpool, evac, pre=None):
            gemm_multi(xT, nblk, kcs, wpool, [(W, N, evac)])

        def load_xT(xT, srcT, b0, nb):
            for k4 in range(0, KC, 8):
                k5 = min(KC, k4 + 8)
                S.dma("sp", xT.t[:, k4:k5, 0:nb * 128], srcT[:, k4:k5, b0 * 128:(b0 + nb) * 128], xT, True)

        def qknorm_T(ph, ps, g_t, dst, h0, col0, ssp, knp, ktp):
            s4 = ssp.next()
            kn = knp.next()
            S.op("pool", lambda e: e.memset(s4.t[:], 0.0), writes=[s4])
            for h in range(4):
                S.op("act", lambda e, h=h: e.activation(out=kn.t[:, h * 128:(h + 1) * 128], in_=ps.t[:, h * 128:(h + 1) * 128],
                                                         func=AF.Square, accum_out=s4.t[:, h:h + 1]),
                     reads=[ps], writes=[kn, s4])
            S.op("dve", lambda e: e.tensor_scalar(out=s4.t[:, 4:8], in0=s4.t[:, 0:4], scalar1=1.0 / HD, scalar2=EPS,
                                                  op0=ALU.mult, op1=ALU.add), reads=[s4], writes=[s4])
            S.op("pool", lambda e: e.tensor_tensor(out=s4.t[:, 4:8], in0=s4.t[:, 4:8], in1=mhalf.t[:, 0:4], op=ALU.pow),
                 reads=[s4, mhalf], writes=[s4])
            for h in range(4):
                S.op("dve", lambda e, h=h: e.scalar_tensor_tensor(
                    out=kn.t[:, h * 128:(h + 1) * 128], in0=ps.t[:, h * 128:(h + 1) * 128], scalar=s4.t[:, 4 + h:5 + h],
                    in1=g_t.t[:], op0=ALU.mult, op1=ALU.mult), reads=[ps, s4, g_t], writes=[kn])
            return lambda: transpose4(kn, dst, h0, col0, ktp)

        def transpose4(kn, dst, h0, col0, ktp):
            pb = PSB.next()
            for h in range(4):
                S.op("pe", lambda e, h=h: e.transpose(pb.t[:, h * 128:(h + 1) * 128], kn.t[:, h * 128:(h + 1) * 128],
                                                      ident.t[:]), reads=[kn, ident], writes=[pb])
            kt = ktp.next()
            S.op("act", lambda e: e.copy(out=kt.t[:], in_=pb.t[:, 0:512]), reads=[pb], writes=[kt])
            S.dma("sp", dst[h0:h0 + 4, :, col0:col0 + 128].rearrange("h p t -> p h t"),
                  kt.t[:].rearrange("p (h t) -> p h t", t=128), kt, False)

        def rotary(ph, ps, cs, tabcol, ro_p, kn_p):
            ro = ro_p.next()
            pv = ps.t[:, 0:512].rearrange("p (h two f) -> p h two f", h=4, two=2)
            rv = ro.t[:].rearrange("p (a h f) -> p a h f", a=4, h=4)
            cosb = cs.t[:, 0:64].unsqueeze(1).to_broadcast([128, 4, 64])
            sinb = cs.t[:, 64:128].unsqueeze(1).to_broadcast([128, 4, 64])
            S.op("dve", lambda e: e.tensor_tensor(out=rv[:, 0], in0=pv[:, :, 0, :], in1=cosb, op=ALU.mult), reads=[ps, cs], writes=[ro])
            S.op("dve", lambda e: e.tensor_tensor(out=rv[:, 1], in0=pv[:, :, 1, :], in1=sinb, op=ALU.mult), reads=[ps, cs], writes=[ro])
            S.op("dve", lambda e: e.tensor_tensor(out=rv[:, 2], in0=pv[:, :, 0, :], in1=sinb, op=ALU.mult), reads=[ps, cs], writes=[ro])
            S.op("dve", lambda e: e.tensor_tensor(out=rv[:, 3], in0=pv[:, :, 1, :], in1=cosb, op=ALU.mult), reads=[ps, cs], writes=[ro])
            S.op("pool", lambda e: e.tensor_tensor(out=rv[:, 0], in0=rv[:, 0], in1=rv[:, 1], op=ALU.subtract), reads=[ro], writes=[ro])
            S.op("pool", lambda e: e.tensor_tensor(out=rv[:, 2], in0=rv[:, 2], in1=rv[:, 3], op=ALU.add), reads=[ro], writes=[ro])
            kn = kn_p.next()
            kv = kn.t[:].rearrange("p (h two f) -> p h two f", h=4, two=2)
            tb_ = tabcol.unsqueeze(2).to_broadcast([128, 4, 64])
            S.op("dve", lambda e: e.tensor_tensor(out=kv[:, :, 0, :], in0=rv[:, 0], in1=tb_, op=ALU.mult), reads=[ro, ktab, qtab], writes=[kn])
            S.op("dve", lambda e: e.tensor_tensor(out=kv[:, :, 1, :], in0=rv[:, 2], in1=tb_, op=ALU.mult), reads=[ro, ktab, qtab], writes=[kn])
            return kn

        TT2 = min(NCTX, 11)
        S.cur_phase = "p2"
        P2 = Phase()
        xT2 = P2.sb("xT2", [128, KC, TT2 * 128], BF16)
        wp2 = P2.pool("w", 2, [128, KC, 512], BF16)
        kg_bc = bcast_load(P2, "kg", kg_d, HD)
        bf_bc = bcast_load(P2, "bfb", bf_d, H)
        ss4 = P2.pool("s4", 3, [128, 8], F32)
        knp = P2.pool("kn", 4, [128, 512], BF16)
        ktp = P2.pool("kt", 3, [128, 512], BF16)
        vsp = P2.pool("vs", 3, [128, 4, 129], BF16)
        rop = P2.pool("ro", 2, [128, 1024], F32)
        csp = P2.pool("cs", 3, [128, 128], F32)
        fz = P2.pool("fz", 2, [128, 48], F32)
        for b0 in range(0, NCTX, TT2):
            nb = min(TT2, NCTX - b0)
            load_xT(xT2, uT_d, b0, nb)

            def ev_ka(tb, nt, ps, nw, b0=b0):
                return qknorm_T(P2, ps, kg_bc, kT_d, nt * 4, (b0 + tb) * 128, ss4, knp, ktp)

            def ev_va(tb, nt, ps, nw, b0=b0):
                blk = b0 + tb
                vs = vsp.next()
                S.op("dve", lambda e: e.tensor_scalar(out=vs.t[:, :, 0:128], in0=ps.t[:, 0:512].rearrange("p (h f) -> p h f", h=4),
                                                      scalar1=valid.t[:, blk:blk + 1], scalar2=None, op0=ALU.mult),
                     reads=[ps, valid], writes=[vs])
                S.op("pool", lambda e: e.tensor_copy(out=vs.t[:, :, 128:129],
                                                     in_=valid.t[:, blk:blk + 1].unsqueeze(1).to_broadcast([128, 4, 1])),
                     reads=[valid], writes=[vs])
                S.dma("sp", vf_d[blk * 128:(blk + 1) * 128, nt * 4:nt * 4 + 4, :], vs.t[:], vs, False)

            def ev_fa(tb, nt, ps, nw, b0=b0):
                blk = b0 + tb
                z = fz.next()
                S.op("dve", lambda e: e.tensor_tensor(out=z.t[:, 0:16], in0=ps.t[:, 0:16], in1=bf_bc.t[:], op=ALU.add),
                     reads=[ps, bf_bc], writes=[z])
                S.op("act", lambda e: e.activation(out=z.t[:, 16:32], in_=z.t[:, 0:16], func=AF.Exp, scale=-1.0), reads=[z], writes=[z])
                S.op("act", lambda e: e.activation(out=z.t[:, 32:48], in_=z.t[:, 16:32], func=AF.Ln, bias=1.0), reads=[z], writes=[z])
                p2 = PSA.next()
                S.op("pe", lambda e: e.matmul(p2.t[:, 0:16], lhsT=trif.t[:], rhs=z.t[:, 32:48], start=True, stop=True),
                     reads=[trif, z], writes=[p2])
                S.op("pe", lambda e: e.matmul(p2.t[:, 16:32], lhsT=onesf.t[:], rhs=z.t[:, 32:48], start=True, stop=True),
                     reads=[onesf, z], writes=[p2])
                S.op("dve", lambda e: e.tensor_tensor(out=Lfull.t[:, blk, :], in0=p2.t[:, 0:16], in1=Lb.t[:, blk, :], op=ALU.add),
                     reads=[p2, Lb], writes=[Lfull])
                S.op("dve", lambda e: e.tensor_tensor(out=Lb.t[:, blk + 1, :], in0=p2.t[:, 16:32], in1=Lb.t[:, blk, :], op=ALU.add),
                     reads=[p2, Lb], writes=[Lb])

            def ev_kr(tb, nt, ps, nw, b0=b0):
                blk = b0 + tb
                cs = csp.next()
                S.dma("sp", cs.t[:], cs_d[blk * 128:(blk + 1) * 128, :], cs, True)
                kn = rotary(P2, ps, cs, ktab.t[:, blk, nt * 4:nt * 4 + 4], rop, knp)
                if blk < OWN0:
                    S.dma("sp", krw_d[blk * 128:(blk + 1) * 128, nt * 4:nt * 4 + 4, :],
                          kn.t[:].rearrange("p (h f) -> p h f", h=4), kn, False)
                else:
                    return lambda: transpose4(kn, krT_d, nt * 4, (blk - OWN0) * 128, ktp)

            def ev_vr(tb, nt, ps, nw, b0=b0):
                blk = b0 + tb
                vs = knp.next()
                S.op("act", lambda e: e.activation(out=vs.t[:], in_=ps.t[:, 0:512], func=AF.Copy,
                                                   scale=valid.t[:, blk:blk + 1]), reads=[ps, valid], writes=[vs])
                S.dma("sp", vr_d[blk * 128:(blk + 1) * 128, nt * 512:(nt + 1) * 512], vs.t[:], vs, False)

            kcs = list(range(KC))
            gemm_multi(xT2, nb, kcs, wp2, [
                (w_in[:, c_fa:c_fa + 16], 16, ev_fa),
                (w_in[:, c_ka:c_ka + 2048], 2048, ev_ka),
                (w_in[:, c_va:c_va + 2048], 2048, ev_va),
                (w_in[:, c_kr:c_kr + 2048], 2048, ev_kr),
                (w_in[:, c_vr:c_vr + 4096], 4096, ev_vr)])
        P2.close()

        TT3 = min(NOWN, 8)
        S.cur_phase = "p3"
        P3 = Phase()
        xT3 = P3.sb("xT3", [128, KC, TT3 * 128], BF16)
        wp3 = P3.pool("w", 2, [128, KC, 512], BF16)
        qg_bc = bcast_load(P3, "qg", qg_d, HD, scale=HD ** -0.5)
        ss4 = P3.pool("s4", 3, [128, 8], F32)
        knp = P3.pool("kn", 4, [128, 512], BF16)
        ktp = P3.pool("kt", 3, [128, 512], BF16)
        rop = P3.pool("ro", 2, [128, 1024], F32)
        csp = P3.pool("cs", 3, [128, 128], F32)
        for b0 in range(0, NOWN, TT3):
            nb = min(TT3, NOWN - b0)
            load_xT(xT3, uT_d, OWN0 + b0, nb)

            def ev_qa(tb, nt, ps, nw, b0=b0):
                return qknorm_T(P3, ps, qg_bc, qT_d, nt * 4, (b0 + tb) * 128, ss4, knp, ktp)

            def ev_qr(tb, nt, ps, nw, b0=b0):
                blk = b0 + tb
                cs = csp.next()
                S.dma("sp", cs.t[:], cs_d[(OWN0 + blk) * 128:(OWN0 + blk + 1) * 128, :], cs, True)
                kn = rotary(P3, ps, cs, qtab.t[:, blk, nt * 4:nt * 4 + 4], rop, knp)
                return lambda: transpose4(kn, qrT_d, nt * 4, blk * 128, ktp)

            def ev_act(dst, func):
                def ev(tb, nt, ps, nw, b0=b0):
                    blk = b0 + tb
                    vs = knp.next()
                    S.op("act", lambda e: e.activation(out=vs.t[:], in_=ps.t[:, 0:512], func=func), reads=[ps], writes=[vs])
                    S.dma("sp", dst[blk * 128:(blk + 1) * 128, nt * 512:(nt + 1) * 512], vs.t[:], vs, False)
                return ev
            kcs = list(range(KC))
            gemm_multi(xT3, nb, kcs, wp3, [
                (w_in[:, c_qa:c_qa + 2048], 2048, ev_qa),
                (w_in[:, c_qr:c_qr + 2048], 2048, ev_qr),
                (w_in[:, c_gr:c_gr + 4096], 4096, ev_act(sg_d, AF.Silu)),
                (w_in[:, c_ga:c_ga + D], D, ev_act(ga_d, AF.Sigmoid)),
                (w_in[:, c_gtr:c_gtr + D], D, ev_act(gr_d, AF.Sigmoid))])
        P3.close()

        S.cur_phase = "p4"
        P4 = Phase()
        qTp = P4.pool("qT", 2, [128, TO], BF16)
        kTp = P4.pool("kT", 2, [128, T], BF16)
        vp = P4.pool("v", 2, [128, NCTX, 129], BF16)
        bip = P4.pool("bias", 2, [128, NOWN, NCTX], F32)
        ptp = P4.pool("pt", 6, [128, 256], BF16)
        yap = P4.pool("yas", 2, [128, NOWN, 128], BF16)
        rcp = P4.pool("rc", 2, [128, 1], F32)
        def p4load(h):
            qT = qTp.next(); kT = kTp.next(); vv = vp.next()
            S.dma("sp", qT.t[:], qT_d[h], qT, True)
            S.dma("sp", kT.t[:], kT_d[h], kT, True)
            S.dma("sp", vv.t[:], vf_d[:, h, :].rearrange("(b p) f -> p b f", p=128), vv, True)
            return qT, kT, vv
        nx4 = p4load(0)
        for h in range(H):
            qT, kT, vv = nx4
            if h + 1 < H:
                nx4 = p4load(h + 1)
            yas = yap.next()
            biasT = bip.next()
            for i in range(NOWN):
                gi = OWN0 + i
                S.op("dve", lambda e, i=i, gi=gi: e.tensor_scalar(
                    out=biasT.t[:, i, 0:gi + 1], in0=Lfull.t[:, 0:gi + 1, h], scalar1=Lb.t[:, gi, h:h + 1], scalar2=None,
                    op0=ALU.subtract), reads=[Lfull, Lb], writes=[biasT])
            units = []
            for i0 in range(0, NOWN, 2):
                nq = 2 if i0 + 1 < NOWN else 1
                for j in range(OWN0 + i0 + 1):
                    units.append((i0, nq, j))
                if nq == 2:
                    units.append((i0 + 1, 1, OWN0 + i0 + 1))
            pss = {}

            def qk(u):
                i0, nq, j = units[u]
                ps = PSA.next()
                pss[u] = ps
                S.op("pe", lambda e: e.matmul(ps.t[:, 0:nq * 128], lhsT=kT.t[:, j * 128:(j + 1) * 128],
                                              rhs=qT.t[:, i0 * 128:(i0 + nq) * 128], start=True, stop=True), reads=[kT, qT], writes=[ps])
            LA = 3
            for u in range(min(LA, len(units))):
                qk(u)
            accs = {}

            def finish(i):
                acc = accs.pop(i)
                rc = rcp.next()
                S.op("dve", lambda e: e.reciprocal(out=rc.t[:], in_=acc.t[:, 128:129]), reads=[acc], writes=[rc])
                S.op("dve", lambda e: e.tensor_scalar(out=yas.t[:, i, :], in0=acc.t[:, 0:128], scalar1=rc.t[:, 0:1], scalar2=None,
                                                      op0=ALU.mult), reads=[acc, rc], writes=[yas])
            for u, (i0, nq, j) in enumerate(units):
                if u + LA < len(units):
                    qk(u + LA)
                iref = i0 + nq - 1
                ps = pss.pop(u)
                pt = ptp.next()
                S.op("act", lambda e: e.activation(out=pt.t[:, 0:nq * 128], in_=ps.t[:, 0:nq * 128], func=AF.Exp,
                                                   bias=biasT.t[:, iref, j:j + 1], scale=1.0), reads=[ps, biasT], writes=[pt])
                if j == OWN0 + i0:
                    S.op("pool", lambda e: e.tensor_tensor(out=pt.t[:, 0:128], in0=pt.t[:, 0:128], in1=tri.t[:], op=ALU.mult),
                         reads=[pt, tri], writes=[pt])
                for q in range(nq):
                    i = i0 + q
                    if i not in accs:
                        accs[i] = PSACC.next()
                        first = True
                    else:
                        first = False
                    acc = accs[i]
                    last = (j == OWN0 + i)
                    S.op("pe", lambda e: e.matmul(acc.t[:, 0:129], lhsT=pt.t[:, q * 128:(q + 1) * 128], rhs=vv.t[:, j, :],
                                                  start=first, stop=last), reads=[pt, vv], writes=[acc])
                    if last:
                        finish(i)
            S.dma("sp", ya_d[:, h * 128:(h + 1) * 128].rearrange("(b p) f -> p b f", p=128), yas.t[:], yas, False)
        P4.close()

        S.cur_phase = "p5"
        P5 = Phase()
        rg_bc = bcast_load(P5, "rg", rg_d, H * DV)
        NPV = max(OWN0, 1)
        krwp = P5.pool("krw", 2, [128, NPV, 128], BF16)
        vrp = P5.pool("vr", 2, [128, NCTX, DV], BF16)
        qrp = P5.pool("qrT", 2, [128, TO], BF16)
        krp = P5.pool("krT", 2, [128, TO], BF16)
        sgp = P5.pool("sg", 2, [128, NOWN, DV], BF16)
        stp = P5.pool("st", 2, [128, DV], BF16)
        ptp = P5.pool("pt", 6, [128, 128], BF16)
        yrp = P5.pool("yrs", 1, [128, NOWN, DV], BF16)
        smp = P5.pool("sm", 3, [128, 8], F32)
        jkp = P5.pool("jk", 2, [128, DV], F32)
        onp = P5.pool("on", 2, [128, DV], F32)
        def p5load(h):
            krw = krwp.next(); vr = vrp.next(); qr = qrp.next(); kr = krp.next(); sg = sgp.next()
            S.dma("sp", krw.t[:, 0:OWN0, :], krw_d[0:OWN0 * 128, h, :].rearrange("(b p) f -> p b f", p=128), krw, True)
            S.dma("sp", vr.t[:], vr_d[:, h * DV:(h + 1) * DV].rearrange("(b p) f -> p b f", p=128), vr, True)
            S.dma("sp", qr.t[:], qrT_d[h], qr, True)
            S.dma("sp", kr.t[:], krT_d[h], kr, True)
            S.dma("sp", sg.t[:], sg_d[:, h * DV:(h + 1) * DV].rearrange("(b p) f -> p b f", p=128), sg, True)
            return krw, vr, qr, kr, sg
        nx5 = p5load(0)
        for h in range(H):
            gam = 1.0 - 2.0 ** (-5.0 - h)
            krw, vr, qr, kr, sg = nx5
            if h + 1 < H:
                nx5 = p5load(h + 1)
            yrs = yrp.next()
            sp_ = PSA.next()
            for j in range(OWN0):
                S.op("pe", lambda e, j=j: e.matmul(sp_.t[:, 0:DV], lhsT=krw.t[:, j, :], rhs=vr.t[:, j, :],
                                                   start=(j == 0), stop=(j == OWN0 - 1)), reads=[krw, vr], writes=[sp_])
            stt = stp.next()
            S.op("act", lambda e, stt=stt, sp_=sp_, gam=gam: e.activation(out=stt.t[:], in_=sp_.t[:, 0:DV], func=AF.Copy, scale=float(gam)),
                 reads=[sp_], writes=[stt])
            if dbg:
                S.dma("sp", stt_d[h], stt.t[:], stt, False)
            pairs = [(i, j) for i in range(NOWN) for j in range(i + 1)]
            pss = {}

            def sk(p):
                i, j = pairs[p]
                ps = PSA.next()
                pss[p] = ps
                S.op("pe", lambda e: e.matmul(ps.t[:, 0:128], lhsT=kr.t[:, j * 128:(j + 1) * 128], rhs=qr.t[:, i * 128:(i + 1) * 128],
                                              start=True, stop=True), reads=[kr, qr], writes=[ps])
            LA = 3
            for p in range(min(LA, len(pairs))):
                sk(p)
            acc = None
            for p, (i, j) in enumerate(pairs):
                if p + LA < len(pairs):
                    sk(p + LA)
                if j == 0:
                    acc = PSACC.next()
                    S.op("pe", lambda e: e.matmul(acc.t[:, 0:DV], lhsT=qr.t[:, i * 128:(i + 1) * 128], rhs=stt.t[:],
                                                  start=True, stop=False), reads=[qr, stt], writes=[acc])
                ps = pss.pop(p)
                pt = ptp.next()
                if j == i:
                    S.op("dve", lambda e: e.tensor_tensor(out=pt.t[:], in0=ps.t[:, 0:128], in1=tri.t[:], op=ALU.mult),
                         reads=[ps, tri], writes=[pt])
                else:
                    S.op("act", lambda e: e.copy(out=pt.t[:], in_=ps.t[:, 0:128]), reads=[ps], writes=[pt])
                S.op("pe", lambda e: e.matmul(acc.t[:, 0:DV], lhsT=pt.t[:], rhs=vr.t[:, OWN0 + j, :], start=False, stop=(j == i)),
                     reads=[pt, vr], writes=[acc])
                if j != i:
                    continue
                sm = smp.next(); jk = jkp.next(); on = onp.next()
                S.op("pool", lambda e, sm=sm: e.memset(sm.t[:], 0.0), writes=[sm])
                S.op("act", lambda e, acc=acc, jk=jk, sm=sm: e.activation(out=jk.t[:], in_=acc.t[:, 0:DV], func=AF.Copy,
                                                                         accum_out=sm.t[:, 0:1]), reads=[acc], writes=[jk, sm])
                if dbg:
                    S.dma("sp", oraw_d[h, i * 128:(i + 1) * 128, :], jk.t[:], jk, False)
                S.op("act", lambda e, acc=acc, jk=jk, sm=sm: e.activation(out=jk.t[:], in_=acc.t[:, 0:DV], func=AF.Square,
                                                                         accum_out=sm.t[:, 1:2]), reads=[acc], writes=[jk, sm])
                S.op("dve", lambda e, sm=sm: e.tensor_scalar(out=sm.t[:, 2:4], in0=sm.t[:, 0:2], scalar1=1.0 / DV, scalar2=None, op0=ALU.mult),
                     reads=[sm], writes=[sm])
                S.op("dve", lambda e, sm=sm: e.tensor_tensor(out=sm.t[:, 4:5], in0=sm.t[:, 2:3], in1=sm.t[:, 2:3], op=ALU.mult),
                     reads=[sm], writes=[sm])
                S.op("dve", lambda e, sm=sm: e.tensor_tensor(out=sm.t[:, 5:6], in0=sm.t[:, 3:4], in1=sm.t[:, 4:5], op=ALU.subtract),
                     reads=[sm], writes=[sm])
                S.op("dve", lambda e, sm=sm: e.tensor_scalar(out=sm.t[:, 6:7], in0=sm.t[:, 5:6], scalar1=EPS, scalar2=None,
                                                              op0=ALU.add), reads=[sm], writes=[sm])
                S.op("pool", lambda e, sm=sm: e.tensor_tensor(out=sm.t[:, 6:7], in0=sm.t[:, 6:7], in1=mhalf.t[:, 0:1], op=ALU.pow),
                     reads=[sm, mhalf], writes=[sm])
                S.op("dve", lambda e, sm=sm, acc=acc, on=on: e.tensor_scalar(
                    out=on.t[:], in0=acc.t[:, 0:DV], scalar1=sm.t[:, 2:3], scalar2=sm.t[:, 6:7], op0=ALU.subtract, op1=ALU.mult),
                    reads=[acc, sm], writes=[on])
                S.op("pool", lambda e, on=on, h=h: e.tensor_tensor(out=on.t[:], in0=on.t[:], in1=rg_bc.t[:, h * DV:(h + 1) * DV], op=ALU.mult),
                     reads=[on, rg_bc], writes=[on])
                S.op("pool", lambda e, on=on, sg=sg, yrs=yrs, i=i: e.tensor_tensor(out=yrs.t[:, i, :], in0=on.t[:], in1=sg.t[:, i, :], op=ALU.mult),
                     reads=[on, sg], writes=[yrs])
            S.dma("sp", yr_d[:, h * DV:(h + 1) * DV].rearrange("(b p) f -> p b f", p=128), yrs.t[:], yrs, False)
        P5.close()
        PA.close()

        TT6 = min(NOWN, 4)
        KA = H * HD // 128
        KR = H * DV // 128
        S.cur_phase = "p6"
        P6 = Phase()
        yT = P6.sb("yT", [128, KA + KR, TT6 * 128], BF16)
        mbf = P6.sb("mbf", [128, TT6, D], BF16)
        mT = yT
        wp6 = P6.pool("w", 2, [128, 32, 512], BF16)
        wap6 = P6.pool("wa", 2, [128, KA, 512], BF16)
        ysp = P6.pool("ys", 2, [128, 2048], BF16)
        gtp = P6.pool("gt", 4, [128, 512], BF16)
        xp6 = P6.pool("x6", 2, [128, 512], F32)
        tp6 = P6.pool("t6", 3, [128, 512], F32)
        for b0 in range(0, NOWN, TT6):
            nb = min(TT6, NOWN - b0)
            for tb in range(nb):
                blk = b0 + tb
                for part in range((KA + KR) // 16):
                    ys = ysp.next()
                    if part == 0:
                        S.dma("sp", ys.t[:], ya_d[blk * 128:(blk + 1) * 128, :], ys, True)
                    else:
                        S.dma("sp", ys.t[:], yr_d[blk * 128:(blk + 1) * 128, (part - 1) * 2048:part * 2048], ys, True)
                    for k8 in range(0, 16, 8):
                        pb = PSB.next()
                        for j in range(8):
                            S.op("pe", lambda e: e.transpose(pb.t[:, j * 128:(j + 1) * 128], ys.t[:, (k8 + j) * 128:(k8 + j + 1) * 128], ident.t[:]),
                                 reads=[ys, ident], writes=[pb])
                        kk0 = part * 16 + k8
                        if (k8 // 8) % 2 == 0:
                            S.op("act", lambda e: e.copy(out=yT.t[:, kk0:kk0 + 8, tb * 128:(tb + 1) * 128],
                                                         in_=pb.t[:].rearrange("p (k t) -> p k t", t=128)), reads=[pb], writes=[yT])
                        else:
                            S.op("dve", lambda e: e.tensor_copy(out=yT.t[:, kk0:kk0 + 8, tb * 128:(tb + 1) * 128],
                                                                in_=pb.t[:].rearrange("p (k t) -> p k t", t=128)), reads=[pb], writes=[yT])
            def p6load(nt):
                wa = wap6.next(); wr = wp6.next()
                for k4 in range(0, KA, 8):
                    S.dma("pool", wa.t[:, k4:k4 + 8, :], wpf[k4 * 128:(k4 + 8) * 128, nt * 512:(nt + 1) * 512].rearrange("(kc p) n -> p kc n", p=128), wa, True)
                for k4 in range(0, KR, 8):
                    S.dma("pool", wr.t[:, k4:k4 + 8, :], wpr[k4 * 128:(k4 + 8) * 128, nt * 512:(nt + 1) * 512].rearrange("(kc p) n -> p kc n", p=128), wr, True)
                return wa, wr
            nx6 = p6load(0)
            for nt in range(D // 512):
                wa, wr = nx6
                if nt + 1 < D // 512:
                    nx6 = p6load(nt + 1)
                for tb in range(nb):
                    blk = b0 + tb
                    pa = PSA.next(); pr = PSA.next()
                    for kc in range(KA):
                        S.op("pe", lambda e, pa=pa, wa=wa, kc=kc, tb=tb: e.matmul(pa.t[:], lhsT=yT.t[:, kc, tb * 128:(tb + 1) * 128],
                                                                                 rhs=wa.t[:, kc, :], start=(kc == 0), stop=(kc == KA - 1)),
                             reads=[yT, wa], writes=[pa])
                    for kc in range(KR):
                        S.op("pe", lambda e, pr=pr, wr=wr, kc=kc, tb=tb: e.matmul(pr.t[:], lhsT=yT.t[:, KA + kc, tb * 128:(tb + 1) * 128],
                                                                                 rhs=wr.t[:, kc, :], start=(kc == 0), stop=(kc == KR - 1)),
                             reads=[yT, wr], writes=[pr])
                    g1 = gtp.next(); g2 = gtp.next(); t1 = tp6.next(); t2 = tp6.next()
                    S.dma("sp", g1.t[:], ga_d[blk * 128:(blk + 1) * 128, nt * 512:(nt + 1) * 512], g1, True)
                    S.dma("sp", g2.t[:], gr_d[blk * 128:(blk + 1) * 128, nt * 512:(nt + 1) * 512], g2, True)
                    S.op("dve", lambda e, t1=t1, pa=pa, g1=g1: e.tensor_tensor(out=t1.t[:], in0=pa.t[:], in1=g1.t[:], op=ALU.mult), reads=[pa, g1], writes=[t1])
                    S.op("dve", lambda e, t2=t2, pr=pr, g2=g2: e.tensor_tensor(out=t2.t[:], in0=pr.t[:], in1=g2.t[:], op=ALU.mult), reads=[pr, g2], writes=[t2])
                    S.op("dve", lambda e, t1=t1, t2=t2, tb=tb, nt=nt: e.tensor_tensor(out=mbf.t[:, tb, nt * 512:(nt + 1) * 512], in0=t1.t[:], in1=t2.t[:], op=ALU.add),
                         reads=[t1, t2], writes=[mbf])
            for tb in range(nb):
                for k8 in range(0, KC, 8):
                    pb = PSB.next()
                    n8 = min(8, KC - k8)
                    for j in range(n8):
                        S.op("pe", lambda e, pb=pb, j=j, k8=k8, tb=tb: e.transpose(
                            pb.t[:, j * 128:(j + 1) * 128], mbf.t[:, tb, (k8 + j) * 128:(k8 + j + 1) * 128], ident.t[:]),
                            reads=[mbf, ident], writes=[pb])
                    S.op("act", lambda e, pb=pb, k8=k8, n8=n8, tb=tb: e.copy(out=mT.t[:, k8:k8 + n8, tb * 128:(tb + 1) * 128],
                                                                            in_=pb.t[:, 0:n8 * 128].rearrange("p (k t) -> p k t", t=128)),
                         reads=[pb], writes=[mT])

            def ev_out(tb, nt, ps, nw, b0=b0):
                blk = b0 + tb
                xt = xp6.next()
                S.dma("sp", xt.t[:], ctx_x[(OWN0 + blk) * 128:(OWN0 + blk + 1) * 128, nt * 512:(nt + 1) * 512], xt, True)
                S.op("dve", lambda e: e.tensor_tensor(out=xt.t[:], in0=ps.t[:], in1=xt.t[:], op=ALU.add), reads=[ps, xt], writes=[xt])
                S.dma("sp", h1_d[blk * 128:(blk + 1) * 128, nt * 512:(nt + 1) * 512], xt.t[:], xt, False)
            gemm(mT, nb, list(range(KC)), wout, D, wp6, ev_out)
        P6.close()

        S.cur_phase = "p7"
        P7 = Phase()
        gf_bc = bcast_load(P7, "gffn", gffn_d, D)
        rmsnorm_T(P7, NOWN, lambda blk: h1_d[blk * 128:(blk + 1) * 128, :], gf_bc, xnT_d,
                  P7.pool("x", 3, [128, D], F32), P7.pool("jk", 1, [128, D], BF16),
                  P7.pool("xn", 2, [128, D], BF16), P7.pool("uo", 2, [128, KC, 128], BF16))
        P7.close()

        TT7 = min(NOWN, 8)
        S.cur_phase = "p7b"
        P7b = Phase()
        xT7 = P7b.sb("xT7", [128, KC, TT7 * 128], BF16)
        wp7 = P7b.pool("w", 2, [128, KC, 512], BF16)
        knp = P7b.pool("kn", 4, [128, 512], BF16)
        ktp = P7b.pool("kt", 3, [128, 512], BF16)
        for b0 in range(0, NOWN, TT7):
            nb = min(TT7, NOWN - b0)
            load_xT(xT7, xnT_d, b0, nb)

            def ev_q(tb, nt, ps, nw, b0=b0):
                kn = knp.next()
                S.op("act", lambda e: e.copy(out=kn.t[:], in_=ps.t[:, 0:512]), reads=[ps], writes=[kn])
                return lambda: transpose4(kn, qpT_d, nt * 4, (b0 + tb) * 128, ktp)
            gemm(xT7, nb, list(range(KC)), wq, PH * 256, wp7, ev_q)
        P7b.close()

        S.cur_phase = "p7c"
        P7c = Phase()
        kk = P7c.sb("kk", [128, 2 * PH, NK], BF16)
        for hh in range(PH):
            S.dma("pool", kk.t[:, 2 * hh, :], k1t[hh], kk, True)
            S.dma("pool", kk.t[:, 2 * hh + 1, :], k2t[hh], kk, True)
        qpp = P7c.pool("qp", 2, [128, 2 * PH, 128], BF16)
        scp = P7c.pool("sc", 2, [128, 2 * PH, NK], F32)
        tmpp = P7c.pool("tmp", 2, [128, 256], F32)
        t16p = P7c.pool("t16", 1, [128, 2 * PH, NK], F32)
        t2p = P7c.pool("t2", 1, [128, PH, 256], F32)
        v12p = P7c.pool("v12", 2, [128, 2 * PH, 16], F32)
        idxp = P7c.pool("idx", 2, [128, PH, 16], U32)
        idfp = P7c.pool("idf", 2, [128, PH * 16], F32)
        candp = P7c.pool("cand", 1, [128, PH, 256], F32)
        tvp = P7c.pool("tv", 2, [128, PH, 16], F32)
        smp = P7c.pool("sm", 2, [128, 4, PH], F32)
        pp_ = P7c.pool("pp", 2, [128, 16, NK], F32)
        ep_ = P7c.pool("ep", 2, [128, 16, NK], F32)
        Rp = P7c.pool("R", 1, [128, 128, NK], BF16)
        Rtp = P7c.pool("Rt", 1, [128, 128, NK], BF16)
        OHp = P7c.pool("OH", 1, [128, 128, NK], BF16)
        itp = P7c.pool("it", 2, [128, 128], F32)
        for blk in range(NOWN):
            qp = qpp.next()
            S.dma("sp", qp.t[:], qpT_d[:, :, blk * 128:(blk + 1) * 128].rearrange("c p t -> p c t"), qp, True)
            sc = scp.next()
            for c4 in range(0, 2 * PH, 4):
                ps = PSA.next()
                for j in range(4):
                    S.op("pe", lambda e, ps=ps, qp=qp, c4=c4, j=j: e.matmul(
                        ps.t[:, j * 128:(j + 1) * 128], lhsT=qp.t[:, c4 + j, :], rhs=kk.t[:, c4 + j, :], start=True, stop=True),
                        reads=[qp, kk], writes=[ps])
                S.op("act", lambda e, ps=ps, sc=sc, c4=c4: e.copy(out=sc.t[:, c4:c4 + 4, :], in_=ps.t[:].rearrange("p (c n) -> p c n", n=NK)),
                     reads=[ps], writes=[sc])
            v12 = v12p.next(); idx = idxp.next(); t16 = t16p.next(); t2 = t2p.next()
            v12s = v12.subs(2 * PH); idxs = idx.subs(PH); t16s = t16.subs(2 * PH); t2s = t2.subs(PH)
            for c in range(2 * PH):
                S.op("dve", lambda e: e.max(out=v12.t[:, c, 0:8], in_=sc.t[:, c, :]), reads=[sc], writes=[v12s[c]])
            for c in range(2 * PH):
                S.op("dve", lambda e: e.match_replace(out=t16.t[:, c, :], in_to_replace=v12.t[:, c, 0:8], in_values=sc.t[:, c, :],
                                                      imm_value=-1e30), reads=[sc, v12s[c]], writes=[t16s[c]])
            for c in range(2 * PH):
                S.op("dve", lambda e: e.max(out=v12.t[:, c, 8:16], in_=t16.t[:, c, :]), reads=[t16s[c]], writes=[v12s[c]])
            for hh in range(PH):
                c = 2 * hh
                S.op("dve", lambda e: e.max_index(out=idx.t[:, hh, 0:8], in_max=v12.t[:, c, 0:8], in_values=sc.t[:, c, :]),
                     reads=[sc, v12s[c]], writes=[idxs[hh]])
            for hh in range(PH):
                c = 2 * hh
                S.op("dve", lambda e: e.max_index(out=idx.t[:, hh, 8:16], in_max=v12.t[:, c, 8:16], in_values=t16.t[:, c, :]),
                     reads=[t16s[c], v12s[c]], writes=[idxs[hh]])
            cand = candp.next(); tv = tvp.next(); sm = smp.next()
            tvs = tv.subs(PH)
            vv = v12.t[:].rearrange("p (h two) k -> p h two k", two=2)
            S.op("pool", lambda e: e.tensor_tensor(
                out=cand.t[:].rearrange("p h (a b) -> p h a b", a=16),
                in0=vv[:, :, 0, :].unsqueeze(3).to_broadcast([128, PH, 16, 16]),
                in1=vv[:, :, 1, :].unsqueeze(2).to_broadcast([128, PH, 16, 16]), op=ALU.add), reads=v12s, writes=[cand])
            for hh in range(PH):
                S.op("dve", lambda e: e.max(out=tv.t[:, hh, 0:8], in_=cand.t[:, hh, :]), reads=[cand], writes=[tvs[hh]])
            for hh in range(PH):
                S.op("dve", lambda e: e.match_replace(out=t2.t[:, hh, :], in_to_replace=tv.t[:, hh, 0:8], in_values=cand.t[:, hh, :],
                                                      imm_value=-1e30), reads=[cand, tvs[hh]], writes=[t2s[hh]])
            for hh in range(PH):
                S.op("dve", lambda e: e.max(out=tv.t[:, hh, 8:16], in_=t2.t[:, hh, :]), reads=[t2s[hh]], writes=[tvs[hh]])
            S.op("dve", lambda e, sm=sm, tv=tv: e.tensor_scalar(out=sm.t[:, 0, :], in0=tv.t[:, :, 0], scalar1=-1.0, scalar2=None, op0=ALU.mult),
                 reads=tvs, writes=[sm])
            S.op("dve", lambda e, sm=sm, tv=tv: e.tensor_copy(out=sm.t[:, 3, :], in_=tv.t[:, :, 15]), reads=tvs, writes=[sm])
            ex = tmpp.next()
            S.op("dve", lambda e, sm=sm, tv=tv, ex=ex: e.tensor_tensor(
                out=ex.t[:, 0:PH * 16].rearrange("p (h k) -> p h k", k=16), in0=tv.t[:],
                in1=sm.t[:, 0, :].unsqueeze(2).to_broadcast([128, PH, 16]), op=ALU.add), reads=tvs + [sm], writes=[ex])
            S.op("act", lambda e, ex=ex: e.activation(out=ex.t[:, 0:PH * 16], in_=ex.t[:, 0:PH * 16], func=AF.Exp), reads=[ex], writes=[ex])
            S.op("dve", lambda e, sm=sm, ex=ex: e.tensor_reduce(out=sm.t[:, 1, :], in_=ex.t[:, 0:PH * 16].rearrange("p (h k) -> p h k", k=16),
                                                               axis=AX.X, op=ALU.add), reads=[ex], writes=[sm])
            S.op("act", lambda e, sm=sm: e.activation(out=sm.t[:, 2, :], in_=sm.t[:, 1, :], func=AF.Ln), reads=[sm], writes=[sm])
            S.op("dve", lambda e, sm=sm: e.tensor_tensor(out=sm.t[:, 2, :], in0=sm.t[:, 0, :], in1=sm.t[:, 2, :], op=ALU.subtract),
                 reads=[sm], writes=[sm])
            R = Rp.next()
            Gs = R.subs(32)
            for hh in range(PH):
                pp = pp_.next(); ep = ep_.next()
                S.op("pool", lambda e, pp=pp, hh=hh: e.tensor_tensor(
                    out=pp.t[:], in0=sc.t[:, 2 * hh + 1, :].unsqueeze(1).to_broadcast([128, 16, NK]),
                    in1=v12.t[:, 2 * hh, :].unsqueeze(2).to_broadcast([128, 16, NK]), op=ALU.add), reads=[sc, v12s[2 * hh]], writes=[pp])
                S.op("act", lambda e, pp=pp, ep=ep, hh=hh, sm=sm: e.activation(out=ep.t[:], in_=pp.t[:], func=AF.Exp,
                                                                              bias=sm.t[:, 2, hh:hh + 1], scale=1.0),
                     reads=[pp, sm], writes=[ep])
                S.op("dve", lambda e, pp=pp, ep=ep, hh=hh, sm=sm, R=R: e.scalar_tensor_tensor(
                    out=R.t[:, hh * 16:(hh + 1) * 16, :], in0=pp.t[:], scalar=sm.t[:, 3, hh:hh + 1], in1=ep.t[:],
                    op0=ALU.is_ge, op1=ALU.mult), reads=[pp, ep, sm], writes=Gs)
            Rt = Rtp.next()
            Rts = Rt.subs(NK // 8)
            for i8 in range(0, NK, 8):
                pb = PSB.next()
                for j in range(8):
                    S.op("pe", lambda e, pb=pb, R=R, i8=i8, j=j: e.transpose(pb.t[:, j * 128:(j + 1) * 128], R.t[:, :, i8 + j], ident.t[:]),
                         reads=Gs + [ident], writes=[pb])
                eng = "act" if (i8 // 8) % 2 == 0 else "dve"
                outv = Rt.t[:, :, i8:i8 + 8].rearrange("p t i -> p i t")
                if eng == "act":
                    S.op("act", lambda e, pb=pb, outv=outv: e.copy(out=outv, in_=pb.t[:].rearrange("p (i t) -> p i t", t=128)), reads=[pb], writes=[Rts[i8 // 8]])
                else:
                    S.op("dve", lambda e, pb=pb, outv=outv: e.tensor_copy(out=outv, in_=pb.t[:].rearrange("p (i t) -> p i t", t=128)), reads=[pb], writes=[Rts[i8 // 8]])
            idf = idfp.next()
            S.op("dve", lambda e, idf=idf, idx=idx: e.tensor_copy(out=idf.t[:], in_=idx.t[:].rearrange("p h k -> p (h k)")), reads=idxs, writes=[idf])
            pt_ = PSA.next()
            S.op("pe", lambda e, pt_=pt_, idf=idf: e.transpose(pt_.t[:, 0:128], idf.t[:], identf.t[:]), reads=[idf, identf], writes=[pt_])
            it = itp.next()
            S.op("act", lambda e, it=it, pt_=pt_: e.copy(out=it.t[:], in_=pt_.t[:, 0:128]), reads=[pt_], writes=[it])
            OH = OHp.next()
            S.op("dve", lambda e, OH=OH, it=it: e.tensor_tensor(
                out=OH.t[:], in0=iotar.t[:].unsqueeze(1).to_broadcast([128, 128, NK]),
                in1=it.t[:].unsqueeze(2).to_broadcast([128, 128, NK]), op=ALU.is_equal), reads=[iotar, it], writes=[OH])
            G = R
            for t4 in range(0, 128, 4):
                ps = PSA.next()
                for j in range(4):
                    S.op("pe", lambda e, ps=ps, t4=t4, j=j: e.matmul(ps.t[:, j * 128:(j + 1) * 128], lhsT=Rt.t[:, t4 + j, :], rhs=OH.t[:, t4 + j, :],
                                                                      start=True, stop=True), reads=Rts + [OH], writes=[ps])
                outv = G.t[:, :, t4:t4 + 4].rearrange("p i t -> p t i")
                if (t4 // 4) % 2 == 0:
                    S.op("act", lambda e, ps=ps, outv=outv: e.copy(out=outv, in_=ps.t[:].rearrange("p (t i) -> p t i", i=NK)), reads=[ps], writes=[Gs[t4 // 4]])
                else:
                    S.op("dve", lambda e, ps=ps, outv=outv: e.tensor_copy(out=outv, in_=ps.t[:].rearrange("p (t i) -> p t i", i=NK)), reads=[ps], writes=[Gs[t4 // 4]])
            for i16 in range(0, NK, 16):
                S.dma("sp", G_d[i16:i16 + 16, :, blk * 128:(blk + 1) * 128].rearrange("i p t -> p i t"), G.t[:, i16:i16 + 16, :], G, False, deps=Gs)
        P7c.close()

        TT8 = min(NOWN, 4)
        S.cur_phase = "p8"
        P8 = Phase()
        xT8 = P8.sb("xT8", [128, KC, TT8 * 128], BF16)
        yacc = P8.sb("yacc", [128, TT8, D], F32)
        utp = P8.pool("ut", 2, [128, KC, 256], BF16)
        vtp = P8.pool("vt", 4, [128, D], BF16)
        vsp8 = P8.pool("vs8", 2, [128, D], F32)
        gp8 = P8.pool("g8", 4, [128, TT8 * 128], BF16)
        gep = P8.pool("ge", 2, [128, TT8 * 128], F32)
        atp = P8.pool("at", 4, [128, TT8 * 128], BF16)
        for b0 in range(0, NOWN, TT8):
            nb = min(TT8, NOWN - b0)
            ntok = nb * 128
            load_xT(xT8, xnT_d, b0, nb)
            ND8 = D // 512
            yss = yacc.subs(TT8 * ND8)
            for tb in range(nb):
                S.dma("sp", yacc.t[:, tb, :], h1_d[(b0 + tb) * 128:(b0 + tb + 1) * 128, :], yacc, True, deps=yss[tb * ND8:(tb + 1) * ND8])
            def p8load(grp):
                ut = utp.next()
                for k4 in range(0, KC, 8):
                    k5 = min(KC, k4 + 8)
                    S.dma("pool", ut.t[:, k4:k5, :], ut_d[grp, :, k4:k5, :], ut, True)
                g8s_ = []
                for cl in range(2):
                    c = grp * 2 + cl
                    g8 = gp8.next()
                    S.dma("sp", g8.t[:, 0:ntok], G_d[c, :, b0 * 128:b0 * 128 + ntok], g8, True)
                    g8s_.append(g8)
                return ut, g8s_

            def p8loadv(grp):
                vts_ = []
                for cl in range(2):
                    c = grp * 2 + cl
                    vt = vtp.next()
                    vs_ = vsp8.next()
                    for d4 in range(0, D, 2048):
                        d5 = min(D, d4 + 2048)
                        S.dma("sp", vs_.t[:, d4:d5], pv_d[c * 128:(c + 1) * 128, d4:d5], vs_, True)
                    S.op("act", lambda e: e.copy(out=vt.t[:], in_=vs_.t[:]), reads=[vs_], writes=[vt])
                    vts_.append(vt)
                return vts_
            def second8(ats, vts):
                for tb in range(nb):
                    for dt in range(D // 512):
                        ps = PSA.next()
                        for cl in range(2):
                            S.op("pe", lambda e: e.matmul(ps.t[:], lhsT=ats[cl].t[:, tb * 128:(tb + 1) * 128],
                                                          rhs=vts[cl].t[:, dt * 512:(dt + 1) * 512], start=(cl == 0), stop=(cl == 1)),
                                 reads=[ats[cl], vts[cl]], writes=[ps])
                        S.op("dve", lambda e: e.tensor_tensor(
                            out=yacc.t[:, tb, dt * 512:(dt + 1) * 512], in0=ps.t[:], in1=yacc.t[:, tb, dt * 512:(dt + 1) * 512], op=ALU.add),
                            reads=[ps, yss[tb * ND8 + dt]], writes=[yss[tb * ND8 + dt]])
            prev8 = None
            nx8 = p8load(0)
            nxv = p8loadv(0)
            NG8 = NE // 256
            for grp in range(NG8):
                ut, g8s = nx8
                vts = nxv
                if grp + 1 < NG8:
                    nx8 = p8load(grp + 1)
                ats = []
                for cl in range(2):
                    g8 = g8s[cl]
                    ps = PSA.next()
                    for kc in range(KC):
                        S.op("pe", lambda e, ps=ps, ut=ut, kc=kc, cl=cl, ntok=ntok: e.matmul(
                            ps.t[:, 0:ntok], lhsT=ut.t[:, kc, cl * 128:(cl + 1) * 128], rhs=xT8.t[:, kc, 0:ntok],
                            start=(kc == 0), stop=(kc == KC - 1)), reads=[ut, xT8], writes=[ps])
                    ge = gep.next(); at = atp.next()
                    S.op("act", lambda e, ps=ps, ge=ge, ntok=ntok: e.activation(out=ge.t[:, 0:ntok], in_=ps.t[:, 0:ntok], func=AF.Gelu),
                         reads=[ps], writes=[ge])
                    S.op("dve", lambda e, ge=ge, g8=g8, at=at, ntok=ntok: e.tensor_tensor(out=at.t[:, 0:ntok], in0=ge.t[:, 0:ntok], in1=g8.t[:, 0:ntok], op=ALU.mult),
                         reads=[ge, g8], writes=[at])
                    ats.append(at)
                if prev8 is not None:
                    second8(*prev8)
                if grp + 1 < NG8:
                    nxv = p8loadv(grp + 1)
                prev8 = (ats, vts)
            second8(*prev8)
            for tb in range(nb):
                S.dma("sp", out_d[(b0 + tb) * 128:(b0 + tb + 1) * 128, :], yacc.t[:, tb, :], yacc, False, deps=yss[tb * ND8:(tb + 1) * ND8])
        P8.close()
        P0.close()
        S.emit()
    return nc


_CACHE = {}


def kernel(x, meta_tokens, norm_mix_g, w_in, b_forget, q_norm_g, k_norm_g, ret_norm_g, w_proj_fox, w_proj_ret,
           w_out, norm_ffn_g, peer_w_q, peer_keys_1, peer_keys_2, peer_u, peer_v, _dbg=False):
    f = lambda a: np.ascontiguousarray(np.asarray(a, dtype=np.float32))
    x = f(x)
    B, SEQ, D = x.shape
    NB = SEQ // 128
    NOWN = NB // 4
    NCTX = 1 + NB
    T = NCTX * 128
    KC = D // 128
    key = (D, NCTX, NOWN)
    key = (D, NCTX, NOWN, _dbg)
    if key not in _CACHE:
        _CACHE[key] = build(D, NCTX, NOWN, _dbg)
    nc = _CACHE[key]
    meta = f(meta_tokens)
    pu = f(peer_u)[0]
    NE = pu.shape[0]
    ut = np.ascontiguousarray(pu.reshape(NE // 256, 256, KC, 128).transpose(0, 3, 2, 1))
    shared = {
        "norm_mix_g": f(norm_mix_g)[0], "w_in": f(w_in)[0], "b_forget": f(b_forget)[0], "q_norm_g": f(q_norm_g)[0],
        "k_norm_g": f(k_norm_g)[0], "ret_norm_g": f(ret_norm_g)[0], "w_proj_fox": f(w_proj_fox)[0],
        "w_proj_ret": f(w_proj_ret)[0], "w_out": f(w_out)[0], "norm_ffn_g": f(norm_ffn_g)[0],
        "peer_w_q": f(peer_w_q)[0],
        "k1t": np.ascontiguousarray(f(peer_keys_1)[0].transpose(0, 2, 1)),
        "k2t": np.ascontiguousarray(f(peer_keys_2)[0].transpose(0, 2, 1)),
        "ut": ut, "peer_v": f(peer_v)[0],
    }
    gam = 1.0 - 2.0 ** (-5.0 - np.arange(H, dtype=np.float64))
    inv = ROPE_BASE ** (-np.arange(64, dtype=np.float64) / 64)
    in_maps = []
    for c in range(8):
        b, g = c // 4, c % 4
        ndum = (3 - g) * NOWN * 128
        nprev = g * NOWN * 128
        ctx = np.zeros((T, D), np.float32)
        ctx[ndum + PAD:ndum + 128] = meta
        ctx[ndum + 128:ndum + 128 + nprev] = x[b, :nprev]
        ctx[T - NOWN * 128:] = x[b, nprev:nprev + NOWN * 128]
        n = np.arange(T) - ndum
        valid = (n >= PAD).astype(np.float32)
        pos = (n - PAD).astype(np.float64)
        ang = pos[:, None] * inv[None, :]
        cossin = np.concatenate([np.cos(ang), np.sin(ang)], axis=1).astype(np.float32)
        start = 128 + nprev
        lpos = (n - start).astype(np.float64)
        own = n >= start
        ktab = np.zeros((T, H), np.float64)
        with np.errstate(over="ignore", under="ignore"):
            ktab[~own] = np.exp(np.log(gam)[None, :] * (start - 1 - n[~own])[:, None])
            ktab[own] = np.exp(-np.log(gam)[None, :] * lpos[own][:, None])
            qtab = np.exp(np.log(gam)[None, :] * lpos[own][:, None])
        ktab = ktab * (128.0 ** -0.5) * valid[:, None]
        m = dict(shared)
        m.update({"ctx_x": ctx, "valid": np.ascontiguousarray(valid.reshape(NCTX, 128).T), "cossin": cossin, "ktab": ktab.astype(np.float32),
                  "qtab": qtab.astype(np.float32)})
        in_maps.append(m)
    res = run_bass_kernel_spmd(nc, in_maps, core_ids=list(range(8)))
    if _dbg:
        return res.results, in_maps
    out = np.zeros((B, SEQ, D), np.float32)
    for c in range(8):
        b, g = c // 4, c % 4
        out[b, g * NOWN * 128:(g + 1) * NOWN * 128] = res.results[c]["out"]
    return out
```
